# Optimizing a Trainium2 kernel written in Bass

```python
import jax
import jax.numpy as jnp
from jax import lax
import numpy as np

D_MODEL = 1024
BATCH = 2
SEQ = 16384
DEPTH = 4

GRID_W = 64
CTX_LEN = 256
HEAD_DIM = 64
SCALE = HEAD_DIM ** -0.5
ROPE_HALF = HEAD_DIM // 2
ROPE_FREQS = HEAD_DIM // 4
ROPE_THETA = 10000.0
Q_BLOCK = 128
A_HEADS = 8
A_KV_HEADS = 2
B_HEADS = 8
B_KV_HEADS = 2
WINDOW = 128
C_HEADS = 8
NA_ROWS = 8
NA_COLS = 16
D_HEADS = 8
DECAY_LORA = 64
ICLR_LORA = 64
GATE_LORA = 128
GN_EPS = 64e-5
RMS_EPS = 1e-6
NEG_INF = -1e30
FFN_HIDDEN = -(-8 * D_MODEL // (3 * 256)) * 256

A_Q = A_HEADS * HEAD_DIM
A_KV = A_KV_HEADS * HEAD_DIM
B_Q = B_HEADS * HEAD_DIM
B_KV = B_KV_HEADS * HEAD_DIM
C_W = C_HEADS * HEAD_DIM
D_W = D_HEADS * HEAD_DIM
EVEN_WIDTHS = (A_Q, A_KV, A_KV, B_Q, B_KV, B_KV)
EVEN_IN = A_Q + 2 * A_KV + B_Q + 2 * B_KV
EVEN_MIX = A_Q + B_Q
D_SHIFT_W = 3 * D_W + DECAY_LORA + ICLR_LORA + GATE_LORA
ODD_IN = 3 * C_W + D_SHIFT_W
ODD_MIX = C_W + D_W
N_EVEN = (DEPTH + 1) // 2
N_ODD = DEPTH // 2

kernel_name = "hybrid_axialgqa_swa_natten_rwkv7_dit"


def _split(t, widths):
    return jnp.split(t, [int(o) for o in np.cumsum(widths)[:-1]], axis=-1)


def rms_norm(t, gain):
    tf = t.astype(jnp.float32)
    tf = tf * lax.rsqrt(jnp.mean(tf * tf, axis=-1, keepdims=True) + RMS_EPS)
    return (tf * gain.astype(jnp.float32)).astype(t.dtype)


def modulate(t, shift, scale):
    return t * (1 + scale) + shift


def heads(t, n):
    return t.reshape(t.shape[:-1] + (n, HEAD_DIM))


def group(q, n_kv):
    return q.reshape(q.shape[:2] + (n_kv, q.shape[2] // n_kv, HEAD_DIM))


def axial_rope_tables(n_tokens):
    t = jnp.arange(n_tokens, dtype=jnp.int32)
    row = (t // GRID_W).astype(jnp.float32)
    col = (t % GRID_W).astype(jnp.float32)
    inv = ROPE_THETA ** (-jnp.arange(ROPE_FREQS, dtype=jnp.float32) / ROPE_FREQS)
    ang = jnp.concatenate([row[:, None] * inv, col[:, None] * inv], axis=-1)
    return jnp.cos(ang), jnp.sin(ang)


def apply_rope(t, cos, sin):
    c = cos[None, :, None, :].astype(t.dtype)
    s = sin[None, :, None, :].astype(t.dtype)
    t1, t2 = t[..., :ROPE_HALF], t[..., ROPE_HALF:]
    return jnp.concatenate([t1 * c - t2 * s, t1 * s + t2 * c], axis=-1)


def _sink_column(sink, shape):
    col = sink.astype(jnp.float32).reshape(1, shape[1], shape[2], 1, 1)
    return jnp.broadcast_to(col, shape[:-1] + (1,))


def dense_attention(q, k, v, sink=None):
    B, L, HKV, G, _ = q.shape
    n = k.shape[1]
    s = jnp.einsum('bqkgd,bnkd->bkgqn', q, k, preferred_element_type=jnp.float32) * SCALE
    if sink is not None:
        s = jnp.concatenate([s, _sink_column(sink, s.shape)], axis=-1)
    p = jax.nn.softmax(s, axis=-1)[..., :n].astype(v.dtype)
    return jnp.einsum('bkgqn,bnkd->bqkgd', p, v).reshape(B, L, HKV * G * HEAD_DIM)


def global_attention(q, k, v, kc, vc):
    B, S, HKV, G, _ = q.shape
    nblk = S // Q_BLOCK
    keys = jnp.concatenate([k, kc], axis=1)
    vals = jnp.concatenate([v, vc], axis=1)
    qb = jnp.moveaxis(q.reshape(B, nblk, Q_BLOCK, HKV, G, HEAD_DIM), 1, 0)

    def block(qblk):
        s = jnp.einsum('bqkgd,bnkd->bkgqn', qblk, keys, preferred_element_type=jnp.float32) * SCALE
        p = jax.nn.softmax(s, axis=-1).astype(vals.dtype)
        return jnp.einsum('bkgqn,bnkd->bqkgd', p, vals)

    o = lax.map(block, qb)
    return jnp.moveaxis(o, 0, 1).reshape(B, S, HKV * G * HEAD_DIM)


def window_attention(q, k, v, kc, vc, sink):
    B, S, HKV, G, _ = q.shape
    nblk = S // Q_BLOCK
    span = Q_BLOCK + 2 * WINDOW
    n_ctx = kc.shape[1]
    pad = ((0, 0), (WINDOW, WINDOW), (0, 0), (0, 0))
    kp, vp = jnp.pad(k, pad), jnp.pad(v, pad)
    qb = jnp.moveaxis(q.reshape(B, nblk, Q_BLOCK, HKV, G, HEAD_DIM), 1, 0)
    qi = jnp.arange(Q_BLOCK)[:, None]
    kj = jnp.arange(span)[None, :]
    band = jnp.abs(kj - WINDOW - qi) <= WINDOW

    def block(args):
        qblk, start = args
        ks = lax.dynamic_slice_in_dim(kp, start, span, axis=1)
        vs = lax.dynamic_slice_in_dim(vp, start, span, axis=1)
        pos = start - WINDOW + kj
        valid = band & (pos >= 0) & (pos < S)
        s_loc = jnp.einsum('bqkgd,bnkd->bkgqn', qblk, ks, preferred_element_type=jnp.float32) * SCALE
        s_loc = jnp.where(valid, s_loc, NEG_INF)
        s_ctx = jnp.einsum('bqkgd,bnkd->bkgqn', qblk, kc, preferred_element_type=jnp.float32) * SCALE
        s = jnp.concatenate([s_loc, s_ctx, _sink_column(sink, s_loc.shape)], axis=-1)
        p = jax.nn.softmax(s, axis=-1).astype(v.dtype)
        return (jnp.einsum('bkgqn,bnkd->bqkgd', p[..., :span], vs)
                + jnp.einsum('bkgqn,bnkd->bqkgd', p[..., span:span + n_ctx], vc))

    o = lax.map(block, (qb, jnp.arange(nblk) * Q_BLOCK))
    return jnp.moveaxis(o, 0, 1).reshape(B, S, HKV * G * HEAD_DIM)


def neighborhood_attention(q, k, v, kc, vc, rpb):
    B, S, H, _ = q.shape
    rows = S // GRID_W
    kr = min(NA_ROWS, rows)
    n_loc = kr * NA_COLS
    qg = jnp.moveaxis(q.reshape(B, rows, GRID_W, H, HEAD_DIM), 1, 0)
    kg = k.reshape(B, rows, GRID_W, H, HEAD_DIM)
    vg = v.reshape(B, rows, GRID_W, H, HEAD_DIM)
    col = jnp.arange(GRID_W)
    col_idx = jnp.clip(col - NA_COLS // 2, 0, GRID_W - NA_COLS)[:, None] + jnp.arange(NA_COLS)[None, :]
    dcol = col_idx - col[:, None] + NA_COLS - 1
    row_start = jnp.clip(jnp.arange(rows) - kr // 2, 0, rows - kr)

    def block(args):
        q_row, r, rs = args
        ks = lax.dynamic_slice_in_dim(kg, rs, kr, axis=1)[:, :, col_idx]
        vs = lax.dynamic_slice_in_dim(vg, rs, kr, axis=1)[:, :, col_idx]
        drow = rs + jnp.arange(kr) - r + NA_ROWS - 1
        bias = rpb[:, drow[None, :, None], dcol[:, None, :]].astype(jnp.float32)
        s_loc = jnp.einsum('bchd,bicjhd->bhcij', q_row, ks, preferred_element_type=jnp.float32) * SCALE + bias
        s_ctx = jnp.einsum('bchd,bnhd->bhcn', q_row, kc, preferred_element_type=jnp.float32) * SCALE
        s = jnp.concatenate([s_loc.reshape(B, H, GRID_W, n_loc), s_ctx], axis=-1)
        p = jax.nn.softmax(s, axis=-1).astype(v.dtype)
        p_loc = p[..., :n_loc].reshape(B, H, GRID_W, kr, NA_COLS)
        return (jnp.einsum('bhcij,bicjhd->bchd', p_loc, vs)
                + jnp.einsum('bhcn,bnhd->bchd', p[..., n_loc:], vc))

    o = lax.map(block, (qg, jnp.arange(rows), row_start))
    return jnp.moveaxis(o, 0, 1).reshape(B, S, H * HEAD_DIM)


def centred_shift_mix(z, mu):
    zp = jnp.pad(z, ((0, 0), (1, 1), (0, 0)))
    return z + (0.5 * (zp[:, :-2] + zp[:, 2:]) - z) * mu


def rwkv7_inputs(zd, w0, w2, a0, a2, g2, k_k, k_a):
    r, k, v, wd, ad, gd = _split(zd, (D_W, D_W, D_W, DECAY_LORA, ICLR_LORA, GATE_LORA))
    g = jax.nn.sigmoid(gd) @ g2
    kk = heads(k * k_k, D_HEADS).astype(jnp.float32)
    kk = kk / jnp.maximum(jnp.sqrt(jnp.sum(kk * kk, axis=-1, keepdims=True)), 1e-12)
    decay, kd, av = [], [], []
    for d in range(2):
        w = -jax.nn.softplus(-(w0[d] + jnp.tanh(wd) @ w2[d])) - 0.5
        a = jax.nn.sigmoid(a0[d] + ad @ a2[d])
        decay.append(heads(jnp.exp(-jnp.exp(w.astype(jnp.float32))), D_HEADS))
        kd.append(heads(k * (1 + (a - 1) * k_a), D_HEADS))
        av.append(heads(a, D_HEADS))
    return {"r": heads(r, D_HEADS), "v": heads(v, D_HEADS), "kk": kk, "g": g,
            "decay": decay, "k": kd, "a": av}


def wkv7_scan(state0, p, d, reverse, emit):
    seq = [p["decay"][d], p["k"][d], p["v"], p["kk"], p["a"][d]] + ([p["r"]] if emit else [])
    xs = tuple(jnp.moveaxis(t.astype(jnp.float32), 1, 0) for t in seq)

    def step(S, inp):
        w_t, k_t, v_t, kk_t, a_t = inp[:5]
        sa = jnp.einsum('bhvk,bhk->bhv', S, kk_t)
        S = (S * w_t[:, :, None, :] - sa[..., None] * (kk_t * a_t)[:, :, None, :]
             + v_t[..., None] * k_t[:, :, None, :])
        return S, (jnp.einsum('bhvk,bhk->bhv', S, inp[5]) if emit else None)

    S, ys = lax.scan(step, state0, xs, reverse=reverse)
    return S, (jnp.moveaxis(ys, 0, 1) if emit else None)


def rwkv7_readout(y, p, r_k, ln_w, ln_b):
    B, L, H, N = y.shape
    mean = jnp.mean(y, axis=-1, keepdims=True)
    var = jnp.mean(jnp.square(y - mean), axis=-1, keepdims=True)
    yn = ((y - mean) * lax.rsqrt(var + GN_EPS)).reshape(B, L, H * N)
    yn = yn * ln_w.astype(jnp.float32) + ln_b.astype(jnp.float32)
    r = p["r"].astype(jnp.float32)
    ksum = (p["k"][0] + p["k"][1]).astype(jnp.float32)
    bonus = jnp.sum(r * ksum * r_k.astype(jnp.float32), axis=-1, keepdims=True) * p["v"].astype(jnp.float32)
    return (yn + bonus.reshape(B, L, H * N)) * p["g"].astype(jnp.float32)


def even_mixer(h, hc, w_in, w_out, q_gain, k_gain, sink, cos, sin, need_ctx):
    qa, ka, va, qb, kb, vb = _split(h @ w_in, EVEN_WIDTHS)
    qca, kca, vca, qcb, kcb, vcb = _split(hc @ w_in, EVEN_WIDTHS)
    qa = apply_rope(rms_norm(heads(qa, A_HEADS), q_gain), cos, sin)
    ka = apply_rope(rms_norm(heads(ka, A_KV_HEADS), k_gain), cos, sin)
    kca = rms_norm(heads(kca, A_KV_HEADS), k_gain)
    va, vca = heads(va, A_KV_HEADS), heads(vca, A_KV_HEADS)
    qb = apply_rope(heads(qb, B_HEADS), cos, sin)
    kb = apply_rope(heads(kb, B_KV_HEADS), cos, sin)
    kcb, vb, vcb = heads(kcb, B_KV_HEADS), heads(vb, B_KV_HEADS), heads(vcb, B_KV_HEADS)
    y_a = global_attention(group(qa, A_KV_HEADS), ka, va, kca, vca)
    y_b = window_attention(group(qb, B_KV_HEADS), kb, vb, kcb, vcb, sink)
    y = jnp.concatenate([y_a, y_b], axis=-1) @ w_out
    if not need_ctx:
        return y, None
    qca = rms_norm(heads(qca, A_HEADS), q_gain)
    yc_a = dense_attention(group(qca, A_KV_HEADS), kca, vca)
    yc_b = dense_attention(group(heads(qcb, B_HEADS), B_KV_HEADS), kcb, vcb, sink)
    return y, jnp.concatenate([yc_a, yc_b], axis=-1) @ w_out


def odd_mixer(h, hc, w_in, w_out, rpb, mu, w0, w2, a0, a2, g2, k_k, k_a, r_k, ln_w, ln_b, need_ctx):
    q, k, v, zd = _split(h @ w_in, (C_W, C_W, C_W, D_SHIFT_W))
    qc, kc, vc, zdc = _split(hc @ w_in, (C_W, C_W, C_W, D_SHIFT_W))
    kc, vc = heads(kc, C_HEADS), heads(vc, C_HEADS)
    y_c = neighborhood_attention(heads(q, C_HEADS), heads(k, C_HEADS), heads(v, C_HEADS), kc, vc, rpb)
    lat = rwkv7_inputs(centred_shift_mix(zd, mu), w0, w2, a0, a2, g2, k_k, k_a)
    cx = rwkv7_inputs(centred_shift_mix(zdc, mu), w0, w2, a0, a2, g2, k_k, k_a)
    zero = jnp.zeros((h.shape[0], D_HEADS, HEAD_DIM, HEAD_DIM), jnp.float32)
    s_fwd, yc_fwd = wkv7_scan(zero, cx, 0, False, need_ctx)
    s_bwd, yc_bwd = wkv7_scan(zero, cx, 1, True, need_ctx)
    _, y_fwd = wkv7_scan(s_fwd, lat, 0, False, True)
    _, y_bwd = wkv7_scan(s_bwd, lat, 1, True, True)
    y_d = rwkv7_readout(y_fwd + y_bwd, lat, r_k, ln_w, ln_b).astype(h.dtype)
    y = jnp.concatenate([y_c, y_d], axis=-1) @ w_out
    if not need_ctx:
        return y, None
    yc_c = dense_attention(heads(qc, C_HEADS)[:, :, :, None], kc, vc)
    yc_d = rwkv7_readout(yc_fwd + yc_bwd, cx, r_k, ln_w, ln_b).astype(hc.dtype)
    return y, jnp.concatenate([yc_c, yc_d], axis=-1) @ w_out


def swiglu(h, w_in, w_out):
    gate, up = jnp.split(h @ w_in, 2, axis=-1)
    return (jax.nn.silu(gate) * up) @ w_out


def setup_inputs(seed: int = 0) -> dict:
    key = jax.random.key(seed)
    ks = iter(jax.random.split(key, 40))

    def nrm(shape, scale):
        return scale * jax.random.normal(next(ks), shape, jnp.float32)

    D = D_MODEL
    return {
        "x": nrm((BATCH, SEQ, D), 1.0),
        "c": nrm((BATCH, D), 1.0),
        "ctx": nrm((BATCH, CTX_LEN, D), 1.0),
        "c_ctx": nrm((D,), 1.0),
        "w_mod": nrm((DEPTH, D, 6 * D), 0.5 * D ** -0.5),
        "b_mod": nrm((DEPTH, 6 * D), 0.02),
        "norm_mix": 1.0 + nrm((DEPTH, D), 0.02),
        "norm_ffn": 1.0 + nrm((DEPTH, D), 0.02),
        "w_in_even": nrm((N_EVEN, D, EVEN_IN), D ** -0.5),
        "w_out_even": nrm((N_EVEN, EVEN_MIX, D), EVEN_MIX ** -0.5),
        "q_norm_a": 1.0 + nrm((N_EVEN, HEAD_DIM), 0.02),
        "k_norm_a": 1.0 + nrm((N_EVEN, HEAD_DIM), 0.02),
        "sink_b": nrm((N_EVEN, B_HEADS), 0.5),
        "w_in_odd": nrm((N_ODD, D, ODD_IN), D ** -0.5),
        "w_out_odd": nrm((N_ODD, ODD_MIX, D), ODD_MIX ** -0.5),
        "rpb_c": nrm((N_ODD, C_HEADS, 2 * NA_ROWS - 1, 2 * NA_COLS - 1), 0.2),
        "shift_mu": jax.random.uniform(next(ks), (N_ODD, D_SHIFT_W), jnp.float32),
        "decay_w0": -1.0 + nrm((N_ODD, 2, D_W), 0.3),
        "decay_w2": nrm((N_ODD, 2, DECAY_LORA, D_W), 0.1),
        "iclr_a0": nrm((N_ODD, 2, D_W), 0.3),
        "iclr_a2": nrm((N_ODD, 2, ICLR_LORA, D_W), 0.1),
        "gate_g2": nrm((N_ODD, GATE_LORA, D_W), GATE_LORA ** -0.5),
        "k_k": 0.85 + nrm((N_ODD, D_W), 0.05),
        "k_a": 1.0 + nrm((N_ODD, D_W), 0.05),
        "r_k": nrm((N_ODD, D_HEADS, HEAD_DIM), 0.1),
        "ln_x_w": 1.0 + nrm((N_ODD, D_W), 0.02),
        "ln_x_b": nrm((N_ODD, D_W), 0.02),
        "w_ffn_in": nrm((DEPTH, D, 2 * FFN_HIDDEN), D ** -0.5),
        "w_ffn_out": nrm((DEPTH, FFN_HIDDEN, D), FFN_HIDDEN ** -0.5),
        "norm_out": 1.0 + nrm((D,), 0.02),
    }


def reference(x, c, ctx, c_ctx, w_mod, b_mod, norm_mix, norm_ffn, w_in_even, w_out_even, q_norm_a, k_norm_a,
              sink_b, w_in_odd, w_out_odd, rpb_c, shift_mu, decay_w0, decay_w2, iclr_a0, iclr_a2, gate_g2,
              k_k, k_a, r_k, ln_x_w, ln_x_b, w_ffn_in, w_ffn_out, norm_out):
    cos, sin = axial_rope_tables(x.shape[1])
    silu_c = jax.nn.silu(c)
    silu_cc = jax.nn.silu(c_ctx)
    for layer in range(DEPTH):
        need_ctx = layer < DEPTH - 1
        mod = (silu_c @ w_mod[layer] + b_mod[layer])[:, None, :]
        modc = silu_cc @ w_mod[layer] + b_mod[layer]
        sh1, sc1, g1, sh2, sc2, g2 = jnp.split(mod, 6, axis=-1)
        csh1, csc1, cg1, csh2, csc2, cg2 = jnp.split(modc, 6, axis=-1)
        h = modulate(rms_norm(x, norm_mix[layer]), sh1, sc1)
        hc = modulate(rms_norm(ctx, norm_mix[layer]), csh1, csc1)
        i = layer // 2
        if layer % 2 == 0:
            y, yc = even_mixer(h, hc, w_in_even[i], w_out_even[i], q_norm_a[i], k_norm_a[i], sink_b[i],
                               cos, sin, need_ctx)
        else:
            y, yc = odd_mixer(h, hc, w_in_odd[i], w_out_odd[i], rpb_c[i], shift_mu[i], decay_w0[i], decay_w2[i],
                              iclr_a0[i], iclr_a2[i], gate_g2[i], k_k[i], k_a[i], r_k[i], ln_x_w[i], ln_x_b[i],
                              need_ctx)
        x = x + g1 * y
        x = x + g2 * swiglu(modulate(rms_norm(x, norm_ffn[layer]), sh2, sc2), w_ffn_in[layer], w_ffn_out[layer])
        if need_ctx:
            ctx = ctx + cg1 * yc
            ctx = ctx + cg2 * swiglu(modulate(rms_norm(ctx, norm_ffn[layer]), csh2, csc2),
                                     w_ffn_in[layer], w_ffn_out[layer])
    return rms_norm(x, norm_out)
```

```python
import numpy as np
import ml_dtypes
import concourse.bass as bass
import concourse.mybir as mybir
from concourse.bass_utils import run_bass_kernel_spmd

F32 = mybir.dt.float32
BF16 = mybir.dt.bfloat16
AF = mybir.ActivationFunctionType
ALU = mybir.AluOpType
AX = mybir.AxisListType
NPBF = ml_dtypes.bfloat16

D = 1024
SEQ = 16384
CTX = 256
DEPTH = 4
HID = 2816
NCORES = 8
RMS_EPS = 1e-6


class Buf:
    __slots__ = ("w", "r", "name", "excl")

    def __init__(self, name="", excl=False):
        self.w = None
        self.r = {}
        self.name = name
        self.excl = excl


class FW:
    NDMA = 24

    def __init__(self, nc):
        self.nc = nc
        self.engs = {"pe": nc.tensor, "act": nc.scalar, "dve": nc.vector, "pool": nc.gpsimd, "sp": nc.sync}
        self.esem = {}
        self.ecnt = {}
        for e in ("pe", "act", "dve", "pool"):
            self.esem[e] = nc.alloc_semaphore("es_" + e)
            self.ecnt[e] = 0
        self.waited = {e: {} for e in self.engs}
        self.dsems = [nc.alloc_semaphore("ds%d" % i) for i in range(self.NDMA)]
        self.dcnt = [0] * self.NDMA
        self.dnext = 0
        self.semid = {}
        self.out_toks = []
        self.nwaits = 0
        self.prog = {e: [] for e in self.engs}

    def _sid(self, sem):
        return id(sem)

    def _wait(self, e, tok):
        sem, val = tok
        k = self._sid(sem)
        if self.waited[e].get(k, 0) >= val:
            return
        self.prog[e].append(("w", sem, val))
        if e == "pe" and sem is not self.esem["pe"]:
            self.prog[e].append(("w", sem, val))
        self.nwaits += 1
        self.waited[e][k] = val

    def _deps(self, e, reads, writes):
        own = self.esem.get(e)
        for t in reads:
            if t.w is not None:
                if not (e == "pe" and t.w[0] is own):
                    self._wait(e, t.w)
            if t.excl:
                for k, tok in t.r.items():
                    if tok[0] is not own:
                        self._wait(e, tok)
        for t in writes:
            if t.w is not None:
                if not (e == "pe" and t.w[0] is own):
                    self._wait(e, t.w)
            for k, tok in t.r.items():
                if e == "pe" and tok[0] is own:
                    continue
                self._wait(e, tok)

    def _post(self, tok, reads, writes):
        k = self._sid(tok[0])
        for t in reads:
            o = t.r.get(k)
            if o is None or o[1] < tok[1]:
                t.r[k] = tok
        for t in writes:
            t.w = tok
            t.r = {}

    def fence(self, e):
        if self.ecnt[e] > 0:
            self._wait(e, (self.esem[e], self.ecnt[e]))

    def op(self, e, reads, writes, meth, *args, **kw):
        self._deps(e, reads, writes)
        self.ecnt[e] += 1
        self.prog[e].append(("i", (lambda en, m=meth, a=args, k=kw: getattr(en, m)(*a, **k)), self.esem[e], 1))
        self._post((self.esem[e], self.ecnt[e]), reads, writes)

    def dma(self, e, reads, writes, out, in_, is_output=False):
        j = self.dnext
        self.dnext = (j + 1) % self.NDMA
        if self.dcnt[j] > 0:
            self._wait(e, (self.dsems[j], self.dcnt[j]))
        self._deps(e, reads, writes)
        self.prog[e].append(("i", (lambda en, o=out, i=in_: en.dma_start(out=o, in_=i)), self.dsems[j], 16))
        self.dcnt[j] += 16
        tok = (self.dsems[j], self.dcnt[j])
        self._post(tok, reads, writes)
        if is_output:
            self.out_toks.append(tok)

    def finish(self):
        for tok in self.out_toks:
            self._wait("sp", tok)
        for j in range(self.NDMA):
            if self.dcnt[j] > 0:
                self._wait("sp", (self.dsems[j], self.dcnt[j]))
        for e in ("pe", "act", "dve", "pool"):
            if self.ecnt[e] > 0:
                self._wait("sp", (self.esem[e], self.ecnt[e]))
        prog = self.prog

        def replay(eng, items):
            for it in items:
                if it[0] == "w":
                    eng.wait_ge(it[1], it[2])
                else:
                    it[1](eng).then_inc(it[2], it[3])

        with self.nc.Block() as block:
            @block.sync
            def _(eng):
                replay(eng, prog["sp"])

            @block.tensor
            def _(eng):
                replay(eng, prog["pe"])

            @block.scalar
            def _(eng):
                replay(eng, prog["act"])

            @block.vector
            def _(eng):
                replay(eng, prog["dve"])

            @block.gpsimd
            def _(eng):
                replay(eng, prog["pool"])


def SBT(nc, name, shape, dt):
    return nc.alloc_sbuf_tensor(name, shape, dt)


def PST(nc, name, shape, dt):
    return nc.alloc_psum_tensor(name, shape, dt)


def load_w_bf16(fw, nc, w_ap, dst, bdst, K, N, stg, bstg, eng_cast="pool", col0=0, ncols=None, dcol0=0):
    if ncols is None:
        ncols = N
    SW = stg[0].shape[1]
    q = 0
    for k in range(K // 128):
        for c0 in range(0, ncols, SW):
            cw = min(SW, ncols - c0)
            s = q % len(stg)
            fw.dma("sp" if q % 2 == 0 else "act", [], [bstg[s]], stg[s][:, 0:cw],
                   w_ap[k * 128:(k + 1) * 128, col0 + c0:col0 + c0 + cw])
            fw.op(eng_cast, [bstg[s]], [bdst],
                  "tensor_copy", out=dst[:, k, dcol0 + c0:dcol0 + c0 + cw],
                                                                 in_=stg[s][:, 0:cw])
            q += 1


def emit_rstd(fw, ss, bss, rstd, brstd, n):
    fw.op("dve", [bss], [brstd], "tensor_scalar", out=rstd, in0=ss, scalar1=1.0 / n, scalar2=RMS_EPS,
                                                           op0=ALU.mult, op1=ALU.add)
    fw.op("act", [brstd], [brstd], "activation", out=rstd, in_=rstd, func=AF.Sqrt)
    fw.op("dve", [brstd], [brstd], "reciprocal", out=rstd, in_=rstd)


def make_ident(fw, nc, name="ident"):
    ident = SBT(nc, name, [128, 128], F32)
    b = Buf()
    fw.op("pool", [], [b], "memset", ident[:], 1.0)
    fw.op("pool", [b], [b], "affine_select", out=ident[:], in_=ident[:], pattern=[[-1, 128]],
                                                      compare_op=ALU.is_equal, fill=0.0, base=0,
                                                      channel_multiplier=1)
    return ident, b


def build_kc(NT_LAT, NT_CTX, final):
    NT = NT_LAT + NT_CTX
    TB = 256
    nc = bass.Bass("TRN2", target_bir_lowering=False)
    x = nc.dram_tensor("x", [NT, D], F32, kind="ExternalInput").ap()
    yT = nc.dram_tensor("yT", [D, NT], BF16, kind="ExternalInput").ap()
    w_out = nc.dram_tensor("w_out", [D, D], F32, kind="ExternalInput").ap()
    w_fi = nc.dram_tensor("w_fi", [D, 2 * HID], F32, kind="ExternalInput").ap()
    w_fo = nc.dram_tensor("w_fo", [HID, D], F32, kind="ExternalInput").ap()
    grep = nc.dram_tensor("grep", [2, 2, 128, D], F32, kind="ExternalInput").ap()
    fmv = nc.dram_tensor("fmv", [128, 40], F32, kind="ExternalInput").ap()
    norep = nc.dram_tensor("norep", [128, D], F32, kind="ExternalInput").ap()
    xo = nc.dram_tensor("xo", [NT, D], F32, kind="ExternalOutput").ap()
    fw = FW(nc)
    NJ = HID // 128
    wout = SBT(nc, "wout", [128, 8, D], BF16); bwout = Buf()
    wfi = SBT(nc, "wfi", [128, 8, 2 * HID], BF16); bwfi = Buf()
    wfo = SBT(nc, "wfo", [128, NJ, D], BF16); bwfo = Buf()
    stg = [SBT(nc, "stg%d" % i, [128, 1024], F32) for i in range(2)]; bstg = [Buf(), Buf()]
    g1 = SBT(nc, "g1", [128, D], F32); bg1 = Buf()
    g2 = SBT(nc, "g2", [128, D], F32); bg2 = Buf()
    fm = SBT(nc, "fm", [128, 40], F32); bfm = Buf()
    gs = SBT(nc, "gs", [128, 16], F32); bgs = Buf()
    yTs = SBT(nc, "yTs", [128, 8, TB], BF16); byT = Buf()
    xin = SBT(nc, "xin", [128, D], F32); bxin = Buf()
    x1 = [SBT(nc, "x1_%d" % i, [128, D], F32) for i in range(TB // 128)]; bx1 = [Buf() for _ in x1]
    xn = SBT(nc, "xn", [128, D], F32); bxn = Buf()
    hT = SBT(nc, "hT", [128, 8, TB], BF16); bhT = Buf()
    aT = SBT(nc, "aT", [128, NJ, TB], BF16); baT = Buf()
    gsb = [SBT(nc, "gsb%d" % i, [128, TB], F32) for i in range(2)]; bgsb = [Buf(), Buf()]
    ss = SBT(nc, "ss", [128, 1], F32); bss = Buf()
    rstd = SBT(nc, "rstd", [128, 1], F32); brstd = Buf()
    pmm = [PST(nc, "pmm%d" % i, [128, D], F32) for i in range(2)]; bpmm = [Buf(), Buf()]
    ptr = PST(nc, "ptr", [128, 8, 128], F32); bptr = Buf()
    pgu = [PST(nc, "pgu%d" % i, [128, 2, TB], F32) for i in range(2)]; bpgu = [Buf() for _ in range(2)]
    ident, bid = make_ident(fw, nc)

    fw.dma("sp", [], [bfm], fm[:], fmv[:, :])
    load_w_bf16(fw, nc, w_out, wout, bwout, D, D, stg, bstg)
    load_w_bf16(fw, nc, w_fi, wfi, bwfi, D, 2 * HID, stg, bstg)
    load_w_bf16(fw, nc, w_fo, wfo, bwfo, HID, D, stg, bstg)
    if final:
        nrep = SBT(nc, "nrep", [128, D], F32); bnrep = Buf()
        fw.dma("sp", [], [bnrep], nrep[:], norep[:, :])

    blocks = [(t0, min(TB, NT_LAT - t0), 0) for t0 in range(0, NT_LAT, TB)]
    blocks += [(NT_LAT + t0, min(TB, NT_CTX - t0), 1) for t0 in range(0, NT_CTX, TB)]
    cur_g = -1
    pi = 0
    for (t0, tb, g) in blocks:
        if g != cur_g:
            cur_g = g
            fw.dma("sp", [], [bg1], g1[:], grep[g, 0])
            fw.dma("sp", [], [bg2], g2[:], grep[g, 1])
            sc = fm[:, 8 + 16 * g:16 + 16 * g]
            fw.op("dve", [bfm], [bgs], "scalar_tensor_tensor", out=gs[:, 0:8], in0=sc, scalar=1.0, in1=fm[:, 0:8], op0=ALU.add, op1=ALU.mult)
        shc = 16 + 16 * g
        nt = tb // 128
        fw.dma("sp", [], [byT], yTs[:, :, 0:tb], yT.rearrange("(c p) t -> p c t", p=128)[:, :, t0:t0 + tb])
        for i in range(nt):
            fw.dma("act", [], [bxin], xin[:], x[t0 + i * 128:t0 + (i + 1) * 128, :])
            po = pmm[pi % 2]; bpo = bpmm[pi % 2]; pi += 1
            for cb in range(2):
                for k in range(8):
                    fw.op("pe", [byT, bwout], [bpo], "matmul", po[:, cb * 512:(cb + 1) * 512], lhsT=yTs[:, k, i * 128:(i + 1) * 128],
                        rhs=wout[:, k, cb * 512:(cb + 1) * 512], start=(k == 0), stop=(k == 7))
            fw.op("dve", [bpo, bg1], [bx1[i]], "tensor_tensor", out=x1[i][:], in0=po[:], in1=g1[:], op=ALU.mult)
            fw.op("dve", [bx1[i], bxin], [bx1[i]], "tensor_tensor", out=x1[i][:], in0=x1[i][:], in1=xin[:], op=ALU.add)
            fw.op("act", [bx1[i]], [bxn, bss], "activation", out=xn[:], in_=x1[i][:], func=AF.Square, accum_out=ss[:])
            emit_rstd(fw, ss[:], bss, rstd[:], brstd, D)
            fw.op("act", [bx1[i], brstd], [bxn], "activation", out=xn[:], in_=x1[i][:], func=AF.Copy, scale=rstd[:, 0:1])
            for c in range(8):
                fw.op("pe", [bxn, bid], [bptr], "transpose", out=ptr[:, c, :], in_=xn[:, c * 128:(c + 1) * 128], identity=ident[:])
            for c in range(8):
                if c % 2 == 0:
                    fw.op("dve", [bptr, bgs, bfm], [bhT], "tensor_scalar", out=hT[:, c, i * 128:(i + 1) * 128], in0=ptr[:, c, :], scalar1=gs[:, c:c + 1],
                        scalar2=fm[:, shc + c:shc + c + 1], op0=ALU.mult, op1=ALU.add)
                else:
                    fw.op("act", [bptr, bgs, bfm], [bhT], "activation", out=hT[:, c, i * 128:(i + 1) * 128], in_=ptr[:, c, :], func=AF.Identity,
                        scale=gs[:, c:c + 1], bias=fm[:, shc + c:shc + c + 1])
        for j in range(NJ):
            pg = pgu[j % 2][:, 0, :]; bpg = bpgu[j % 2]
            pu = pgu[j % 2][:, 1, :]; bpu = bpgu[j % 2]
            for k in range(8):
                fw.op("pe", [bhT, bwfi], [bpg], "matmul", pg[:, 0:tb], lhsT=wfi[:, k, j * 128:(j + 1) * 128], rhs=hT[:, k, 0:tb],
                    start=(k == 0), stop=(k == 7))
            for k in range(8):
                fw.op("pe", [bhT, bwfi], [bpu], "matmul", pu[:, 0:tb], lhsT=wfi[:, k, HID + j * 128:HID + (j + 1) * 128], rhs=hT[:, k, 0:tb],
                    start=(k == 0), stop=(k == 7))
            gb = gsb[j % 2]; bgb = bgsb[j % 2]
            fw.op("act", [bpg], [bgb], "activation", out=gb[:, 0:tb], in_=pg[:, 0:tb], func=AF.Silu)
            fw.op("dve", [bgb, bpu], [baT], "tensor_tensor", out=aT[:, j, 0:tb], in0=pu[:, 0:tb], in1=gb[:, 0:tb], op=ALU.mult)
        for i in range(nt):
            pf = pmm[pi % 2]; bpf = bpmm[pi % 2]; pi += 1
            for cb in range(2):
                for j in range(NJ):
                    fw.op("pe", [baT, bwfo], [bpf], "matmul", pf[:, cb * 512:(cb + 1) * 512], lhsT=aT[:, j, i * 128:(i + 1) * 128],
                        rhs=wfo[:, j, cb * 512:(cb + 1) * 512], start=(j == 0), stop=(j == NJ - 1))
            fw.op("dve", [bpf, bg2], [bxn], "tensor_tensor", out=xn[:], in0=pf[:], in1=g2[:], op=ALU.mult)
            fw.op("dve", [bxn, bx1[i]], [bxn], "tensor_tensor", out=xn[:], in0=xn[:], in1=x1[i][:], op=ALU.add)
            if final:
                fw.op("act", [bxn], [bx1[i], bss], "activation", out=x1[i][:], in_=xn[:], func=AF.Square, accum_out=ss[:])
                emit_rstd(fw, ss[:], bss, rstd[:], brstd, D)
                fw.op("dve", [bxn, brstd, bnrep], [bxn], "scalar_tensor_tensor", out=xn[:], in0=xn[:], scalar=rstd[:, 0:1], in1=nrep[:], op0=ALU.mult, op1=ALU.mult)
            fw.dma("sp", [bxn], [], xo[t0 + i * 128:t0 + (i + 1) * 128, :], xn[:], is_output=True)
    fw.finish()
    return nc


EV_FM = [(0, "a"), (128, "a"), (256, "a"), (384, "a"), (512, "ak"), (768, "b"), (896, "b"), (1024, "b"),
         (1152, "b"), (1280, "b")]
EV_TM = [(640, 128), (1408, 128)]
OD_FM_BF = [(c * 128, "c") for c in range(8)]
OD_FM_F32 = [(1536 + c * 128, "z") for c in range(14)]
OD_TM = [(1024, 512)]


def build_ka(NT_LAT, NT_CTX, even):
    NT = NT_LAT + NT_CTX
    TB = 512
    NIN = 1536 if even else 3328
    nc = bass.Bass("TRN2", target_bir_lowering=False)
    x = nc.dram_tensor("x", [NT, D], F32, kind="ExternalInput").ap()
    w_in = nc.dram_tensor("w_in", [D, NIN], F32, kind="ExternalInput").ap()
    fmv = nc.dram_tensor("fmv", [128, 40], F32, kind="ExternalInput").ap()
    fw = FW(nc)
    if even:
        fm_list = EV_FM
        tm_list = EV_TM
        gv_d = nc.dram_tensor("gv", [128, 2], F32, kind="ExternalInput").ap()
        rm_d = nc.dram_tensor("rm", [128, 128], F32, kind="ExternalInput").ap()
        ob_d = nc.dram_tensor("ob", [128, 128], F32, kind="ExternalInput").ap()
        cs_d = nc.dram_tensor("cs", [2, 128, NT], F32, kind="ExternalInput").ap()
        qk_o = nc.dram_tensor("qk_o", [len(fm_list), 128, NT], BF16, kind="ExternalOutput").ap()
        NTM = 256
    else:
        fm_list = OD_FM_BF + OD_FM_F32
        tm_list = OD_TM
        qk_o = nc.dram_tensor("qk_o", [8, 128, NT], BF16, kind="ExternalOutput").ap()
        zd_o = nc.dram_tensor("zd_o", [14, 128, NT], F32, kind="ExternalOutput").ap()
        NTM = 512
    v_o = nc.dram_tensor("v_o", [NT, NTM], BF16, kind="ExternalOutput").ap()

    win = SBT(nc, "win", [128, 8, NIN], BF16); bwin = Buf()
    stg = [SBT(nc, "stg%d" % i, [128, 1024], F32) for i in range(2)]; bstg = [Buf(), Buf()]
    fm = SBT(nc, "fm", [128, 40], F32); bfm = Buf()
    gs = SBT(nc, "gs", [128, 8], F32); bgs = Buf()
    xin = [SBT(nc, "xin%d" % i, [128, D], F32) for i in range(2)]; bxin = [Buf(), Buf()]
    xn = SBT(nc, "xn", [128, D], F32); bxn = Buf()
    hT = SBT(nc, "hT", [128, 8, TB], BF16); bhT = Buf()
    ss = SBT(nc, "ss", [128, 1], F32); bss = Buf()
    rstd = SBT(nc, "rstd", [128, 1], F32); brstd = Buf()
    ptr = PST(nc, "ptr", [128, 8, 128], F32); bptr = Buf()
    pfm = [PST(nc, "pfm%d" % i, [128, TB], F32) for i in range(2)]; bpfm = [Buf(), Buf()]
    paux = [PST(nc, "paux%d" % i, [128, TB], F32) for i in range(2)]; bpaux = [Buf(), Buf()]
    ptm = [PST(nc, "ptm%d" % i, [128, 512], F32) for i in range(2)]; bptm = [Buf(), Buf()]
    ofm = [SBT(nc, "ofm%d" % i, [128, TB], F32) for i in range(3)]; bofm = [Buf() for _ in range(3)]
    ofb = [SBT(nc, "ofb%d" % i, [128, TB], BF16) for i in range(3)]; bofb = [Buf() for _ in range(3)]
    otm = [SBT(nc, "otm%d" % i, [128, NTM], BF16) for i in range(2)]; botm = [Buf(), Buf()]
    ident, bid = make_ident(fw, nc)
    fw.dma("sp", [], [bfm], fm[:], fmv[:, :])
    if even:
        gv = SBT(nc, "gv_s", [128, 2], F32); bgv = Buf()
        rm = SBT(nc, "rm_s", [128, 128], F32); brm = Buf()
        ob = SBT(nc, "ob_s", [128, 128], F32); bob = Buf()
        cst = SBT(nc, "cst", [128, 2, TB], F32); bcst = Buf()
        sq = SBT(nc, "sq", [128, TB], F32); bsq = Buf()
        rs = SBT(nc, "rs", [128, TB], F32); brs = Buf()
        t1 = SBT(nc, "t1", [128, TB], F32); bt1 = Buf()
        fw.dma("sp", [], [bgv], gv[:], gv_d[:, :])
        fw.dma("sp", [], [brm], rm[:], rm_d[:, :])
        fw.dma("sp", [], [bob], ob[:], ob_d[:, :])
    load_w_bf16(fw, nc, w_in, win, bwin, D, NIN, stg, bstg)
    blocks = [(t0, min(TB, NT_LAT - t0), 0) for t0 in range(0, NT_LAT, TB)]
    blocks += [(NT_LAT + t0, min(TB, NT_CTX - t0), 1) for t0 in range(0, NT_CTX, TB)]
    cur_g = -1
    xi = 0
    fi = 0
    oi = 0
    ti = 0
    for (t0, tb, g) in blocks:
        if g != cur_g:
            cur_g = g
            fw.op("dve", [bfm], [bgs], "scalar_tensor_tensor", out=gs[:, 0:8], in0=fm[:, 8 + 16 * g:16 + 16 * g],
                  scalar=1.0, in1=fm[:, 0:8], op0=ALU.add, op1=ALU.mult)
        shc = 16 + 16 * g
        nt = tb // 128
        if even:
            fw.dma("act", [], [bcst], cst[:, :, 0:tb], cs_d.rearrange("a p t -> p a t")[:, :, t0:t0 + tb])
        for i in range(nt):
            xt = xin[xi % 2]; bxt = bxin[xi % 2]; xi += 1
            fw.dma("sp", [], [bxt], xt[:], x[t0 + i * 128:t0 + (i + 1) * 128, :])
            fw.op("act", [bxt], [bxn, bss], "activation", out=xn[:], in_=xt[:], func=AF.Square, accum_out=ss[:])
            emit_rstd(fw, ss[:], bss, rstd[:], brstd, D)
            fw.op("act", [bxt, brstd], [bxn], "activation", out=xn[:], in_=xt[:], func=AF.Copy, scale=rstd[:, 0:1])
            for c in range(8):
                fw.op("pe", [bxn, bid], [bptr], "transpose", out=ptr[:, c, :], in_=xn[:, c * 128:(c + 1) * 128],
                      identity=ident[:])
            for c in range(8):
                if c % 2 == 0:
                    fw.op("dve", [bptr, bgs, bfm], [bhT], "tensor_scalar", out=hT[:, c, i * 128:(i + 1) * 128],
                          in0=ptr[:, c, :], scalar1=gs[:, c:c + 1], scalar2=fm[:, shc + c:shc + c + 1],
                          op0=ALU.mult, op1=ALU.add)
                else:
                    fw.op("act", [bptr, bgs, bfm], [bhT], "activation", out=hT[:, c, i * 128:(i + 1) * 128],
                          in_=ptr[:, c, :], func=AF.Identity, scale=gs[:, c:c + 1],
                          bias=fm[:, shc + c:shc + c + 1])
        for ci, (col, kind) in enumerate(fm_list):
            pf = pfm[fi % 2]; bpf = bpfm[fi % 2]; fi += 1
            for k in range(8):
                fw.op("pe", [bhT, bwin], [bpf], "matmul", pf[:, 0:tb], lhsT=win[:, k, col:col + 128],
                      rhs=hT[:, k, 0:tb], start=(k == 0), stop=(k == 7))
            if kind == "c":
                o = ofb[oi % 3]; bo = bofb[oi % 3]; oi += 1
                if ci % 2 == 0:
                    fw.op("act", [bpf], [bo], "activation", out=o[:, 0:tb], in_=pf[:, 0:tb], func=AF.Copy)
                else:
                    fw.op("dve", [bpf], [bo], "tensor_copy", out=o[:, 0:tb], in_=pf[:, 0:tb])
                fw.dma("sp", [bo], [], qk_o[ci, :, t0:t0 + tb], o[:, 0:tb], is_output=True)
            elif kind == "z":
                o = ofm[oi % 3]; bo = bofm[oi % 3]; oi += 1
                if ci % 2 == 0:
                    fw.op("act", [bpf], [bo], "activation", out=o[:, 0:tb], in_=pf[:, 0:tb], func=AF.Copy)
                else:
                    fw.op("dve", [bpf], [bo], "tensor_copy", out=o[:, 0:tb], in_=pf[:, 0:tb])
                fw.dma("sp", [bo], [], zd_o[ci - 8, :, t0:t0 + tb], o[:, 0:tb], is_output=True)
            else:
                pa = paux[fi % 2]; bpa = bpaux[fi % 2]
                if kind in ("a", "ak"):
                    gcol = 0 if kind == "a" else 1
                    fw.op("act", [bpf], [bsq], "activation", out=sq[:, 0:tb], in_=pf[:, 0:tb], func=AF.Square)
                    fw.op("pe", [bsq, bob], [bpa], "matmul", pa[:, 0:tb], lhsT=ob[:], rhs=sq[:, 0:tb],
                          start=True, stop=True)
                    fw.op("dve", [bpa], [brs], "tensor_scalar", out=rs[:, 0:tb], in0=pa[:, 0:tb], scalar1=RMS_EPS,
                          scalar2=None, op0=ALU.add)
                    fw.op("act", [brs], [brs], "activation", out=rs[:, 0:tb], in_=rs[:, 0:tb], func=AF.Sqrt)
                    fw.op("dve", [brs], [brs], "reciprocal", out=rs[:, 0:tb], in_=rs[:, 0:tb])
                    fw.op("dve", [bpf, brs, bgv], [bt1], "scalar_tensor_tensor", out=t1[:, 0:tb], in0=pf[:, 0:tb],
                          scalar=gv[:, gcol:gcol + 1], in1=rs[:, 0:tb], op0=ALU.mult, op1=ALU.mult)
                else:
                    fw.op("act", [bpf], [bt1], "activation", out=t1[:, 0:tb], in_=pf[:, 0:tb], func=AF.Copy)
                pr = paux[(fi + 1) % 2]; bpr = bpaux[(fi + 1) % 2]
                fw.op("pe", [bt1, brm], [bpr], "matmul", pr[:, 0:tb], lhsT=rm[:], rhs=t1[:, 0:tb],
                      start=True, stop=True)
                fw.op("dve", [bpr, bcst], [bsq], "tensor_tensor", out=sq[:, 0:tb], in0=pr[:, 0:tb],
                      in1=cst[:, 1, 0:tb], op=ALU.mult)
                fw.op("pool", [bt1, bcst], [bt1], "tensor_tensor", out=t1[:, 0:tb], in0=t1[:, 0:tb],
                      in1=cst[:, 0, 0:tb], op=ALU.mult)
                o = ofb[oi % 3]; bo = bofb[oi % 3]; oi += 1
                fw.op("dve", [bt1, bsq], [bo], "tensor_tensor", out=o[:, 0:tb], in0=t1[:, 0:tb], in1=sq[:, 0:tb],
                      op=ALU.add)
                fw.dma("sp", [bo], [], qk_o[ci, :, t0:t0 + tb], o[:, 0:tb], is_output=True)
        for i in range(nt):
            pt = ptm[ti % 2]; bpt = bptm[ti % 2]
            o = otm[ti % 2]; bo = botm[ti % 2]; ti += 1
            oc = 0
            for (col, wd) in tm_list:
                for k in range(8):
                    fw.op("pe", [bhT, bwin], [bpt], "matmul", pt[:, oc:oc + wd], lhsT=hT[:, k, i * 128:(i + 1) * 128],
                          rhs=win[:, k, col:col + wd], start=(k == 0 and oc == 0), stop=(k == 7),
                          skip_group_check=True)
                oc += wd
            fw.op("act", [bpt], [bo], "activation", out=o[:], in_=pt[:, 0:NTM], func=AF.Copy)
            fw.dma("pool", [bo], [], v_o[t0 + i * 128:t0 + (i + 1) * 128, :], o[:], is_output=True)
    fw.finish()
    return nc


def rope_tables(pos_tok, is_ctx):
    t = pos_tok.astype(np.int64)
    row = (t // 64).astype(np.float32)
    colp = (t % 64).astype(np.float32)
    inv = (10000.0 ** (-np.arange(16, dtype=np.float32) / 16)).astype(np.float32)
    ang = np.concatenate([row[:, None] * inv, colp[:, None] * inv], axis=-1)
    cos = np.cos(ang).astype(np.float32)
    sin = np.sin(ang).astype(np.float32)
    cos[is_ctx] = 1.0
    sin[is_ctx] = 0.0
    cosf = np.concatenate([cos, cos, cos, cos], axis=1).T
    sinf = np.concatenate([sin, sin, sin, sin], axis=1).T
    return np.ascontiguousarray(np.stack([cosf, sinf]).astype(np.float32))


def rope_consts():
    rm = np.zeros((128, 128), np.float32)
    for hb in (0, 64):
        for m in range(32):
            rm[hb + m + 32, hb + m] = -1.0
            rm[hb + m, hb + m + 32] = 1.0
    ob = np.zeros((128, 128), np.float32)
    ob[0:64, 0:64] = 1.0 / 64
    ob[64:128, 64:128] = 1.0 / 64
    return rm, ob


SCALE = 64 ** -0.5


def band_masks():
    m = np.zeros((6, 128, 512), np.float32)
    p = np.arange(128)[:, None]
    f = np.arange(512)[None, :]
    for r in range(6):
        m[r] = (np.abs((r - 1) * 128 + p - f) <= 128)
    return m.astype(NPBF)


def build_kbe(NLAT, NCTX):
    NTOT = NLAT + NCTX
    NKB = NTOT // 128
    NKL = NLAT // 128
    QB = 512
    PC = 2048
    nc = bass.Bass("TRN2", target_bir_lowering=False)
    q_d = [nc.dram_tensor("q%d" % m, [128, NTOT], BF16, kind="ExternalInput").ap() for m in range(2)]
    k_d = [nc.dram_tensor("k%d" % m, [128, NTOT], BF16, kind="ExternalInput").ap() for m in range(2)]
    v_d = [nc.dram_tensor("v%d" % m, [128, NKB * 65], BF16, kind="ExternalInput").ap() for m in range(2)]
    mask_d = nc.dram_tensor("mask", [6, 128, 512], BF16, kind="ExternalInput").ap()
    sink_d = nc.dram_tensor("sink", [64, 2], F32, kind="ExternalInput").ap()
    sel_d = nc.dram_tensor("sel", [65, 64], F32, kind="ExternalInput").ap()
    y_o = nc.dram_tensor("y_o", [4, 64, NTOT], BF16, kind="ExternalOutput").ap()
    fw = FW(nc)
    npc = max(1, NLAT // PC)
    pcs = [(i * PC, PC if i < npc - 1 else NTOT - i * PC) for i in range(npc)]
    Q = []; K = []; V = []; bQ = []; bK = []; bV = []
    for m in range(2):
        Q.append(SBT(nc, "Q%d" % m, [128, NTOT], BF16)); bQ.append([Buf() for _ in pcs])
        K.append(SBT(nc, "K%d" % m, [128, NTOT], BF16)); bK.append([Buf() for _ in pcs])
        V.append(SBT(nc, "V%d" % m, [128, NKB, 65], BF16)); bV.append([Buf() for _ in pcs])
    msk = SBT(nc, "msk", [128, 6, 512], BF16); bmsk = Buf()
    snk = SBT(nc, "snk", [64, 2], F32); bsnk = Buf()
    sel = SBT(nc, "sel_s", [65, 64], F32); bsel = Buf()
    fw.dma("sp", [], [bmsk], msk[:], mask_d.rearrange("r p f -> p r f"))
    fw.dma("sp", [], [bsnk], snk[:], sink_d[:, :])
    fw.dma("sp", [], [bsel], sel[:], sel_d[:, :])
    fw.op("act", [bsnk], [bsnk], "activation", out=snk[:], in_=snk[:], func=AF.Exp)
    for m in range(2):
        for pi, (c0, cw) in enumerate(pcs):
            fw.dma("sp", [], [bK[m][pi]], K[m][:, c0:c0 + cw], k_d[m][:, c0:c0 + cw])
            fw.dma("act", [], [bQ[m][pi]], Q[m][:, c0:c0 + cw], q_d[m][:, c0:c0 + cw])
            kb0 = c0 // 128; kbn = cw // 128
            fw.dma("pool", [], [bV[m][pi]], V[m][:, kb0:kb0 + kbn, :],
                   v_d[m][:, kb0 * 65:(kb0 + kbn) * 65].rearrange("p (k d) -> p k d", d=65))

    def pc_of(col):
        return min(col // PC, npc - 1)

    NPB = 3
    pS = [PST(nc, "pS%d" % i, [128, QB], F32) for i in range(NPB)]; bpS = [Buf() for _ in range(NPB)]
    pO = [PST(nc, "pO%d" % i, [128, QB], F32) for i in range(2)]; bpO = [Buf(), Buf()]
    pB = PST(nc, "pB", [128, QB], F32); bpB = Buf()
    PT = [SBT(nc, "PT%d" % i, [128, QB], BF16) for i in range(NPB)]; bPT = [Buf() for _ in range(NPB)]
    PM = [SBT(nc, "PM%d" % i, [128, QB], BF16) for i in range(2)]; bPM = [Buf(), Buf()]
    osb = [SBT(nc, "osb%d" % i, [65, QB], F32) for i in range(2)]; bosb = [Buf(), Buf()]
    rden = SBT(nc, "rden", [64, QB], F32); brden = Buf()
    yo = [SBT(nc, "yo%d" % i, [64, QB], BF16) for i in range(2)]; byo = [Buf(), Buf()]
    cnt = {"s": 0, "o": 0, "m": 0}

    def attn(m, hh, q0, qn, kbl, use_sink):
        hs = slice(hh * 64, (hh + 1) * 64)
        oi = cnt["o"]; cnt["o"] += 1
        po = pO[oi % 2]; bpo = bpO[oi % 2]
        bq = bQ[m][pc_of(q0)]
        for n, (kb, mi) in enumerate(kbl):
            si = cnt["s"]; cnt["s"] += 1
            ps = pS[si % NPB]; bps = bpS[si % NPB]
            pt = PT[si % NPB]; bpt = bPT[si % NPB]
            fw.op("pe", [bK[m][pc_of(kb * 128)], bq], [bps], "matmul", ps[:, 0:qn],
                  lhsT=K[m][hs, kb * 128:(kb + 1) * 128], rhs=Q[m][hs, q0:q0 + qn], start=True, stop=True)
            fw.op("act", [bps], [bpt], "activation", out=pt[:, 0:qn], in_=ps[:, 0:qn], func=AF.Exp, scale=SCALE)
            src = pt; bsrc = bpt
            if mi is not None:
                mm = cnt["m"]; cnt["m"] += 1
                pm = PM[mm % 2]; bpm = bPM[mm % 2]
                fw.op("dve", [bpt, bmsk], [bpm], "tensor_tensor", out=pm[:, 0:qn], in0=pt[:, 0:qn],
                      in1=msk[:, mi, 0:qn], op=ALU.mult)
                src = pm; bsrc = bpm
            fw.op("pe", [bsrc, bV[m][pc_of(kb * 128)]], [bpo], "matmul", po[0:65, 0:qn], lhsT=V[m][:, kb, 0:65],
                  rhs=src[:, 0:qn], start=(n == 0), stop=(n == len(kbl) - 1))
        ob = osb[oi % 2]; bob = bosb[oi % 2]
        fw.op("act", [bpo], [bob], "activation", out=ob[:, 0:qn], in_=po[0:65, 0:qn], func=AF.Copy)
        fw.op("pe", [bob, bsel], [bpB], "matmul", pB[0:64, 0:qn], lhsT=sel[:], rhs=ob[:, 0:qn], start=True, stop=True)
        if use_sink:
            fw.op("dve", [bpB, bsnk], [brden], "tensor_scalar", out=rden[:, 0:qn], in0=pB[0:64, 0:qn],
                  scalar1=snk[:, hh:hh + 1], scalar2=None, op0=ALU.add)
            fw.op("dve", [brden], [brden], "reciprocal", out=rden[:, 0:qn], in_=rden[:, 0:qn])
        else:
            fw.op("dve", [bpB], [brden], "reciprocal", out=rden[:, 0:qn], in_=pB[0:64, 0:qn])
        y = yo[oi % 2]; by = byo[oi % 2]
        fw.op("dve", [bob, brden], [by], "tensor_tensor", out=y[:, 0:qn], in0=ob[0:64, 0:qn], in1=rden[:, 0:qn],
              op=ALU.mult)
        fw.dma("sp", [by], [], y_o[m * 2 + hh, :, q0:q0 + qn], y[:, 0:qn], is_output=True)

    ctx_kb = [(NKL + i, None) for i in range(NCTX // 128)]
    for hh in range(2):
        for qb in range(NLAT // QB):
            kbl = []
            for r in range(6):
                kb = 4 * qb + r - 1
                if 0 <= kb < NKL:
                    kbl.append((kb, r))
            attn(1, hh, qb * QB, QB, kbl + ctx_kb, True)
        attn(1, hh, NLAT, NCTX, ctx_kb, True)
    for hh in range(2):
        for qb in range(NLAT // QB):
            attn(0, hh, qb * QB, QB, [(kb, None) for kb in range(NKL)] + ctx_kb, False)
        attn(0, hh, NLAT, NCTX, ctx_kb, False)
    fw.finish()
    return nc


def vaug_layout(v):
    ntot = v.shape[0]
    nkb = ntot // 128
    o = np.ones((128, nkb, 65), v.dtype)
    o[:, :, 0:64] = v.reshape(nkb, 128, 64).transpose(1, 0, 2)
    return np.ascontiguousarray(o.reshape(128, nkb * 65))


def sel_const():
    s = np.zeros((65, 64), np.float32)
    s[64, :] = 1.0
    return s


GN_EPS = 64e-5
LAM = -float(np.exp(-0.5))
CH = 64


def rwkv_consts():
    p = np.arange(128)[:, None] % 64
    f = np.arange(64)[None, :]
    su = (f > p).astype(np.float32)
    sl = (f < p).astype(np.float32)
    iu = (f >= p).astype(np.float32)
    il = (f <= p).astype(np.float32)
    mk = np.zeros((2, 128, 5, 64), np.float32)
    mk[0] = np.stack([su, iu, su, iu, sl], 1)
    mk[1] = np.stack([sl, il, sl, il, su], 1)
    ob1 = np.zeros((128, 128), np.float32)
    ob1[:64, :64] = 1.0
    ob1[64:, 64:] = 1.0
    sm = np.ones((128, 512), np.float32)
    sm[:, ::64] = 0.0
    id2 = (p == f).astype(np.float32)
    return mk, ob1, sm, id2


def build_kbr(NLAT, NCTX, dbg=3, dumps=None):
    NTOT = NLAT + NCTX
    TBK = 512
    nc = bass.Bass("TRN2", target_bir_lowering=False)
    z_d = nc.dram_tensor("z", [5, 128, NTOT], F32, kind="ExternalInput").ap()
    pv_d = nc.dram_tensor("pv", [128, 14], F32, kind="ExternalInput").ap()
    w2_d = nc.dram_tensor("w2", [2, 64, 128], F32, kind="ExternalInput").ap()
    a2_d = nc.dram_tensor("a2", [2, 64, 128], F32, kind="ExternalInput").ap()
    g2_d = nc.dram_tensor("g2", [128, 128], F32, kind="ExternalInput").ap()
    mk_d = nc.dram_tensor("mk", [2, 128, 5, 64], F32, kind="ExternalInput").ap()
    ob1_d = nc.dram_tensor("ob1", [128, 128], F32, kind="ExternalInput").ap()
    sm_d = nc.dram_tensor("sm", [128, 512], F32, kind="ExternalInput").ap()
    id2_d = nc.dram_tensor("id2", [128, 64], F32, kind="ExternalInput").ap()
    yd_o = nc.dram_tensor("yd_o", [128, NTOT], BF16, kind="ExternalOutput").ap()
    yf_o = nc.dram_tensor("yf_o", [128, NTOT], F32, kind="ExternalOutput").ap()
    fw = FW(nc)

    def sb(name, shape, dt=F32):
        return SBT(nc, name, shape, dt), Buf(name)

    pv, bpv = sb("pv_s", [128, 14])
    pd, bpd = sb("pd_s", [128, 16])
    w2s, bw2 = sb("w2_s", [64, 2, 128])
    a2s, ba2 = sb("a2_s", [128, 2, 128])
    g2s, bg2 = sb("g2_s", [128, 128])
    mk, bmk = sb("mk_s", [128, 2, 5, 64])
    ob1, bob1 = sb("ob1_s", [128, 128])
    ob64, bob64 = sb("ob64_s", [128, 128])
    sm, bsm = sb("sm_s", [128, 512])
    id2, bid2 = sb("id2_s", [128, 64])
    ident, bid = make_ident(fw, nc)
    fw.dma("sp", [], [bpv], pv[:], pv_d[:, :])
    fw.dma("sp", [], [bw2], w2s[:], w2_d.rearrange("d k n -> k d n"))
    fw.dma("sp", [], [ba2], a2s[64:128, :, :], a2_d.rearrange("d k n -> k d n"))
    fw.dma("sp", [], [bg2], g2s[:], g2_d[:, :])
    fw.dma("act", [], [bmk], mk[:], mk_d.rearrange("d p a f -> p d a f"))
    fw.dma("act", [], [bob1], ob1[:], ob1_d[:, :])
    fw.dma("act", [], [bsm], sm[:], sm_d[:, :])
    fw.dma("act", [], [bid2], id2[:], id2_d[:, :])
    fw.op("dve", [bob1], [bob64], "tensor_scalar", out=ob64[:], in0=ob1[:], scalar1=1.0 / 64, scalar2=None, op0=ALU.mult)
    fw.op("dve", [bpv], [bpd], "tensor_scalar", out=pd[:, 0:5], in0=pv[:, 0:5], scalar1=-1.0, scalar2=1.0,
          op0=ALU.mult, op1=ALU.add)
    fw.op("dve", [bpv], [bpd], "tensor_scalar", out=pd[:, 5:10], in0=pv[:, 0:5], scalar1=0.5, scalar2=None, op0=ALU.mult)
    fw.op("dve", [bpv], [bpd], "tensor_scalar", out=pd[:, 10:11], in0=pv[:, 10:11], scalar1=-1.0, scalar2=1.0,
          op0=ALU.mult, op1=ALU.add)
    fw.op("dve", [bpv], [bpd], "tensor_scalar", out=pd[:, 11:12], in0=pv[:, 10:11], scalar1=-2.0, scalar2=2.0,
          op0=ALU.mult, op1=ALU.add)

    NZ = 2
    zt = [[sb("zt%d_%d" % (i, q), [128, TBK + 2]) for q in range(5)] for i in range(NZ)]
    zs = [sb("zs%d" % q, [128, TBK]) for q in range(5)]
    names = ["tmp", "aa", "tws", "sg", "ad", "af", "kk", "kk2", "rin", "kkn", "tt", "khat", "bt", "c1", "c0", "cr", "c2",
             "ei", "ee", "en", "eh", "Bt", "Kt", "Bh", "Kh", "yb", "yfb", "yy", "yc", "yc2", "rs2", "sgd", "ks", "pr"]
    T = {n: sb("r_" + n, [128, TBK]) for n in names}
    AR, bAR = sb("AR", [128, TBK // CH, 2, CH])
    gC, bgC = sb("gC", [128, TBK // CH])
    ob_out, bob_out = sb("ob_out", [128, TBK], BF16)
    Msb = [sb("Msb%d" % i, [128, 5, CH]) for i in range(2)]
    PP = [sb("PP%d" % i, [128, 2, CH]) for i in range(2)]
    XX = [sb("XX%d" % i, [128, CH]) for i in range(2)]
    TM = [sb("TM%d" % i, [128, 3, CH]) for i in range(2)]
    W1, bW1 = sb("W1", [128, CH])
    U, bU = sb("U", [128, CH])
    ST = [sb("ST%d" % i, [128, CH]) for i in range(2)]
    pL = [(PST(nc, "pL%d" % i, [128, TBK], F32), Buf(excl=True)) for i in range(2)]
    pA = (PST(nc, "pA", [128, 512], F32), Buf(excl=True))
    pI = (PST(nc, "pI", [128, 512], F32), Buf(excl=True))
    pTt = (PST(nc, "pTt", [128, 512], F32), Buf(excl=True))
    pSt = [(PST(nc, "pSt%d" % i, [128, 512], F32), Buf(excl=True)) for i in range(2)]
    cnt = {"z": 0, "l": 0, "m": 0, "x": 0, "t": 0, "s": 0, "st": 0}

    def lora_ps():
        p = pL[cnt["l"] % 2]; cnt["l"] += 1
        return p

    dumped = set()

    def dump(name, ap, b, shape):
        if dumps is None or name in dumped:
            return
        dumped.add(name)
        dt_ = nc.dram_tensor("dbg_" + name, list(shape), F32, kind="ExternalOutput").ap()
        fw.dma("sp", [b], [], dt_, ap, is_output=True)
        dumps.append(name)

    def prep(t0, tb, lo, hi, d):
        nch = tb // CH
        zi = zt[cnt["z"] % NZ]; cnt["z"] += 1
        a = max(lo, t0 - 1); b = min(hi, t0 + tb + 1)
        for q in range(5):
            z, bz = zi[q]
            if t0 - 1 < lo:
                fw.op("pool", [], [bz], "memset", z[:, 0:1], 0.0)
            if t0 + tb + 1 > hi:
                fw.op("pool", [], [bz], "memset", z[:, tb + 1:tb + 2], 0.0)
            fw.dma("sp" if q % 2 == 0 else "act", [], [bz], z[:, a - (t0 - 1):b - (t0 - 1)], z_d[q, :, a:b])
            tmp, btmp = T["tmp"]; aa, baa = T["aa"]
            fw.op("pool", [bz], [btmp], "tensor_tensor", out=tmp[:, 0:tb], in0=z[:, 0:tb], in1=z[:, 2:tb + 2], op=ALU.add)
            fw.op("dve", [bz, bpd], [baa], "tensor_scalar", out=aa[:, 0:tb], in0=z[:, 1:tb + 1], scalar1=pd[:, q:q + 1],
                  scalar2=None, op0=ALU.mult)
            fw.op("dve", [btmp, baa, bpd], [zs[q][1]], "scalar_tensor_tensor", out=zs[q][0][:, 0:tb], in0=tmp[:, 0:tb],
                  scalar=pd[:, 5 + q:6 + q], in1=aa[:, 0:tb], op0=ALU.mult, op1=ALU.add)
        (zr, bzr), (zk, bzk), (zv, bzv), (zw, bzw), (zg, bzg) = zs
        tws, btws = T["tws"]
        fw.op("act", [bzw], [btws], "activation", out=tws[0:64, 0:tb], in_=zw[0:64, 0:tb], func=AF.Tanh)
        pw, bpw = lora_ps()
        fw.op("pe", [btws, bw2], [bpw], "matmul", pw[:, 0:tb], lhsT=w2s[:, d, :], rhs=tws[0:64, 0:tb], start=True, stop=True)
        sg, bsg = T["sg"]
        fw.op("act", [bpw, bpv], [bsg], "activation", out=sg[:, 0:tb], in_=pw[:, 0:tb], func=AF.Sigmoid,
              bias=pv[:, 5 + d:6 + d])
        pa, bpa = lora_ps()
        fw.op("pe", [bzw, ba2], [bpa], "matmul", pa[:, 0:tb], lhsT=a2s[64:128, d, :], rhs=zw[64:128, 0:tb],
              start=True, stop=True)
        ad, bad = T["ad"]
        fw.op("act", [bpa, bpv], [bad], "activation", out=ad[:, 0:tb], in_=pa[:, 0:tb], func=AF.Sigmoid,
              bias=pv[:, 7 + d:8 + d])
        kk, bkk = T["kk"]; kk2, bkk2 = T["kk2"]; rin, brin = T["rin"]; kkn, bkkn = T["kkn"]
        fw.op("dve", [bzk, bpv], [bkk], "tensor_scalar", out=kk[:, 0:tb], in0=zk[:, 0:tb], scalar1=pv[:, 9:10],
              scalar2=None, op0=ALU.mult)
        fw.op("pool", [bkk], [bkk2], "tensor_tensor", out=kk2[:, 0:tb], in0=kk[:, 0:tb], in1=kk[:, 0:tb], op=ALU.mult)
        pss, bpss = lora_ps()
        fw.op("pe", [bkk2, bob1], [bpss], "matmul", pss[:, 0:tb], lhsT=ob1[:], rhs=kk2[:, 0:tb], start=True, stop=True)
        fw.op("act", [bpss], [brin], "activation", out=rin[:, 0:tb], in_=pss[:, 0:tb], func=AF.Sqrt)
        fw.op("dve", [brin], [brin], "tensor_scalar", out=rin[:, 0:tb], in0=rin[:, 0:tb], scalar1=1e-12, scalar2=None,
              op0=ALU.max)
        fw.op("dve", [brin], [brin], "reciprocal", out=rin[:, 0:tb], in_=rin[:, 0:tb])
        fw.op("pool", [bkk, brin], [bkkn], "tensor_tensor", out=kkn[:, 0:tb], in0=kk[:, 0:tb], in1=rin[:, 0:tb], op=ALU.mult)
        tt, btt = T["tt"]; khat, bkhat = T["khat"]; bt, bbt = T["bt"]
        fw.op("dve", [bad, bpv, bpd], [btt], "tensor_scalar", out=tt[:, 0:tb], in0=ad[:, 0:tb], scalar1=pv[:, 10:11],
              scalar2=pd[:, 10:11], op0=ALU.mult, op1=ALU.add)
        fw.op("pool", [bzk, btt], [bkhat], "tensor_tensor", out=khat[:, 0:tb], in0=zk[:, 0:tb], in1=tt[:, 0:tb], op=ALU.mult)
        fw.op("pool", [bkkn, bad], [bbt], "tensor_tensor", out=bt[:, 0:tb], in0=kkn[:, 0:tb], in1=ad[:, 0:tb], op=ALU.mult)
        c1, bc1 = T["c1"]; c0, bc0 = T["c0"]; cr, bcr = T["cr"]; c2, bc2 = T["c2"]
        fw.op("dve", [bsm, bsg], [bc1], "tensor_tensor_scan", out=c1[:, 0:tb], data0=sm[:, 0:tb], data1=sg[:, 0:tb],
              initial=0.0, op0=ALU.mult, op1=ALU.add)
        fw.op("pool", [bc1, bsg], [bc0], "tensor_tensor", out=c0[:, 0:tb], in0=c1[:, 0:tb], in1=sg[:, 0:tb], op=ALU.subtract)
        c1v = c1[:, 0:tb].rearrange("p (c j) -> p c j", j=CH)
        totb = c1v[:, :, CH - 1:CH].to_broadcast([128, nch, CH])
        fw.op("dve", [bc1], [bcr], "tensor_tensor", out=cr[:, 0:tb].rearrange("p (c j) -> p c j", j=CH), in0=totb,
              in1=c1v, op=ALU.subtract)
        if d == 0:
            incl, bincl, excl, bexcl, hn, bhn = c1, bc1, c0, bc0, cr, bcr
        else:
            fw.op("pool", [bcr, bsg], [bc2], "tensor_tensor", out=c2[:, 0:tb], in0=cr[:, 0:tb], in1=sg[:, 0:tb], op=ALU.add)
            incl, bincl, excl, bexcl, hn, bhn = c2, bc2, cr, bcr, c0, bc0
        ei, bei = T["ei"]; ee, bee = T["ee"]; en, ben = T["en"]; eh, beh = T["eh"]
        fw.op("act", [bincl], [bei], "activation", out=ei[:, 0:tb], in_=incl[:, 0:tb], func=AF.Exp, scale=LAM)
        fw.op("act", [bexcl], [bee], "activation", out=ee[:, 0:tb], in_=excl[:, 0:tb], func=AF.Exp, scale=LAM)
        fw.op("act", [bincl], [ben], "activation", out=en[:, 0:tb], in_=incl[:, 0:tb], func=AF.Exp, scale=-LAM)
        fw.op("act", [bhn], [beh], "activation", out=eh[:, 0:tb], in_=hn[:, 0:tb], func=AF.Exp, scale=LAM)
        fw.op("act", [bc1], [bgC], "activation", out=gC[:, 0:nch], in_=c1v[:, :, CH - 1], func=AF.Exp, scale=LAM)
        v3 = lambda ap: ap[:, 0:tb].rearrange("p (c j) -> p c j", j=CH)
        fw.op("dve", [bkkn, bee], [bAR], "scalar_tensor_tensor", out=AR[:, 0:nch, 0, :], in0=v3(kkn), scalar=-1.0,
              in1=v3(ee), op0=ALU.mult, op1=ALU.mult)
        fw.op("pool", [bzr, bei], [bAR], "tensor_tensor", out=AR[:, 0:nch, 1, :], in0=v3(zr), in1=v3(ei), op=ALU.mult)
        for (nm, x0, bx0, e0, be0) in (("Bt", bt, bbt, en, ben), ("Kt", khat, bkhat, en, ben), ("Bh", bt, bbt, eh, beh),
                                       ("Kh", khat, bkhat, eh, beh)):
            eng = "dve" if nm in ("Bt", "Bh") else "pool"
            fw.op(eng, [bx0, be0], [T[nm][1]], "tensor_tensor", out=T[nm][0][:, 0:tb], in0=x0[:, 0:tb], in1=e0[:, 0:tb],
                  op=ALU.mult)
        for q in range(5):
            dump("zs%d" % q, zs[q][0][:, 0:tb], zs[q][1], [128, tb])
        for nm in ("sg", "ad", "kkn", "khat", "bt", "c1", "c0", "cr", "ei", "ee", "en", "eh", "Bt", "Kt", "Bh", "Kh"):
            dump(nm, T[nm][0][:, 0:tb], T[nm][1], [128, tb])
        dump("AR", AR[:, 0:nch, :, :], bAR, [128, nch, 2, CH])
        dump("gC", gC[:, 0:nch], bgC, [128, nch])

    def chunk(c, d, ycol):
        cs = slice(c * CH, (c + 1) * CH)
        Bt, bBt = T["Bt"]; Kt, bKt = T["Kt"]; Bh, bBh = T["Bh"]; Kh, bKh = T["Kh"]
        zv, bzv = zs[2]
        pa, bpa = pA
        for h in range(2):
            hs = slice(h * 64, (h + 1) * 64)
            arf = AR[hs, c, :, :].rearrange("p a j -> p (a j)")
            fw.op("pe", [bBt, bAR], [bpa], "matmul", pa[hs, 0:128], lhsT=Bt[hs, cs], rhs=arf, start=True, stop=True)
            fw.op("pe", [bKt, bAR], [bpa], "matmul", pa[hs, 128:256], lhsT=Kt[hs, cs], rhs=arf, start=True, stop=True)
            fw.op("pe", [bBt, bAR], [bpa], "matmul", pa[hs, 256:320], lhsT=AR[hs, c, 0, :], rhs=Bt[hs, cs],
                  start=True, stop=True)
        M, bM = Msb[cnt["m"] % 2]; cnt["m"] += 1
        fw.op("dve", [bpa, bmk], [bM], "tensor_tensor", out=M[:], in0=pa[:, 0:320].rearrange("p (a j) -> p a j", j=CH),
              in1=mk[:, d, :, :], op=ALU.mult)
        dump("M", M[:], bM, [128, 5, CH])
        if dbg == 21:
            return
        P, bP = PP[cnt["x"] % 2]
        X, bX = XX[cnt["x"] % 2]
        cnt["x"] += 1
        fw.op("pool", [bM, bid2], [bX], "tensor_tensor", out=X[:], in0=M[:, 0, :], in1=id2[:], op=ALU.add)
        pi_, bpi = pI
        curP = (M, bM, 0, 4)
        for i in range(1, 6):
            t_, bt_, ip, ipt = curP
            for h in range(2):
                hs = slice(h * 64, (h + 1) * 64)
                if i < 5:
                    fw.op("pe", [bt_], [bpi], "matmul", pi_[hs, 0:64], lhsT=t_[hs, ipt, :], rhs=t_[hs, ip, :],
                          start=True, stop=True)
                fw.op("pe", [bt_], [bpi], "matmul", pi_[hs, 64:128], lhsT=t_[hs, ip, :], rhs=t_[hs, ipt, :],
                      start=True, stop=True)
            Pn, bPn = PP[cnt["x"] % 2]; cnt["x"] += 1
            if i < 5:
                fw.op("act", [bpi], [bPn], "activation", out=Pn[:], in_=pi_[:, 0:128].rearrange("p (a j) -> p a j", j=CH),
                      func=AF.Copy)
            else:
                fw.op("act", [bpi], [bPn], "activation", out=Pn[:, 1, :], in_=pi_[:, 64:128], func=AF.Copy)
            for h in range(2):
                hs = slice(h * 64, (h + 1) * 64)
                fw.op("pe", [bPn, bX], [bpi], "matmul", pi_[hs, 128:192], lhsT=Pn[hs, 1, :], rhs=X[hs, :],
                      start=True, stop=True)
            Xn, bXn = XX[cnt["x"] % 2]
            fw.op("dve", [bpi, bX], [bXn], "tensor_tensor", out=Xn[:], in0=pi_[:, 128:192], in1=X[:], op=ALU.add)
            X, bX = Xn, bXn
            curP = (Pn, bPn, 0, 1)
        dump("X", X[:], bX, [128, CH])
        if dbg == 22:
            return
        ptt, bptt = pTt
        for h in range(2):
            hs = slice(h * 64, (h + 1) * 64)
            fw.op("pe", [bzv, bid], [bptt], "matmul", ptt[hs, 0:64], lhsT=zv[hs, cs], rhs=ident[hs, hs], start=True, stop=True)
            fw.op("pe", [bBh, bid], [bptt], "matmul", ptt[hs, 64:128], lhsT=Bh[hs, cs], rhs=ident[hs, hs], start=True, stop=True)
            fw.op("pe", [bKh, bid], [bptt], "matmul", ptt[hs, 128:192], lhsT=Kh[hs, cs], rhs=ident[hs, hs], start=True, stop=True)
        tm, btm = TM[cnt["t"] % 2]; cnt["t"] += 1
        fw.op("act", [bptt], [btm], "activation", out=tm[:], in_=ptt[:, 0:192].rearrange("p (a j) -> p a j", j=CH),
              func=AF.Copy)
        dump("tm", tm[:], btm, [128, 3, CH])
        if dbg == 23:
            return
        S0, bS0 = ST[cnt["st"] % 2]
        S1, bS1 = ST[(cnt["st"] + 1) % 2]
        cnt["st"] += 1
        ps, bps = pSt[cnt["s"] % 2]; cnt["s"] += 1
        for h in range(2):
            hs = slice(h * 64, (h + 1) * 64)
            fw.op("pe", [bAR, bS0], [bps], "matmul", ps[hs, 0:64], lhsT=AR[hs, c, 0, :], rhs=S0[hs, :], start=True, stop=False)
            fw.op("pe", [bM, btm], [bps], "matmul", ps[hs, 0:64], lhsT=M[hs, 2, :], rhs=tm[hs, 0, :], start=False, stop=True)
        fw.op("act", [bps], [bW1], "activation", out=W1[:], in_=ps[:, 0:64], func=AF.Copy)
        for h in range(2):
            hs = slice(h * 64, (h + 1) * 64)
            fw.op("pe", [bX, bW1], [bps], "matmul", ps[hs, 64:128], lhsT=X[hs, :], rhs=W1[hs, :], start=True, stop=True)
        fw.op("act", [bps], [bU], "activation", out=U[:], in_=ps[:, 64:128], func=AF.Copy)
        for h in range(2):
            hs = slice(h * 64, (h + 1) * 64)
            fw.op("pe", [btm, bU], [bps], "matmul", ps[hs, 128:192], lhsT=tm[hs, 1, :], rhs=U[hs, :], start=True, stop=False)
            fw.op("pe", [btm], [bps], "matmul", ps[hs, 128:192], lhsT=tm[hs, 2, :], rhs=tm[hs, 0, :], start=False, stop=True)
        for h in range(2):
            hs = slice(h * 64, (h + 1) * 64)
            fw.op("pe", [bS0, bAR], [bps], "matmul", ps[hs, 192:256], lhsT=S0[hs, :], rhs=AR[hs, c, 1, :], start=True, stop=False)
            fw.op("pe", [bU, bM], [bps], "matmul", ps[hs, 192:256], lhsT=U[hs, :], rhs=M[hs, 1, :], start=False, stop=False)
            fw.op("pe", [btm, bM], [bps], "matmul", ps[hs, 192:256], lhsT=tm[hs, 0, :], rhs=M[hs, 3, :], start=False, stop=True)
        fw.op("dve", [bS0, bgC, bps], [bS1], "scalar_tensor_tensor", out=S1[:], in0=S0[:], scalar=gC[:, c:c + 1],
              in1=ps[:, 128:192], op0=ALU.mult, op1=ALU.add)
        yb, byb = T["yb"]
        fw.op("act", [bps], [byb], "activation", out=yb[:, cs], in_=ps[:, 192:256], func=AF.Copy)
        dump("W1", W1[:], bW1, [128, CH])
        dump("U", U[:], bU, [128, CH])
        dump("S1", S1[:], bS1, [128, CH])
        dump("ybc", yb[:, cs], byb, [128, CH])

    def readout(t0, tb):
        (zr, bzr), (zk, bzk), (zv, bzv), (zw, bzw), (zg, bzg) = zs
        yb, byb = T["yb"]; yfb, byfb = T["yfb"]; yy, byy = T["yy"]; yc, byc = T["yc"]; yc2, byc2 = T["yc2"]
        rs2, brs2 = T["rs2"]; sgd, bsgd = T["sgd"]; ks, bks = T["ks"]; pr, bpr = T["pr"]; af, baf = T["af"]
        ad, bad = T["ad"]
        fw.dma("sp", [byf_dram], [byfb], yfb[:, 0:tb], yf_o[:, t0:t0 + tb])
        fw.op("dve", [byb, byfb], [byy], "tensor_tensor", out=yy[:, 0:tb], in0=yb[:, 0:tb], in1=yfb[:, 0:tb], op=ALU.add)
        pm, bpm = lora_ps()
        fw.op("pe", [byy, bob64], [bpm], "matmul", pm[:, 0:tb], lhsT=ob64[:], rhs=yy[:, 0:tb], start=True, stop=True)
        fw.op("dve", [byy, bpm], [byc], "tensor_tensor", out=yc[:, 0:tb], in0=yy[:, 0:tb], in1=pm[:, 0:tb], op=ALU.subtract)
        fw.op("pool", [byc], [byc2], "tensor_tensor", out=yc2[:, 0:tb], in0=yc[:, 0:tb], in1=yc[:, 0:tb], op=ALU.mult)
        pvv, bpvv = lora_ps()
        fw.op("pe", [byc2, bob64], [bpvv], "matmul", pvv[:, 0:tb], lhsT=ob64[:], rhs=yc2[:, 0:tb], start=True, stop=True)
        fw.op("dve", [bpvv], [brs2], "tensor_scalar", out=rs2[:, 0:tb], in0=pvv[:, 0:tb], scalar1=GN_EPS, scalar2=None,
              op0=ALU.add)
        fw.op("act", [brs2], [brs2], "activation", out=rs2[:, 0:tb], in_=rs2[:, 0:tb], func=AF.Sqrt)
        fw.op("dve", [brs2], [brs2], "reciprocal", out=rs2[:, 0:tb], in_=rs2[:, 0:tb])
        fw.op("pool", [byc, brs2], [byc], "tensor_tensor", out=yc[:, 0:tb], in0=yc[:, 0:tb], in1=rs2[:, 0:tb], op=ALU.mult)
        fw.op("dve", [byc, bpv], [byc], "tensor_scalar", out=yc[:, 0:tb], in0=yc[:, 0:tb], scalar1=pv[:, 12:13],
              scalar2=pv[:, 13:14], op0=ALU.mult, op1=ALU.add)
        pa, bpa = lora_ps()
        fw.op("pe", [bzw, ba2], [bpa], "matmul", pa[:, 0:tb], lhsT=a2s[64:128, 0, :], rhs=zw[64:128, 0:tb],
              start=True, stop=True)
        fw.op("act", [bpa, bpv], [baf], "activation", out=af[:, 0:tb], in_=pa[:, 0:tb], func=AF.Sigmoid, bias=pv[:, 7:8])
        fw.op("pool", [baf, bad], [baf], "tensor_tensor", out=af[:, 0:tb], in0=af[:, 0:tb], in1=ad[:, 0:tb], op=ALU.add)
        fw.op("dve", [baf, bpv, bpd], [baf], "tensor_scalar", out=af[:, 0:tb], in0=af[:, 0:tb], scalar1=pv[:, 10:11],
              scalar2=pd[:, 11:12], op0=ALU.mult, op1=ALU.add)
        fw.op("pool", [bzk, baf], [bks], "tensor_tensor", out=ks[:, 0:tb], in0=zk[:, 0:tb], in1=af[:, 0:tb], op=ALU.mult)
        fw.op("dve", [bzr, bpv, bks], [bpr], "scalar_tensor_tensor", out=pr[:, 0:tb], in0=zr[:, 0:tb], scalar=pv[:, 11:12],
              in1=ks[:, 0:tb], op0=ALU.mult, op1=ALU.mult)
        pb, bpb = lora_ps()
        fw.op("pe", [bpr, bob1], [bpb], "matmul", pb[:, 0:tb], lhsT=ob1[:], rhs=pr[:, 0:tb], start=True, stop=True)
        fw.op("dve", [bpb, bzv], [bks], "tensor_tensor", out=ks[:, 0:tb], in0=pb[:, 0:tb], in1=zv[:, 0:tb], op=ALU.mult)
        fw.op("pool", [bks, byc], [byc], "tensor_tensor", out=yc[:, 0:tb], in0=yc[:, 0:tb], in1=ks[:, 0:tb], op=ALU.add)
        fw.op("act", [bzg], [bsgd], "activation", out=sgd[:, 0:tb], in_=zg[:, 0:tb], func=AF.Sigmoid)
        pg, bpg = lora_ps()
        fw.op("pe", [bsgd, bg2], [bpg], "matmul", pg[:, 0:tb], lhsT=g2s[:], rhs=sgd[:, 0:tb], start=True, stop=True)
        fw.op("dve", [bpg, byc], [bob_out], "tensor_tensor", out=ob_out[:, 0:tb], in0=pg[:, 0:tb], in1=yc[:, 0:tb], op=ALU.mult)
        fw.dma("sp", [bob_out], [], yd_o[:, t0:t0 + tb], ob_out[:, 0:tb], is_output=True)

    byf_dram = Buf("yf_dram")
    lat_blocks = [(t0, min(TBK, NLAT - t0), 0, NLAT) for t0 in range(0, NLAT, TBK)]
    ctx_blocks = [(NLAT + t0, min(TBK, NCTX - t0), NLAT, NTOT) for t0 in range(0, NCTX, TBK)]
    for d in range(2):
        S0, bS0 = ST[cnt["st"] % 2]
        fw.op("pool", [], [bS0], "memset", S0[:], 0.0)
        if d == 0:
            order = ctx_blocks + lat_blocks
        else:
            order = ctx_blocks[::-1] + lat_blocks[::-1]
        for (t0, tb, lo, hi) in order:
            prep(t0, tb, lo, hi, d)
            nch = tb // CH
            chs = range(nch) if d == 0 else range(nch - 1, -1, -1)
            for c in chs:
                if dbg >= 2:
                    chunk(c, d, None)
            if dbg != 3:
                continue
            if d == 0:
                yb, byb = T["yb"]
                fw.dma("sp", [byb], [byf_dram], yf_o[:, t0:t0 + tb], yb[:, 0:tb], is_output=True)
            else:
                readout(t0, tb)
    fw.finish()
    return nc


def natten_patterns(rows):
    def rs(r):
        return int(np.clip(r - 4, 0, rows - 8))
    pats = {}
    per_pair = []
    nkl = rows // 2
    for r in range(0, rows, 2):
        ws = rs(r)
        kb0 = ws // 2
        nblk = min(5, nkl - kb0)
        key = (rs(r) - r, rs(r + 1) - (r + 1), nblk)
        if key not in pats:
            pats[key] = (len(pats), r)
        per_pair.append((pats[key][0], kb0, nblk))
    return pats, per_pair


def natten_bias_tables(rpb2, rows):
    pats, _ = natten_patterns(rows)
    def rs(r):
        return int(np.clip(r - 4, 0, rows - 8))
    tab = np.full((len(pats), 2, 128, 5, 128), -30000.0, np.float32)
    p = np.arange(128)
    f = np.arange(128)
    kc = (p % 64)[:, None]
    c = (f % 64)[None, :]
    csc = np.clip(c - 8, 0, 48)
    colok = (kc >= csc) & (kc < csc + 16)
    dcol = kc - c + 15
    for key, (pid, r) in pats.items():
        ws = rs(r)
        for blk in range(key[2]):
            keyrow = (ws + 2 * blk + p // 64)[:, None]
            qrow = (r + f // 64)[None, :]
            rsq = np.clip(qrow - 4, 0, rows - 8)
            ok = colok & (keyrow >= rsq) & (keyrow < rsq + 8)
            drow = keyrow - qrow + 7
            idx_r = np.clip(drow, 0, 14)
            idx_c = np.clip(dcol, 0, 30)
            for hh in range(2):
                vals = rpb2[hh][idx_r, idx_c]
                tab[pid, hh, :, blk, :] = np.where(ok, vals, np.float32(-30000.0))
    return tab


def build_kbn(NLAT, NCTX):
    NTOT = NLAT + NCTX
    NKB = NTOT // 128
    NKL = NLAT // 128
    rows = NLAT // 64
    pats, per_pair = natten_patterns(rows)
    NP = len(pats)
    PC = 2048
    nc = bass.Bass("TRN2", target_bir_lowering=False)
    q_d = nc.dram_tensor("q", [128, NTOT], BF16, kind="ExternalInput").ap()
    k_d = nc.dram_tensor("k", [128, NTOT], BF16, kind="ExternalInput").ap()
    v_d = [nc.dram_tensor("v%d" % m, [128, NKB * 65], BF16, kind="ExternalInput").ap() for m in range(2)]
    tab_d = nc.dram_tensor("tab", [NP, 2, 128, 5, 128], F32, kind="ExternalInput").ap()
    sel_d = nc.dram_tensor("sel", [65, 64], F32, kind="ExternalInput").ap()
    y_o = nc.dram_tensor("y_o", [2, 64, NTOT], BF16, kind="ExternalOutput").ap()
    fw = FW(nc)
    npc = max(1, NLAT // PC)
    pcs = [(i * PC, PC if i < npc - 1 else NTOT - i * PC) for i in range(npc)]
    Q = SBT(nc, "Q", [128, NTOT], BF16); bQ = [Buf() for _ in pcs]
    K = SBT(nc, "K", [128, NTOT], BF16); bK = [Buf() for _ in pcs]
    V = [SBT(nc, "V%d" % m, [128, NKB, 65], BF16) for m in range(2)]; bV = [[Buf() for _ in pcs] for m in range(2)]
    E = SBT(nc, "E", [128, NP * 2, 5, 128], F32); bE = Buf()
    sel = SBT(nc, "sel_s", [65, 64], F32); bsel = Buf()
    fw.dma("sp", [], [bsel], sel[:], sel_d[:, :])
    for pid in range(NP):
        for hh in range(2):
            fw.dma("sp" if hh == 0 else "act", [], [bE], E[:, pid * 2 + hh, :, :], tab_d[pid, hh])
    for pid in range(NP):
        fw.op("act", [bE], [bE], "activation", out=E[:, pid * 2:pid * 2 + 2, :, :], in_=E[:, pid * 2:pid * 2 + 2, :, :],
              func=AF.Exp)
    for pi, (c0, cw) in enumerate(pcs):
        fw.dma("sp", [], [bK[pi]], K[:, c0:c0 + cw], k_d[:, c0:c0 + cw])
        fw.dma("act", [], [bQ[pi]], Q[:, c0:c0 + cw], q_d[:, c0:c0 + cw])
        kb0 = c0 // 128; kbn = cw // 128
        for m in range(2):
            fw.dma("pool", [], [bV[m][pi]], V[m][:, kb0:kb0 + kbn, :],
                   v_d[m][:, kb0 * 65:(kb0 + kbn) * 65].rearrange("p (k d) -> p k d", d=65))

    def pc_of(col):
        return min(col // PC, npc - 1)

    pS = [PST(nc, "pS%d" % i, [128, 8, 128], F32) for i in range(2)]; bpS = [Buf(excl=True) for _ in range(2)]
    pO = [PST(nc, "pO%d" % i, [128, 512], F32) for i in range(2)]; bpO = [Buf(excl=True), Buf(excl=True)]
    pB = PST(nc, "pB", [128, 512], F32); bpB = Buf(excl=True)
    PTf = [SBT(nc, "PTf%d" % i, [128, 5, 128], F32) for i in range(2)]; bPTf = [Buf(), Buf()]
    PTb = [SBT(nc, "PTb%d" % i, [128, 7, 128], BF16) for i in range(2)]; bPTb = [Buf(), Buf()]
    osb = [SBT(nc, "osb%d" % i, [65, 512], F32) for i in range(2)]; bosb = [Buf(), Buf()]
    rden = SBT(nc, "rden", [64, 512], F32); brden = Buf()
    yo = [SBT(nc, "yo%d" % i, [64, 512], BF16) for i in range(2)]; byo = [Buf(), Buf()]
    cnt = {"s": 0, "o": 0}
    nctxb = NCTX // 128

    def epilogue(po, bpo, oi, hh, q0, qn):
        ob = osb[oi % 2]; bob = bosb[oi % 2]
        fw.op("act", [bpo], [bob], "activation", out=ob[:, 0:qn], in_=po[0:65, 0:qn], func=AF.Copy)
        fw.op("pe", [bob, bsel], [bpB], "matmul", pB[0:64, 0:qn], lhsT=sel[:], rhs=ob[:, 0:qn], start=True, stop=True)
        fw.op("dve", [bpB], [brden], "reciprocal", out=rden[:, 0:qn], in_=pB[0:64, 0:qn])
        y = yo[oi % 2]; by = byo[oi % 2]
        fw.op("dve", [bob, brden], [by], "tensor_tensor", out=y[:, 0:qn], in0=ob[0:64, 0:qn], in1=rden[:, 0:qn],
              op=ALU.mult)
        fw.dma("sp", [by], [], y_o[hh, :, q0:q0 + qn], y[:, 0:qn], is_output=True)

    for hh in range(2):
        hs = slice(hh * 64, (hh + 1) * 64)
        npairs = rows // 2
        for g0 in range(0, npairs, 4):
            oi = cnt["o"]; cnt["o"] += 1
            po = pO[oi % 2]; bpo = bpO[oi % 2]
            gn = min(4, npairs - g0)
            for sl in range(gn):
                pr = g0 + sl
                pid, kb0, nblk = per_pair[pr]
                q0 = pr * 128
                si = cnt["s"]; cnt["s"] += 1
                ps = pS[si % 2]; bps = bpS[si % 2]
                ptf = PTf[si % 2]; bptf = bPTf[si % 2]
                ptb = PTb[si % 2]; bptb = bPTb[si % 2]
                bq = bQ[pc_of(q0)]
                for blk in range(nblk):
                    kb = kb0 + blk
                    fw.op("pe", [bK[pc_of(kb * 128)], bq], [bps], "matmul", ps[:, blk, :],
                          lhsT=K[hs, kb * 128:(kb + 1) * 128], rhs=Q[hs, q0:q0 + 128], start=True, stop=True)
                for i in range(nctxb):
                    kb = NKL + i
                    fw.op("pe", [bK[pc_of(kb * 128)], bq], [bps], "matmul", ps[:, 5 + i, :],
                          lhsT=K[hs, kb * 128:(kb + 1) * 128], rhs=Q[hs, q0:q0 + 128], start=True, stop=True)
                fw.op("act", [bps], [bptf], "activation", out=ptf[:, 0:nblk, :], in_=ps[:, 0:nblk, :], func=AF.Exp,
                      scale=SCALE)
                fw.op("act", [bps], [bptb], "activation", out=ptb[:, 5:5 + nctxb, :], in_=ps[:, 5:5 + nctxb, :],
                      func=AF.Exp, scale=SCALE)
                fw.op("dve", [bptf, bE], [bptb], "tensor_tensor", out=ptb[:, 0:nblk, :], in0=ptf[:, 0:nblk, :],
                      in1=E[:, pid * 2 + hh, 0:nblk, :], op=ALU.mult)
                nmm = nblk + nctxb
                n = 0
                for blk in range(nblk):
                    kb = kb0 + blk
                    fw.op("pe", [bptb, bV[hh][pc_of(kb * 128)]], [bpo], "matmul", po[0:65, sl * 128:(sl + 1) * 128],
                          lhsT=V[hh][:, kb, 0:65], rhs=ptb[:, blk, :], start=(n == 0), stop=(n == nmm - 1))
                    n += 1
                for i in range(nctxb):
                    kb = NKL + i
                    fw.op("pe", [bptb, bV[hh][pc_of(kb * 128)]], [bpo], "matmul", po[0:65, sl * 128:(sl + 1) * 128],
                          lhsT=V[hh][:, kb, 0:65], rhs=ptb[:, 5 + i, :], start=(n == 0), stop=(n == nmm - 1))
                    n += 1
            epilogue(po, bpo, oi, hh, g0 * 128, gn * 128)
        oi = cnt["o"]; cnt["o"] += 1
        po = pO[oi % 2]; bpo = bpO[oi % 2]
        si = cnt["s"]; cnt["s"] += 1
        ps = pS[si % 2]; bps = bpS[si % 2]
        ptb = PTb[si % 2]; bptb = bPTb[si % 2]
        for i in range(nctxb):
            kb = NKL + i
            for qi in range(nctxb):
                fw.op("pe", [bK[pc_of(kb * 128)], bQ[pc_of(NLAT)]], [bps], "matmul", ps[:, i * nctxb + qi, :],
                      lhsT=K[hs, kb * 128:(kb + 1) * 128], rhs=Q[hs, NLAT + qi * 128:NLAT + (qi + 1) * 128],
                      start=True, stop=True)
        fw.op("act", [bps], [bptb], "activation", out=ptb[:, 0:nctxb * nctxb, :], in_=ps[:, 0:nctxb * nctxb, :],
              func=AF.Exp, scale=SCALE)
        for qi in range(nctxb):
            for i in range(nctxb):
                kb = NKL + i
                fw.op("pe", [bptb, bV[hh][pc_of(kb * 128)]], [bpo], "matmul", po[0:65, qi * 128:(qi + 1) * 128],
                      lhsT=V[hh][:, kb, 0:65], rhs=ptb[:, i * nctxb + qi, :], start=(i == 0), stop=(i == nctxb - 1))
        epilogue(po, bpo, oi, hh, NLAT, NCTX)
    fw.finish()
    return nc


def build_k0():
    NCOL = 3072
    nc = bass.Bass("TRN2", target_bir_lowering=False)
    cT_d = nc.dram_tensor("cT", [128, 8, 3], F32, kind="ExternalInput").ap()
    wm_d = nc.dram_tensor("wm", [D, NCOL], F32, kind="ExternalInput").ap()
    bm_d = nc.dram_tensor("bm", [3, NCOL], F32, kind="ExternalInput").ap()
    mo_d = nc.dram_tensor("mo", [3, NCOL], F32, kind="ExternalOutput").ap()
    fw = FW(nc)
    cT = SBT(nc, "cT_s", [128, 8, 3], F32); bc = Buf()
    sT = SBT(nc, "sT_s", [128, 8, 3], F32); bs = Buf()
    wm = SBT(nc, "wm_s", [128, 8, NCOL], F32); bw = [Buf() for _ in range(8)]
    bm = SBT(nc, "bm_s", [3, NCOL], F32); bb = Buf()
    mo = SBT(nc, "mo_s", [3, NCOL], F32); bmo = Buf()
    ps = [PST(nc, "ps%d" % i, [128, 512], F32) for i in range(2)]; bps = [Buf(excl=True), Buf(excl=True)]
    fw.dma("sp", [], [bc], cT[:], cT_d[:, :, :])
    fw.dma("sp", [], [bb], bm[:], bm_d[:, :])
    for k in range(8):
        fw.dma("sp" if k % 2 == 0 else "act", [], [bw[k]], wm[:, k, :], wm_d[k * 128:(k + 1) * 128, :])
    fw.op("act", [bc], [bs], "activation", out=sT[:], in_=cT[:], func=AF.Silu)
    for cb in range(NCOL // 512):
        p = ps[cb % 2]; bp = bps[cb % 2]
        for k in range(8):
            fw.op("pe", [bs, bw[k]], [bp], "matmul", p[0:3, :], lhsT=sT[:, k, :], rhs=wm[:, k, cb * 512:(cb + 1) * 512],
                  start=(k == 0), stop=(k == 7))
        fw.op("dve", [bp, bb], [bmo], "tensor_tensor", out=mo[:, cb * 512:(cb + 1) * 512], in0=p[0:3, :],
              in1=bm[:, cb * 512:(cb + 1) * 512], op=ALU.add)
    fw.dma("sp", [bmo], [], mo_d[:, :], mo[:], is_output=True)
    fw.finish()
    return nc


_NC_CACHE = {}


def _get_nc(key, builder):
    if key not in _NC_CACHE:
        _NC_CACHE[key] = builder()
    return _NC_CACHE[key]


def _run(key, builder, in_maps):
    nc = _get_nc(key, builder)
    res = run_bass_kernel_spmd(nc, in_maps, core_ids=list(range(NCORES)))
    return res.results


def _fmcol(v):
    return np.asarray(v, np.float32).reshape(8, 128).T


def _c(a):
    return np.ascontiguousarray(a)


def kernel(x, c, ctx, c_ctx, w_mod, b_mod, norm_mix, norm_ffn, w_in_even, w_out_even, q_norm_a, k_norm_a,
           sink_b, w_in_odd, w_out_odd, rpb_c, shift_mu, decay_w0, decay_w2, iclr_a0, iclr_a2, gate_g2,
           k_k, k_a, r_k, ln_x_w, ln_x_b, w_ffn_in, w_ffn_out, norm_out):
    f32 = lambda a: np.asarray(a, np.float32)
    x = f32(x); c = f32(c); ctx = f32(ctx); c_ctx = f32(c_ctx)
    w_mod = f32(w_mod); b_mod = f32(b_mod)
    B = 2
    NQ = 4
    TL = SEQ // NQ
    NTC = TL + CTX
    NTOT = SEQ + CTX
    cvec = np.stack([c[0], c[1], c_ctx], 1)
    cT = _c(cvec.reshape(8, 128, 3).transpose(1, 0, 2))
    ims = []
    for i in range(NCORES):
        l, hf = i // 2, i % 2
        ims.append({"cT": cT, "wm": _c(w_mod[l][:, hf * 3072:(hf + 1) * 3072]),
                    "bm": _c(np.broadcast_to(b_mod[l][hf * 3072:(hf + 1) * 3072], (3, 3072)))})
    r0 = _run("k0", build_k0, ims)
    mods = np.zeros((DEPTH, 3, 6 * D), np.float32)
    for i in range(NCORES):
        l, hf = i // 2, i % 2
        mods[l][:, hf * 3072:(hf + 1) * 3072] = r0[i]["mo"]

    def mv(l, g, which):
        return mods[l, g, which * D:(which + 1) * D]

    xs = [[_c(x[b, q * TL:(q + 1) * TL]) for q in range(NQ)] for b in range(B)]
    cx = [_c(ctx[b]) for b in range(B)]
    rm, ob = rope_consts()
    masks = band_masks()
    sel = sel_const()
    mk, ob1, sm, id2 = rwkv_consts()
    out = None
    for l in range(DEPTH):
        even = (l % 2 == 0)
        li = l // 2
        ims = []
        for b in range(B):
            for q in range(NQ):
                fmv = _c(np.concatenate([_fmcol(norm_mix[l]), _fmcol(mv(l, b, 1)), _fmcol(mv(l, b, 0)),
                                         _fmcol(mv(l, 2, 1)), _fmcol(mv(l, 2, 0))], 1))
                d = {"x": _c(np.concatenate([xs[b][q], cx[b]], 0)), "fmv": fmv}
                if even:
                    d["w_in"] = f32(w_in_even[li])
                    pos = np.concatenate([np.arange(TL) + q * TL, np.zeros(CTX, np.int64)])
                    isc = np.arange(NTC) >= TL
                    d["cs"] = rope_tables(pos, isc)
                    d["gv"] = _c(np.stack([np.tile(f32(q_norm_a[li]), 2), np.tile(f32(k_norm_a[li]), 2)], 1))
                    d["rm"] = rm
                    d["ob"] = ob
                else:
                    d["w_in"] = f32(w_in_odd[li])
                ims.append(d)
        ra = _run("ka_even" if even else "ka_odd", lambda: build_ka(TL, CTX, even), ims)

        def gather_fm(name, b):
            parts = [ra[b * NQ + q][name][:, :, :TL] for q in range(NQ)] + [ra[b * NQ][name][:, :, TL:]]
            return np.concatenate(parts, axis=2)

        def gather_tm(name, b):
            parts = [ra[b * NQ + q][name][:TL] for q in range(NQ)] + [ra[b * NQ][name][TL:]]
            return np.concatenate(parts, axis=0)

        yT = [np.zeros((D, NTOT), NPBF) for _ in range(B)]
        if even:
            ims = []
            for b in range(B):
                qk = gather_fm("qk_o", b)
                v = gather_tm("v_o", b)
                for j in range(4):
                    kvh = j // 2
                    ka = qk[4][kvh * 64:(kvh + 1) * 64]
                    kb = qk[9][kvh * 64:(kvh + 1) * 64]
                    ims.append({"q0": _c(qk[j]), "k0": _c(np.concatenate([ka, ka], 0)),
                                "v0": vaug_layout(_c(v[:, kvh * 64:(kvh + 1) * 64])),
                                "q1": _c(qk[5 + j]), "k1": _c(np.concatenate([kb, kb], 0)),
                                "v1": vaug_layout(_c(v[:, 128 + kvh * 64:128 + (kvh + 1) * 64])),
                                "mask": masks, "sel": sel,
                                "sink": _c(np.broadcast_to(f32(sink_b[li])[2 * j:2 * j + 2], (64, 2)))})
            rb = _run("kbe", lambda: build_kbe(SEQ, CTX), ims)
            for b in range(B):
                for j in range(4):
                    yo = rb[b * 4 + j]["y_o"]
                    for hh in range(2):
                        h = 2 * j + hh
                        yT[b][h * 64:(h + 1) * 64] = yo[hh]
                        yT[b][512 + h * 64:512 + (h + 1) * 64] = yo[2 + hh]
        else:
            imn = []
            imr = []
            hcs = [slice(j * 128, (j + 1) * 128) for j in range(4)]
            mu = f32(shift_mu[li])
            for b in range(B):
                qk = gather_fm("qk_o", b)
                v = gather_tm("v_o", b)
                zd = gather_fm("zd_o", b)
                for j in range(4):
                    hc = hcs[j]
                    imn.append({"q": _c(qk[j]), "k": _c(qk[4 + j]),
                                "v0": vaug_layout(_c(v[:, (2 * j) * 64:(2 * j + 1) * 64])),
                                "v1": vaug_layout(_c(v[:, (2 * j + 1) * 64:(2 * j + 2) * 64])),
                                "tab": natten_bias_tables(f32(rpb_c[li])[2 * j:2 * j + 2], SEQ // 64), "sel": sel})
                    pv = np.stack([mu[j * 128:(j + 1) * 128], mu[512 + j * 128:512 + (j + 1) * 128],
                                   mu[1024 + j * 128:1024 + (j + 1) * 128], mu[1536:1664], mu[1664:1792],
                                   f32(decay_w0[li])[0][hc], f32(decay_w0[li])[1][hc], f32(iclr_a0[li])[0][hc],
                                   f32(iclr_a0[li])[1][hc], f32(k_k[li])[hc], f32(k_a[li])[hc],
                                   f32(r_k[li]).reshape(-1)[hc], f32(ln_x_w[li])[hc], f32(ln_x_b[li])[hc]], 1)
                    imr.append({"z": _c(np.stack([zd[j], zd[4 + j], zd[8 + j], zd[12], zd[13]])),
                                "pv": _c(pv.astype(np.float32)), "w2": _c(f32(decay_w2[li])[:, :, hc]),
                                "a2": _c(f32(iclr_a2[li])[:, :, hc]), "g2": _c(f32(gate_g2[li])[:, hc]),
                                "mk": mk, "ob1": ob1, "sm": sm, "id2": id2})
            rn = _run("kbn", lambda: build_kbn(SEQ, CTX), imn)
            rr = _run("kbr", lambda: build_kbr(SEQ, CTX), imr)
            for b in range(B):
                for j in range(4):
                    yo = rn[b * 4 + j]["y_o"]
                    for hh in range(2):
                        h = 2 * j + hh
                        yT[b][h * 64:(h + 1) * 64] = yo[hh]
                    yT[b][512 + j * 128:512 + (j + 1) * 128] = rr[b * 4 + j]["yd_o"]
        final = (l == DEPTH - 1)
        w_o = f32(w_out_even[li]) if even else f32(w_out_odd[li])
        ims = []
        for b in range(B):
            for q in range(NQ):
                grep = np.zeros((2, 2, 128, D), np.float32)
                grep[0, 0] = mv(l, b, 2)[None]
                grep[0, 1] = mv(l, b, 5)[None]
                grep[1, 0] = mv(l, 2, 2)[None]
                grep[1, 1] = mv(l, 2, 5)[None]
                fmv = _c(np.concatenate([_fmcol(norm_ffn[l]), _fmcol(mv(l, b, 4)), _fmcol(mv(l, b, 3)),
                                         _fmcol(mv(l, 2, 4)), _fmcol(mv(l, 2, 3))], 1))
                yTc = _c(np.concatenate([yT[b][:, q * TL:(q + 1) * TL], yT[b][:, SEQ:]], 1))
                ims.append({"x": _c(np.concatenate([xs[b][q], cx[b]], 0)), "yT": yTc, "w_out": w_o,
                            "w_fi": f32(w_ffn_in[l]), "w_fo": f32(w_ffn_out[l]), "grep": grep, "fmv": fmv,
                            "norep": _c(np.broadcast_to(f32(norm_out), (128, D)))})
        rc = _run("kc_final" if final else "kc", lambda: build_kc(TL, CTX, final), ims)
        for b in range(B):
            for q in range(NQ):
                xo = rc[b * NQ + q]["xo"]
                xs[b][q] = _c(xo[:TL])
            cx[b] = _c(rc[b * NQ]["xo"][TL:])
    out = np.stack([np.concatenate(xs[b], 0) for b in range(B)], 0).astype(np.float32)
    return out
```

```python
import numpy as np
import ml_dtypes
import concourse.bass as bass
import concourse.mybir as mybir
from concourse.bass_utils import run_bass_kernel_spmd

F32 = mybir.dt.float32
BF16 = mybir.dt.bfloat16
AF = mybir.ActivationFunctionType
ALU = mybir.AluOpType
AX = mybir.AxisListType
NPBF = ml_dtypes.bfloat16

D = 1024
SEQ = 16384
CTX = 256
DEPTH = 4
HID = 2816
NCORES = 8
RMS_EPS = 1e-6


class Buf:
    __slots__ = ("w", "r", "name", "excl")

    def __init__(self, name="", excl=False):
        self.w = None
        self.r = {}
        self.name = name
        self.excl = excl


class FW:
    NDMA = 24

    def __init__(self, nc):
        self.nc = nc
        self.engs = {"pe": nc.tensor, "act": nc.scalar, "dve": nc.vector, "pool": nc.gpsimd, "sp": nc.sync}
        self.esem = {}
        self.ecnt = {}
        for e in ("pe", "act", "dve", "pool"):
            self.esem[e] = nc.alloc_semaphore("es_" + e)
            self.ecnt[e] = 0
        self.waited = {e: {} for e in self.engs}
        self.dsems = [nc.alloc_semaphore("ds%d" % i) for i in range(self.NDMA)]
        self.dcnt = [0] * self.NDMA
        self.dnext = 0
        self.semid = {}
        self.out_toks = []
        self.nwaits = 0
        self.prog = {e: [] for e in self.engs}

    def _sid(self, sem):
        return id(sem)

    def _wait(self, e, tok):
        sem, val = tok
        k = self._sid(sem)
        if self.waited[e].get(k, 0) >= val:
            return
        self.prog[e].append(("w", sem, val))
        if e == "pe" and sem is not self.esem["pe"]:
            self.prog[e].append(("w", sem, val))
        self.nwaits += 1
        self.waited[e][k] = val

    def _deps(self, e, reads, writes):
        own = self.esem.get(e)
        for t in reads:
            if t.w is not None:
                if not (e == "pe" and t.w[0] is own):
                    self._wait(e, t.w)
            if t.excl:
                for k, tok in t.r.items():
                    if tok[0] is not own:
                        self._wait(e, tok)
        for t in writes:
            if t.w is not None:
                if not (e == "pe" and t.w[0] is own):
                    self._wait(e, t.w)
            for k, tok in t.r.items():
                if e == "pe" and tok[0] is own:
                    continue
                self._wait(e, tok)

    def _post(self, tok, reads, writes):
        k = self._sid(tok[0])
        for t in reads:
            o = t.r.get(k)
            if o is None or o[1] < tok[1]:
                t.r[k] = tok
        for t in writes:
            t.w = tok
            t.r = {}

    def fence(self, e):
        if self.ecnt[e] > 0:
            self._wait(e, (self.esem[e], self.ecnt[e]))

    def op(self, e, reads, writes, meth, *args, **kw):
        self._deps(e, reads, writes)
        self.ecnt[e] += 1
        self.prog[e].append(("i", (lambda en, m=meth, a=args, k=kw: getattr(en, m)(*a, **k)), self.esem[e], 1))
        self._post((self.esem[e], self.ecnt[e]), reads, writes)

    def dma(self, e, reads, writes, out, in_, is_output=False):
        j = self.dnext
        self.dnext = (j + 1) % self.NDMA
        if self.dcnt[j] > 0:
            self._wait(e, (self.dsems[j], self.dcnt[j]))
        self._deps(e, reads, writes)
        self.prog[e].append(("i", (lambda en, o=out, i=in_: en.dma_start(out=o, in_=i)), self.dsems[j], 16))
        self.dcnt[j] += 16
        tok = (self.dsems[j], self.dcnt[j])
        self._post(tok, reads, writes)
        if is_output:
            self.out_toks.append(tok)

    def dma_fn(self, e, reads, writes, fn):
        j = self.dnext
        self.dnext = (j + 1) % self.NDMA
        if self.dcnt[j] > 0:
            self._wait(e, (self.dsems[j], self.dcnt[j]))
        self._deps(e, reads, writes)
        self.prog[e].append(("i", fn, self.dsems[j], 16))
        self.dcnt[j] += 16
        tok = (self.dsems[j], self.dcnt[j])
        self._post(tok, reads, writes)

    def raw(self, e, fn):
        self.prog[e].append(("r", fn))

    def cc(self, kind, reads, writes, ins, outs, replica_groups):
        e = "pool"
        j = self.dnext
        self.dnext = (j + 1) % self.NDMA
        if self.dcnt[j] > 0:
            self._wait(e, (self.dsems[j], self.dcnt[j]))
        self._deps(e, reads, writes)
        self.prog[e].append(("i", (lambda en, k=kind, i=ins, o=outs, rg=replica_groups: en.collective_compute(
            k, ALU.bypass, replica_groups=rg, ins=i, outs=o)), self.dsems[j], 16))
        self.dcnt[j] += 16
        tok = (self.dsems[j], self.dcnt[j])
        self._post(tok, reads, writes)

    def barrier(self):
        toks = [(self.dsems[j], self.dcnt[j]) for j in range(self.NDMA) if self.dcnt[j] > 0]
        toks += [(self.esem[e], self.ecnt[e]) for e in ("pe", "act", "dve", "pool") if self.ecnt[e] > 0]
        for e in self.engs:
            for tok in toks:
                if tok[0] is self.esem.get(e):
                    continue
                self._wait(e, tok)

    def finish(self):
        for tok in self.out_toks:
            self._wait("sp", tok)
        for j in range(self.NDMA):
            if self.dcnt[j] > 0:
                self._wait("sp", (self.dsems[j], self.dcnt[j]))
        for e in ("pe", "act", "dve", "pool"):
            if self.ecnt[e] > 0:
                self._wait("sp", (self.esem[e], self.ecnt[e]))
        prog = self.prog

        def replay(eng, items):
            for it in items:
                if it[0] == "w":
                    eng.wait_ge(it[1], it[2])
                elif it[0] == "r":
                    it[1](eng)
                else:
                    it[1](eng).then_inc(it[2], it[3])

        with self.nc.Block() as block:
            @block.sync
            def _(eng):
                replay(eng, prog["sp"])

            @block.tensor
            def _(eng):
                replay(eng, prog["pe"])

            @block.scalar
            def _(eng):
                replay(eng, prog["act"])

            @block.vector
            def _(eng):
                replay(eng, prog["dve"])

            @block.gpsimd
            def _(eng):
                replay(eng, prog["pool"])


def SBT(nc, name, shape, dt):
    return nc.alloc_sbuf_tensor(name, shape, dt)


def PST(nc, name, shape, dt):
    return nc.alloc_psum_tensor(name, shape, dt)


def load_w_bf16(fw, nc, w_ap, dst, bdst, K, N, stg, bstg, eng_cast="pool", col0=0, ncols=None, dcol0=0):
    if ncols is None:
        ncols = N
    SW = stg[0].shape[1]
    q = 0
    for k in range(K // 128):
        for c0 in range(0, ncols, SW):
            cw = min(SW, ncols - c0)
            s = q % len(stg)
            fw.dma("sp" if q % 2 == 0 else "act", [], [bstg[s]], stg[s][:, 0:cw],
                   w_ap[k * 128:(k + 1) * 128, col0 + c0:col0 + c0 + cw])
            fw.op(eng_cast, [bstg[s]], [bdst],
                  "tensor_copy", out=dst[:, k, dcol0 + c0:dcol0 + c0 + cw],
                                                                 in_=stg[s][:, 0:cw])
            q += 1


def emit_rstd(fw, ss, bss, rstd, brstd, n):
    fw.op("dve", [bss], [brstd], "tensor_scalar", out=rstd, in0=ss, scalar1=1.0 / n, scalar2=RMS_EPS,
                                                           op0=ALU.mult, op1=ALU.add)
    fw.op("act", [brstd], [brstd], "activation", out=rstd, in_=rstd, func=AF.Sqrt)
    fw.op("dve", [brstd], [brstd], "reciprocal", out=rstd, in_=rstd)


def make_ident(fw, nc, name="ident"):
    ident = SBT(nc, name, [128, 128], F32)
    b = Buf()
    fw.op("pool", [], [b], "memset", ident[:], 1.0)
    fw.op("pool", [b], [b], "affine_select", out=ident[:], in_=ident[:], pattern=[[-1, 128]],
                                                      compare_op=ALU.is_equal, fill=0.0, base=0,
                                                      channel_multiplier=1)
    return ident, b


def build_kc(NT_LAT, NT_CTX, final):
    NT = NT_LAT + NT_CTX
    TB = 256
    nc = bass.Bass("TRN2", target_bir_lowering=False)
    x = nc.dram_tensor("x", [NT, D], F32, kind="ExternalInput").ap()
    yT = nc.dram_tensor("yT", [D, NT], BF16, kind="ExternalInput").ap()
    w_out = nc.dram_tensor("w_out", [D, D], F32, kind="ExternalInput").ap()
    w_fi = nc.dram_tensor("w_fi", [D, 2 * HID], F32, kind="ExternalInput").ap()
    w_fo = nc.dram_tensor("w_fo", [HID, D], F32, kind="ExternalInput").ap()
    grep = nc.dram_tensor("grep", [2, 2, 128, D], F32, kind="ExternalInput").ap()
    fmv = nc.dram_tensor("fmv", [128, 40], F32, kind="ExternalInput").ap()
    norep = nc.dram_tensor("norep", [128, D], F32, kind="ExternalInput").ap()
    xo = nc.dram_tensor("xo", [NT, D], F32, kind="ExternalOutput").ap()
    fw = FW(nc)
    NJ = HID // 128
    wout = SBT(nc, "wout", [128, 8, D], BF16); bwout = Buf()
    wfi = SBT(nc, "wfi", [128, 8, 2 * HID], BF16); bwfi = Buf()
    wfo = SBT(nc, "wfo", [128, NJ, D], BF16); bwfo = Buf()
    stg = [SBT(nc, "stg%d" % i, [128, 1024], F32) for i in range(2)]; bstg = [Buf(), Buf()]
    g1 = SBT(nc, "g1", [128, D], F32); bg1 = Buf()
    g2 = SBT(nc, "g2", [128, D], F32); bg2 = Buf()
    fm = SBT(nc, "fm", [128, 40], F32); bfm = Buf()
    gs = SBT(nc, "gs", [128, 16], F32); bgs = Buf()
    yTs = SBT(nc, "yTs", [128, 8, TB], BF16); byT = Buf()
    xin = SBT(nc, "xin", [128, D], F32); bxin = Buf()
    x1 = [SBT(nc, "x1_%d" % i, [128, D], F32) for i in range(TB // 128)]; bx1 = [Buf() for _ in x1]
    xn = SBT(nc, "xn", [128, D], F32); bxn = Buf()
    hT = SBT(nc, "hT", [128, 8, TB], BF16); bhT = Buf()
    aT = SBT(nc, "aT", [128, NJ, TB], BF16); baT = Buf()
    gsb = [SBT(nc, "gsb%d" % i, [128, TB], F32) for i in range(2)]; bgsb = [Buf(), Buf()]
    ss = SBT(nc, "ss", [128, 1], F32); bss = Buf()
    rstd = SBT(nc, "rstd", [128, 1], F32); brstd = Buf()
    pmm = [PST(nc, "pmm%d" % i, [128, D], F32) for i in range(2)]; bpmm = [Buf(), Buf()]
    ptr = PST(nc, "ptr", [128, 8, 128], F32); bptr = Buf()
    pgu = [PST(nc, "pgu%d" % i, [128, 2, TB], F32) for i in range(2)]; bpgu = [Buf() for _ in range(2)]
    ident, bid = make_ident(fw, nc)

    fw.dma("sp", [], [bfm], fm[:], fmv[:, :])
    load_w_bf16(fw, nc, w_out, wout, bwout, D, D, stg, bstg)
    load_w_bf16(fw, nc, w_fi, wfi, bwfi, D, 2 * HID, stg, bstg)
    load_w_bf16(fw, nc, w_fo, wfo, bwfo, HID, D, stg, bstg)
    if final:
        nrep = SBT(nc, "nrep", [128, D], F32); bnrep = Buf()
        fw.dma("sp", [], [bnrep], nrep[:], norep[:, :])

    blocks = [(t0, min(TB, NT_LAT - t0), 0) for t0 in range(0, NT_LAT, TB)]
    blocks += [(NT_LAT + t0, min(TB, NT_CTX - t0), 1) for t0 in range(0, NT_CTX, TB)]
    cur_g = -1
    pi = 0
    for (t0, tb, g) in blocks:
        if g != cur_g:
            cur_g = g
            fw.dma("sp", [], [bg1], g1[:], grep[g, 0])
            fw.dma("sp", [], [bg2], g2[:], grep[g, 1])
            sc = fm[:, 8 + 16 * g:16 + 16 * g]
            fw.op("dve", [bfm], [bgs], "scalar_tensor_tensor", out=gs[:, 0:8], in0=sc, scalar=1.0, in1=fm[:, 0:8], op0=ALU.add, op1=ALU.mult)
        shc = 16 + 16 * g
        nt = tb // 128
        fw.dma("sp", [], [byT], yTs[:, :, 0:tb], yT.rearrange("(c p) t -> p c t", p=128)[:, :, t0:t0 + tb])
        for i in range(nt):
            fw.dma("act", [], [bxin], xin[:], x[t0 + i * 128:t0 + (i + 1) * 128, :])
            po = pmm[pi % 2]; bpo = bpmm[pi % 2]; pi += 1
            for cb in range(2):
                for k in range(8):
                    fw.op("pe", [byT, bwout], [bpo], "matmul", po[:, cb * 512:(cb + 1) * 512], lhsT=yTs[:, k, i * 128:(i + 1) * 128],
                        rhs=wout[:, k, cb * 512:(cb + 1) * 512], start=(k == 0), stop=(k == 7))
            fw.op("dve", [bpo, bg1], [bx1[i]], "tensor_tensor", out=x1[i][:], in0=po[:], in1=g1[:], op=ALU.mult)
            fw.op("dve", [bx1[i], bxin], [bx1[i]], "tensor_tensor", out=x1[i][:], in0=x1[i][:], in1=xin[:], op=ALU.add)
            fw.op("act", [bx1[i]], [bxn, bss], "activation", out=xn[:], in_=x1[i][:], func=AF.Square, accum_out=ss[:])
            emit_rstd(fw, ss[:], bss, rstd[:], brstd, D)
            fw.op("act", [bx1[i], brstd], [bxn], "activation", out=xn[:], in_=x1[i][:], func=AF.Copy, scale=rstd[:, 0:1])
            for c in range(8):
                fw.op("pe", [bxn, bid], [bptr], "transpose", out=ptr[:, c, :], in_=xn[:, c * 128:(c + 1) * 128], identity=ident[:])
            for c in range(8):
                if c % 2 == 0:
                    fw.op("dve", [bptr, bgs, bfm], [bhT], "tensor_scalar", out=hT[:, c, i * 128:(i + 1) * 128], in0=ptr[:, c, :], scalar1=gs[:, c:c + 1],
                        scalar2=fm[:, shc + c:shc + c + 1], op0=ALU.mult, op1=ALU.add)
                else:
                    fw.op("act", [bptr, bgs, bfm], [bhT], "activation", out=hT[:, c, i * 128:(i + 1) * 128], in_=ptr[:, c, :], func=AF.Identity,
                        scale=gs[:, c:c + 1], bias=fm[:, shc + c:shc + c + 1])
        for j in range(NJ):
            pg = pgu[j % 2][:, 0, :]; bpg = bpgu[j % 2]
            pu = pgu[j % 2][:, 1, :]; bpu = bpgu[j % 2]
            for k in range(8):
                fw.op("pe", [bhT, bwfi], [bpg], "matmul", pg[:, 0:tb], lhsT=wfi[:, k, j * 128:(j + 1) * 128], rhs=hT[:, k, 0:tb],
                    start=(k == 0), stop=(k == 7))
            for k in range(8):
                fw.op("pe", [bhT, bwfi], [bpu], "matmul", pu[:, 0:tb], lhsT=wfi[:, k, HID + j * 128:HID + (j + 1) * 128], rhs=hT[:, k, 0:tb],
                    start=(k == 0), stop=(k == 7))
            gb = gsb[j % 2]; bgb = bgsb[j % 2]
            fw.op("act", [bpg], [bgb], "activation", out=gb[:, 0:tb], in_=pg[:, 0:tb], func=AF.Silu)
            fw.op("dve", [bgb, bpu], [baT], "tensor_tensor", out=aT[:, j, 0:tb], in0=pu[:, 0:tb], in1=gb[:, 0:tb], op=ALU.mult)
        for i in range(nt):
            pf = pmm[pi % 2]; bpf = bpmm[pi % 2]; pi += 1
            for cb in range(2):
                for j in range(NJ):
                    fw.op("pe", [baT, bwfo], [bpf], "matmul", pf[:, cb * 512:(cb + 1) * 512], lhsT=aT[:, j, i * 128:(i + 1) * 128],
                        rhs=wfo[:, j, cb * 512:(cb + 1) * 512], start=(j == 0), stop=(j == NJ - 1))
            fw.op("dve", [bpf, bg2], [bxn], "tensor_tensor", out=xn[:], in0=pf[:], in1=g2[:], op=ALU.mult)
            fw.op("dve", [bxn, bx1[i]], [bxn], "tensor_tensor", out=xn[:], in0=xn[:], in1=x1[i][:], op=ALU.add)
            if final:
                fw.op("act", [bxn], [bx1[i], bss], "activation", out=x1[i][:], in_=xn[:], func=AF.Square, accum_out=ss[:])
                emit_rstd(fw, ss[:], bss, rstd[:], brstd, D)
                fw.op("dve", [bxn, brstd, bnrep], [bxn], "scalar_tensor_tensor", out=xn[:], in0=xn[:], scalar=rstd[:, 0:1], in1=nrep[:], op0=ALU.mult, op1=ALU.mult)
            fw.dma("sp", [bxn], [], xo[t0 + i * 128:t0 + (i + 1) * 128, :], xn[:], is_output=True)
    fw.finish()
    return nc


EV_FM = [(0, "a"), (128, "a"), (256, "a"), (384, "a"), (512, "ak"), (768, "b"), (896, "b"), (1024, "b"),
         (1152, "b"), (1280, "b")]
EV_TM = [(640, 128), (1408, 128)]
OD_FM_BF = [(c * 128, "c") for c in range(8)]
OD_FM_F32 = [(1536 + c * 128, "z") for c in range(14)]
OD_TM = [(1024, 512)]


def build_ka(NT_LAT, NT_CTX, even):
    NT = NT_LAT + NT_CTX
    TB = 512
    NIN = 1536 if even else 3328
    nc = bass.Bass("TRN2", target_bir_lowering=False)
    x = nc.dram_tensor("x", [NT, D], F32, kind="ExternalInput").ap()
    w_in = nc.dram_tensor("w_in", [D, NIN], F32, kind="ExternalInput").ap()
    fmv = nc.dram_tensor("fmv", [128, 40], F32, kind="ExternalInput").ap()
    fw = FW(nc)
    if even:
        fm_list = EV_FM
        tm_list = EV_TM
        gv_d = nc.dram_tensor("gv", [128, 2], F32, kind="ExternalInput").ap()
        rm_d = nc.dram_tensor("rm", [128, 128], F32, kind="ExternalInput").ap()
        ob_d = nc.dram_tensor("ob", [128, 128], F32, kind="ExternalInput").ap()
        cs_d = nc.dram_tensor("cs", [2, 128, NT], F32, kind="ExternalInput").ap()
        qk_o = nc.dram_tensor("qk_o", [len(fm_list), 128, NT], BF16, kind="ExternalOutput").ap()
        NTM = 256
    else:
        fm_list = OD_FM_BF + OD_FM_F32
        tm_list = OD_TM
        qk_o = nc.dram_tensor("qk_o", [8, 128, NT], BF16, kind="ExternalOutput").ap()
        zd_o = nc.dram_tensor("zd_o", [14, 128, NT], F32, kind="ExternalOutput").ap()
        NTM = 512
    v_o = nc.dram_tensor("v_o", [NT, NTM], BF16, kind="ExternalOutput").ap()

    win = SBT(nc, "win", [128, 8, NIN], BF16); bwin = Buf()
    stg = [SBT(nc, "stg%d" % i, [128, 1024], F32) for i in range(2)]; bstg = [Buf(), Buf()]
    fm = SBT(nc, "fm", [128, 40], F32); bfm = Buf()
    gs = SBT(nc, "gs", [128, 8], F32); bgs = Buf()
    xin = [SBT(nc, "xin%d" % i, [128, D], F32) for i in range(2)]; bxin = [Buf(), Buf()]
    xn = SBT(nc, "xn", [128, D], F32); bxn = Buf()
    hT = SBT(nc, "hT", [128, 8, TB], BF16); bhT = Buf()
    ss = SBT(nc, "ss", [128, 1], F32); bss = Buf()
    rstd = SBT(nc, "rstd", [128, 1], F32); brstd = Buf()
    ptr = PST(nc, "ptr", [128, 8, 128], F32); bptr = Buf()
    pfm = [PST(nc, "pfm%d" % i, [128, TB], F32) for i in range(2)]; bpfm = [Buf(), Buf()]
    paux = [PST(nc, "paux%d" % i, [128, TB], F32) for i in range(2)]; bpaux = [Buf(), Buf()]
    ptm = [PST(nc, "ptm%d" % i, [128, 512], F32) for i in range(2)]; bptm = [Buf(), Buf()]
    ofm = [SBT(nc, "ofm%d" % i, [128, TB], F32) for i in range(3)]; bofm = [Buf() for _ in range(3)]
    ofb = [SBT(nc, "ofb%d" % i, [128, TB], BF16) for i in range(3)]; bofb = [Buf() for _ in range(3)]
    otm = [SBT(nc, "otm%d" % i, [128, NTM], BF16) for i in range(2)]; botm = [Buf(), Buf()]
    ident, bid = make_ident(fw, nc)
    fw.dma("sp", [], [bfm], fm[:], fmv[:, :])
    if even:
        gv = SBT(nc, "gv_s", [128, 2], F32); bgv = Buf()
        rm = SBT(nc, "rm_s", [128, 128], F32); brm = Buf()
        ob = SBT(nc, "ob_s", [128, 128], F32); bob = Buf()
        cst = SBT(nc, "cst", [128, 2, TB], F32); bcst = Buf()
        sq = SBT(nc, "sq", [128, TB], F32); bsq = Buf()
        rs = SBT(nc, "rs", [128, TB], F32); brs = Buf()
        t1 = SBT(nc, "t1", [128, TB], F32); bt1 = Buf()
        fw.dma("sp", [], [bgv], gv[:], gv_d[:, :])
        fw.dma("sp", [], [brm], rm[:], rm_d[:, :])
        fw.dma("sp", [], [bob], ob[:], ob_d[:, :])
    load_w_bf16(fw, nc, w_in, win, bwin, D, NIN, stg, bstg)
    blocks = [(t0, min(TB, NT_LAT - t0), 0) for t0 in range(0, NT_LAT, TB)]
    blocks += [(NT_LAT + t0, min(TB, NT_CTX - t0), 1) for t0 in range(0, NT_CTX, TB)]
    cur_g = -1
    xi = 0
    fi = 0
    oi = 0
    ti = 0
    for (t0, tb, g) in blocks:
        if g != cur_g:
            cur_g = g
            fw.op("dve", [bfm], [bgs], "scalar_tensor_tensor", out=gs[:, 0:8], in0=fm[:, 8 + 16 * g:16 + 16 * g],
                  scalar=1.0, in1=fm[:, 0:8], op0=ALU.add, op1=ALU.mult)
        shc = 16 + 16 * g
        nt = tb // 128
        if even:
            fw.dma("act", [], [bcst], cst[:, :, 0:tb], cs_d.rearrange("a p t -> p a t")[:, :, t0:t0 + tb])
        for i in range(nt):
            xt = xin[xi % 2]; bxt = bxin[xi % 2]; xi += 1
            fw.dma("sp", [], [bxt], xt[:], x[t0 + i * 128:t0 + (i + 1) * 128, :])
            fw.op("act", [bxt], [bxn, bss], "activation", out=xn[:], in_=xt[:], func=AF.Square, accum_out=ss[:])
            emit_rstd(fw, ss[:], bss, rstd[:], brstd, D)
            fw.op("act", [bxt, brstd], [bxn], "activation", out=xn[:], in_=xt[:], func=AF.Copy, scale=rstd[:, 0:1])
            for c in range(8):
                fw.op("pe", [bxn, bid], [bptr], "transpose", out=ptr[:, c, :], in_=xn[:, c * 128:(c + 1) * 128],
                      identity=ident[:])
            for c in range(8):
                if c % 2 == 0:
                    fw.op("dve", [bptr, bgs, bfm], [bhT], "tensor_scalar", out=hT[:, c, i * 128:(i + 1) * 128],
                          in0=ptr[:, c, :], scalar1=gs[:, c:c + 1], scalar2=fm[:, shc + c:shc + c + 1],
                          op0=ALU.mult, op1=ALU.add)
                else:
                    fw.op("act", [bptr, bgs, bfm], [bhT], "activation", out=hT[:, c, i * 128:(i + 1) * 128],
                          in_=ptr[:, c, :], func=AF.Identity, scale=gs[:, c:c + 1],
                          bias=fm[:, shc + c:shc + c + 1])
        for ci, (col, kind) in enumerate(fm_list):
            pf = pfm[fi % 2]; bpf = bpfm[fi % 2]; fi += 1
            for k in range(8):
                fw.op("pe", [bhT, bwin], [bpf], "matmul", pf[:, 0:tb], lhsT=win[:, k, col:col + 128],
                      rhs=hT[:, k, 0:tb], start=(k == 0), stop=(k == 7))
            if kind == "c":
                o = ofb[oi % 3]; bo = bofb[oi % 3]; oi += 1
                if ci % 2 == 0:
                    fw.op("act", [bpf], [bo], "activation", out=o[:, 0:tb], in_=pf[:, 0:tb], func=AF.Copy)
                else:
                    fw.op("dve", [bpf], [bo], "tensor_copy", out=o[:, 0:tb], in_=pf[:, 0:tb])
                fw.dma("sp", [bo], [], qk_o[ci, :, t0:t0 + tb], o[:, 0:tb], is_output=True)
            elif kind == "z":
                o = ofm[oi % 3]; bo = bofm[oi % 3]; oi += 1
                if ci % 2 == 0:
                    fw.op("act", [bpf], [bo], "activation", out=o[:, 0:tb], in_=pf[:, 0:tb], func=AF.Copy)
                else:
                    fw.op("dve", [bpf], [bo], "tensor_copy", out=o[:, 0:tb], in_=pf[:, 0:tb])
                fw.dma("sp", [bo], [], zd_o[ci - 8, :, t0:t0 + tb], o[:, 0:tb], is_output=True)
            else:
                pa = paux[fi % 2]; bpa = bpaux[fi % 2]
                if kind in ("a", "ak"):
                    gcol = 0 if kind == "a" else 1
                    fw.op("act", [bpf], [bsq], "activation", out=sq[:, 0:tb], in_=pf[:, 0:tb], func=AF.Square)
                    fw.op("pe", [bsq, bob], [bpa], "matmul", pa[:, 0:tb], lhsT=ob[:], rhs=sq[:, 0:tb],
                          start=True, stop=True)
                    fw.op("dve", [bpa], [brs], "tensor_scalar", out=rs[:, 0:tb], in0=pa[:, 0:tb], scalar1=RMS_EPS,
                          scalar2=None, op0=ALU.add)
                    fw.op("act", [brs], [brs], "activation", out=rs[:, 0:tb], in_=rs[:, 0:tb], func=AF.Sqrt)
                    fw.op("dve", [brs], [brs], "reciprocal", out=rs[:, 0:tb], in_=rs[:, 0:tb])
                    fw.op("dve", [bpf, brs, bgv], [bt1], "scalar_tensor_tensor", out=t1[:, 0:tb], in0=pf[:, 0:tb],
                          scalar=gv[:, gcol:gcol + 1], in1=rs[:, 0:tb], op0=ALU.mult, op1=ALU.mult)
                else:
                    fw.op("act", [bpf], [bt1], "activation", out=t1[:, 0:tb], in_=pf[:, 0:tb], func=AF.Copy)
                pr = paux[(fi + 1) % 2]; bpr = bpaux[(fi + 1) % 2]
                fw.op("pe", [bt1, brm], [bpr], "matmul", pr[:, 0:tb], lhsT=rm[:], rhs=t1[:, 0:tb],
                      start=True, stop=True)
                fw.op("dve", [bpr, bcst], [bsq], "tensor_tensor", out=sq[:, 0:tb], in0=pr[:, 0:tb],
                      in1=cst[:, 1, 0:tb], op=ALU.mult)
                fw.op("pool", [bt1, bcst], [bt1], "tensor_tensor", out=t1[:, 0:tb], in0=t1[:, 0:tb],
                      in1=cst[:, 0, 0:tb], op=ALU.mult)
                o = ofb[oi % 3]; bo = bofb[oi % 3]; oi += 1
                fw.op("dve", [bt1, bsq], [bo], "tensor_tensor", out=o[:, 0:tb], in0=t1[:, 0:tb], in1=sq[:, 0:tb],
                      op=ALU.add)
                fw.dma("sp", [bo], [], qk_o[ci, :, t0:t0 + tb], o[:, 0:tb], is_output=True)
        for i in range(nt):
            pt = ptm[ti % 2]; bpt = bptm[ti % 2]
            o = otm[ti % 2]; bo = botm[ti % 2]; ti += 1
            oc = 0
            for (col, wd) in tm_list:
                for k in range(8):
                    fw.op("pe", [bhT, bwin], [bpt], "matmul", pt[:, oc:oc + wd], lhsT=hT[:, k, i * 128:(i + 1) * 128],
                          rhs=win[:, k, col:col + wd], start=(k == 0 and oc == 0), stop=(k == 7),
                          skip_group_check=True)
                oc += wd
            fw.op("act", [bpt], [bo], "activation", out=o[:], in_=pt[:, 0:NTM], func=AF.Copy)
            fw.dma("pool", [bo], [], v_o[t0 + i * 128:t0 + (i + 1) * 128, :], o[:], is_output=True)
    fw.finish()
    return nc


def rope_tables(pos_tok, is_ctx):
    t = pos_tok.astype(np.int64)
    row = (t // 64).astype(np.float32)
    colp = (t % 64).astype(np.float32)
    inv = (10000.0 ** (-np.arange(16, dtype=np.float32) / 16)).astype(np.float32)
    ang = np.concatenate([row[:, None] * inv, colp[:, None] * inv], axis=-1)
    cos = np.cos(ang).astype(np.float32)
    sin = np.sin(ang).astype(np.float32)
    cos[is_ctx] = 1.0
    sin[is_ctx] = 0.0
    cosf = np.concatenate([cos, cos, cos, cos], axis=1).T
    sinf = np.concatenate([sin, sin, sin, sin], axis=1).T
    return np.ascontiguousarray(np.stack([cosf, sinf]).astype(np.float32))


def rope_consts():
    rm = np.zeros((128, 128), np.float32)
    for hb in (0, 64):
        for m in range(32):
            rm[hb + m + 32, hb + m] = -1.0
            rm[hb + m, hb + m + 32] = 1.0
    ob = np.zeros((128, 128), np.float32)
    ob[0:64, 0:64] = 1.0 / 64
    ob[64:128, 64:128] = 1.0 / 64
    return rm, ob


SCALE = 64 ** -0.5


def band_masks():
    m = np.zeros((6, 128, 512), np.float32)
    p = np.arange(128)[:, None]
    f = np.arange(512)[None, :]
    for r in range(6):
        m[r] = (np.abs((r - 1) * 128 + p - f) <= 128)
    return m.astype(NPBF)


def build_kbe(NLAT, NCTX):
    NTOT = NLAT + NCTX
    NKB = NTOT // 128
    NKL = NLAT // 128
    QB = 512
    PC = 2048
    nc = bass.Bass("TRN2", target_bir_lowering=False)
    q_d = [nc.dram_tensor("q%d" % m, [128, NTOT], BF16, kind="ExternalInput").ap() for m in range(2)]
    k_d = [nc.dram_tensor("k%d" % m, [128, NTOT], BF16, kind="ExternalInput").ap() for m in range(2)]
    v_d = [nc.dram_tensor("v%d" % m, [128, NKB * 65], BF16, kind="ExternalInput").ap() for m in range(2)]
    mask_d = nc.dram_tensor("mask", [6, 128, 512], BF16, kind="ExternalInput").ap()
    sink_d = nc.dram_tensor("sink", [64, 2], F32, kind="ExternalInput").ap()
    sel_d = nc.dram_tensor("sel", [65, 64], F32, kind="ExternalInput").ap()
    y_o = nc.dram_tensor("y_o", [4, 64, NTOT], BF16, kind="ExternalOutput").ap()
    fw = FW(nc)
    npc = max(1, NLAT // PC)
    pcs = [(i * PC, PC if i < npc - 1 else NTOT - i * PC) for i in range(npc)]
    Q = []; K = []; V = []; bQ = []; bK = []; bV = []
    for m in range(2):
        Q.append(SBT(nc, "Q%d" % m, [128, NTOT], BF16)); bQ.append([Buf() for _ in pcs])
        K.append(SBT(nc, "K%d" % m, [128, NTOT], BF16)); bK.append([Buf() for _ in pcs])
        V.append(SBT(nc, "V%d" % m, [128, NKB, 65], BF16)); bV.append([Buf() for _ in pcs])
    msk = SBT(nc, "msk", [128, 6, 512], BF16); bmsk = Buf()
    snk = SBT(nc, "snk", [64, 2], F32); bsnk = Buf()
    sel = SBT(nc, "sel_s", [65, 64], F32); bsel = Buf()
    fw.dma("sp", [], [bmsk], msk[:], mask_d.rearrange("r p f -> p r f"))
    fw.dma("sp", [], [bsnk], snk[:], sink_d[:, :])
    fw.dma("sp", [], [bsel], sel[:], sel_d[:, :])
    fw.op("act", [bsnk], [bsnk], "activation", out=snk[:], in_=snk[:], func=AF.Exp)
    for m in range(2):
        for pi, (c0, cw) in enumerate(pcs):
            fw.dma("sp", [], [bK[m][pi]], K[m][:, c0:c0 + cw], k_d[m][:, c0:c0 + cw])
            fw.dma("act", [], [bQ[m][pi]], Q[m][:, c0:c0 + cw], q_d[m][:, c0:c0 + cw])
            kb0 = c0 // 128; kbn = cw // 128
            fw.dma("pool", [], [bV[m][pi]], V[m][:, kb0:kb0 + kbn, :],
                   v_d[m][:, kb0 * 65:(kb0 + kbn) * 65].rearrange("p (k d) -> p k d", d=65))

    def pc_of(col):
        return min(col // PC, npc - 1)

    NPB = 4
    pS = [PST(nc, "pS%d" % i, [128, QB], F32) for i in range(NPB)]; bpS = [Buf(excl=True) for _ in range(NPB)]
    pO = [PST(nc, "pO%d" % i, [128, QB], F32) for i in range(2)]; bpO = [Buf(excl=True), Buf(excl=True)]
    pB = PST(nc, "pB", [128, QB], F32); bpB = Buf(excl=True)
    PT = [SBT(nc, "PT%d" % i, [128, QB], BF16) for i in range(NPB)]; bPT = [Buf() for _ in range(NPB)]
    PM = [SBT(nc, "PM%d" % i, [128, QB], BF16) for i in range(4)]; bPM = [Buf() for _ in range(4)]
    osb = [SBT(nc, "osb%d" % i, [65, QB], F32) for i in range(2)]; bosb = [Buf(), Buf()]
    rden = SBT(nc, "rden", [64, QB], F32); brden = Buf()
    yo = [SBT(nc, "yo%d" % i, [64, QB], BF16) for i in range(2)]; byo = [Buf(), Buf()]
    cnt = {"s": 0, "o": 0, "m": 0}

    def attn2(m, q0, qn, kbl, use_sink):
        bq = bQ[m][pc_of(q0)]
        for n, (kb, mi) in enumerate(kbl):
            srcs = []
            pss = []
            for hh in range(2):
                hs = slice(hh * 64, (hh + 1) * 64)
                si = cnt["s"]; cnt["s"] += 1
                ps = pS[si % NPB]; bps = bpS[si % NPB]
                fw.op("pe", [bK[m][pc_of(kb * 128)], bq], [bps], "matmul", ps[:, 0:qn],
                      lhsT=K[m][hs, kb * 128:(kb + 1) * 128], rhs=Q[m][hs, q0:q0 + qn], start=True, stop=True)
                pss.append((ps, bps, PT[si % NPB], bPT[si % NPB]))
            for hh in range(2):
                ps, bps, pt, bpt = pss[hh]
                fw.op("act", [bps], [bpt], "activation", out=pt[:, 0:qn], in_=ps[:, 0:qn], func=AF.Exp, scale=SCALE)
                src = pt; bsrc = bpt
                if mi is not None:
                    mm = cnt["m"]; cnt["m"] += 1
                    pm = PM[mm % 4]; bpm = bPM[mm % 4]
                    fw.op("dve", [bpt, bmsk], [bpm], "tensor_tensor", out=pm[:, 0:qn], in0=pt[:, 0:qn],
                          in1=msk[:, mi, 0:qn], op=ALU.mult)
                    src = pm; bsrc = bpm
                srcs.append((src, bsrc))
            for hh in range(2):
                src, bsrc = srcs[hh]
                po = pO[hh]; bpo = bpO[hh]
                fw.op("pe", [bsrc, bV[m][pc_of(kb * 128)]], [bpo], "matmul", po[0:65, 0:qn], lhsT=V[m][:, kb, 0:65],
                      rhs=src[:, 0:qn], start=(n == 0), stop=(n == len(kbl) - 1))
        for hh in range(2):
            po = pO[hh]; bpo = bpO[hh]
            oi = cnt["o"]; cnt["o"] += 1
            ob = osb[oi % 2]; bob = bosb[oi % 2]
            fw.op("act", [bpo], [bob], "activation", out=ob[:, 0:qn], in_=po[0:65, 0:qn], func=AF.Copy)
            fw.op("pe", [bob, bsel], [bpB], "matmul", pB[0:64, 0:qn], lhsT=sel[:], rhs=ob[:, 0:qn], start=True, stop=True)
            if use_sink:
                fw.op("dve", [bpB, bsnk], [brden], "tensor_scalar", out=rden[:, 0:qn], in0=pB[0:64, 0:qn],
                      scalar1=snk[:, hh:hh + 1], scalar2=None, op0=ALU.add)
                fw.op("dve", [brden], [brden], "reciprocal", out=rden[:, 0:qn], in_=rden[:, 0:qn])
            else:
                fw.op("dve", [bpB], [brden], "reciprocal", out=rden[:, 0:qn], in_=pB[0:64, 0:qn])
            y = yo[oi % 2]; by = byo[oi % 2]
            fw.op("dve", [bob, brden], [by], "tensor_tensor", out=y[:, 0:qn], in0=ob[0:64, 0:qn], in1=rden[:, 0:qn],
                  op=ALU.mult)
            fw.dma("sp", [by], [], y_o[m * 2 + hh, :, q0:q0 + qn], y[:, 0:qn], is_output=True)

    ctx_kb = [(NKL + i, None) for i in range(NCTX // 128)]
    for qb in range(NLAT // QB):
        kbl = []
        for r in range(6):
            kb = 4 * qb + r - 1
            if 0 <= kb < NKL:
                kbl.append((kb, r))
        attn2(1, qb * QB, QB, kbl + ctx_kb, True)
    attn2(1, NLAT, NCTX, ctx_kb, True)
    for qb in range(NLAT // QB):
        attn2(0, qb * QB, QB, [(kb, None) for kb in range(NKL)] + ctx_kb, False)
    attn2(0, NLAT, NCTX, ctx_kb, False)
    fw.finish()
    return nc


def vaug_layout(v):
    ntot = v.shape[0]
    nkb = ntot // 128
    o = np.ones((128, nkb, 65), v.dtype)
    o[:, :, 0:64] = v.reshape(nkb, 128, 64).transpose(1, 0, 2)
    return np.ascontiguousarray(o.reshape(128, nkb * 65))


def sel_const():
    s = np.zeros((65, 64), np.float32)
    s[64, :] = 1.0
    return s


GN_EPS = 64e-5
LAM = -float(np.exp(-0.5))
CH = 64


def rwkv_consts():
    p = np.arange(128)[:, None] % 64
    f = np.arange(64)[None, :]
    su = (f > p).astype(np.float32)
    sl = (f < p).astype(np.float32)
    iu = (f >= p).astype(np.float32)
    il = (f <= p).astype(np.float32)
    mk = np.zeros((2, 128, 5, 64), np.float32)
    mk[0] = np.stack([su, iu, su, iu, sl], 1)
    mk[1] = np.stack([sl, il, sl, il, su], 1)
    ob1 = np.zeros((128, 128), np.float32)
    ob1[:64, :64] = 1.0
    ob1[64:, 64:] = 1.0
    sm = np.ones((128, 512), np.float32)
    sm[:, ::64] = 0.0
    id2 = (p == f).astype(np.float32)
    return mk, ob1, sm, id2


def build_kbr(NLAT, NCTX, dbg=3, dumps=None):
    NTOT = NLAT + NCTX
    TBK = 512
    nc = bass.Bass("TRN2", target_bir_lowering=False)
    z_d = nc.dram_tensor("z", [5, 128, NTOT], F32, kind="ExternalInput").ap()
    pv_d = nc.dram_tensor("pv", [128, 14], F32, kind="ExternalInput").ap()
    w2_d = nc.dram_tensor("w2", [2, 64, 128], F32, kind="ExternalInput").ap()
    a2_d = nc.dram_tensor("a2", [2, 64, 128], F32, kind="ExternalInput").ap()
    g2_d = nc.dram_tensor("g2", [128, 128], F32, kind="ExternalInput").ap()
    mk_d = nc.dram_tensor("mk", [2, 128, 5, 64], F32, kind="ExternalInput").ap()
    ob1_d = nc.dram_tensor("ob1", [128, 128], F32, kind="ExternalInput").ap()
    sm_d = nc.dram_tensor("sm", [128, 512], F32, kind="ExternalInput").ap()
    id2_d = nc.dram_tensor("id2", [128, 64], F32, kind="ExternalInput").ap()
    yd_o = nc.dram_tensor("yd_o", [128, NTOT], BF16, kind="ExternalOutput").ap()
    yf_o = nc.dram_tensor("yf_o", [128, NTOT], F32, kind="ExternalOutput").ap()
    fw = FW(nc)

    def sb(name, shape, dt=F32):
        return SBT(nc, name, shape, dt), Buf(name)

    pv, bpv = sb("pv_s", [128, 14])
    pd, bpd = sb("pd_s", [128, 16])
    w2s, bw2 = sb("w2_s", [64, 2, 128])
    a2s, ba2 = sb("a2_s", [128, 2, 128])
    g2s, bg2 = sb("g2_s", [128, 128])
    mk, bmk = sb("mk_s", [128, 2, 5, 64])
    ob1, bob1 = sb("ob1_s", [128, 128])
    ob64, bob64 = sb("ob64_s", [128, 128])
    sm, bsm = sb("sm_s", [128, 512])
    id2, bid2 = sb("id2_s", [128, 64])
    ident, bid = make_ident(fw, nc)
    fw.dma("sp", [], [bpv], pv[:], pv_d[:, :])
    fw.dma("sp", [], [bw2], w2s[:], w2_d.rearrange("d k n -> k d n"))
    fw.dma("sp", [], [ba2], a2s[64:128, :, :], a2_d.rearrange("d k n -> k d n"))
    fw.dma("sp", [], [bg2], g2s[:], g2_d[:, :])
    fw.dma("act", [], [bmk], mk[:], mk_d.rearrange("d p a f -> p d a f"))
    fw.dma("act", [], [bob1], ob1[:], ob1_d[:, :])
    fw.dma("act", [], [bsm], sm[:], sm_d[:, :])
    fw.dma("act", [], [bid2], id2[:], id2_d[:, :])
    fw.op("dve", [bob1], [bob64], "tensor_scalar", out=ob64[:], in0=ob1[:], scalar1=1.0 / 64, scalar2=None, op0=ALU.mult)
    fw.op("dve", [bpv], [bpd], "tensor_scalar", out=pd[:, 0:5], in0=pv[:, 0:5], scalar1=-1.0, scalar2=1.0,
          op0=ALU.mult, op1=ALU.add)
    fw.op("dve", [bpv], [bpd], "tensor_scalar", out=pd[:, 5:10], in0=pv[:, 0:5], scalar1=0.5, scalar2=None, op0=ALU.mult)
    fw.op("dve", [bpv], [bpd], "tensor_scalar", out=pd[:, 10:11], in0=pv[:, 10:11], scalar1=-1.0, scalar2=1.0,
          op0=ALU.mult, op1=ALU.add)
    fw.op("dve", [bpv], [bpd], "tensor_scalar", out=pd[:, 11:12], in0=pv[:, 10:11], scalar1=-2.0, scalar2=2.0,
          op0=ALU.mult, op1=ALU.add)

    NZ = 2
    zt = [[sb("zt%d_%d" % (i, q), [128, TBK + 2]) for q in range(5)] for i in range(NZ)]
    zs = [sb("zs%d" % q, [128, TBK]) for q in range(5)]
    names = ["tmp", "aa", "tws", "sg", "ad", "af", "kk", "kk2", "rin", "kkn", "tt", "khat", "bt", "c1", "c0", "cr", "c2",
             "ei", "ee", "en", "eh", "Bt", "Kt", "Bh", "Kh", "yb", "yfb", "yy", "yc", "yc2", "rs2", "sgd", "ks", "pr"]
    T = {n: sb("r_" + n, [128, TBK]) for n in names}
    AR, bAR = sb("AR", [128, TBK // CH, 2, CH])
    gC, bgC = sb("gC", [128, TBK // CH])
    ob_out, bob_out = sb("ob_out", [128, TBK], BF16)
    Msb = [sb("Msb%d" % i, [128, 5, CH]) for i in range(2)]
    PP = [sb("PP%d" % i, [128, 2, CH]) for i in range(2)]
    XX = [sb("XX%d" % i, [128, CH]) for i in range(2)]
    TM = [sb("TM%d" % i, [128, 3, CH]) for i in range(2)]
    W1, bW1 = sb("W1", [128, CH])
    U, bU = sb("U", [128, CH])
    ST = [sb("ST%d" % i, [128, CH]) for i in range(2)]
    pL = [(PST(nc, "pL%d" % i, [128, TBK], F32), Buf(excl=True)) for i in range(2)]
    pA = (PST(nc, "pA", [128, 512], F32), Buf(excl=True))
    pI = (PST(nc, "pI", [128, 512], F32), Buf(excl=True))
    pTt = (PST(nc, "pTt", [128, 512], F32), Buf(excl=True))
    pSt = [(PST(nc, "pSt%d" % i, [128, 512], F32), Buf(excl=True)) for i in range(2)]
    cnt = {"z": 0, "l": 0, "m": 0, "x": 0, "t": 0, "s": 0, "st": 0}

    def lora_ps():
        p = pL[cnt["l"] % 2]; cnt["l"] += 1
        return p

    dumped = set()

    def dump(name, ap, b, shape):
        if dumps is None or name in dumped:
            return
        dumped.add(name)
        dt_ = nc.dram_tensor("dbg_" + name, list(shape), F32, kind="ExternalOutput").ap()
        fw.dma("sp", [b], [], dt_, ap, is_output=True)
        dumps.append(name)

    def prep(t0, tb, lo, hi, d):
        nch = tb // CH
        zi = zt[cnt["z"] % NZ]; cnt["z"] += 1
        a = max(lo, t0 - 1); b = min(hi, t0 + tb + 1)
        for q in range(5):
            z, bz = zi[q]
            if t0 - 1 < lo:
                fw.op("pool", [], [bz], "memset", z[:, 0:1], 0.0)
            if t0 + tb + 1 > hi:
                fw.op("pool", [], [bz], "memset", z[:, tb + 1:tb + 2], 0.0)
            fw.dma("sp" if q % 2 == 0 else "act", [], [bz], z[:, a - (t0 - 1):b - (t0 - 1)], z_d[q, :, a:b])
            tmp, btmp = T["tmp"]; aa, baa = T["aa"]
            fw.op("pool", [bz], [btmp], "tensor_tensor", out=tmp[:, 0:tb], in0=z[:, 0:tb], in1=z[:, 2:tb + 2], op=ALU.add)
            fw.op("dve", [bz, bpd], [baa], "tensor_scalar", out=aa[:, 0:tb], in0=z[:, 1:tb + 1], scalar1=pd[:, q:q + 1],
                  scalar2=None, op0=ALU.mult)
            fw.op("dve", [btmp, baa, bpd], [zs[q][1]], "scalar_tensor_tensor", out=zs[q][0][:, 0:tb], in0=tmp[:, 0:tb],
                  scalar=pd[:, 5 + q:6 + q], in1=aa[:, 0:tb], op0=ALU.mult, op1=ALU.add)
        (zr, bzr), (zk, bzk), (zv, bzv), (zw, bzw), (zg, bzg) = zs
        tws, btws = T["tws"]
        fw.op("act", [bzw], [btws], "activation", out=tws[0:64, 0:tb], in_=zw[0:64, 0:tb], func=AF.Tanh)
        pw, bpw = lora_ps()
        fw.op("pe", [btws, bw2], [bpw], "matmul", pw[:, 0:tb], lhsT=w2s[:, d, :], rhs=tws[0:64, 0:tb], start=True, stop=True)
        sg, bsg = T["sg"]
        fw.op("act", [bpw, bpv], [bsg], "activation", out=sg[:, 0:tb], in_=pw[:, 0:tb], func=AF.Sigmoid,
              bias=pv[:, 5 + d:6 + d])
        pa, bpa = lora_ps()
        fw.op("pe", [bzw, ba2], [bpa], "matmul", pa[:, 0:tb], lhsT=a2s[64:128, d, :], rhs=zw[64:128, 0:tb],
              start=True, stop=True)
        ad, bad = T["ad"]
        fw.op("act", [bpa, bpv], [bad], "activation", out=ad[:, 0:tb], in_=pa[:, 0:tb], func=AF.Sigmoid,
              bias=pv[:, 7 + d:8 + d])
        kk, bkk = T["kk"]; kk2, bkk2 = T["kk2"]; rin, brin = T["rin"]; kkn, bkkn = T["kkn"]
        fw.op("dve", [bzk, bpv], [bkk], "tensor_scalar", out=kk[:, 0:tb], in0=zk[:, 0:tb], scalar1=pv[:, 9:10],
              scalar2=None, op0=ALU.mult)
        fw.op("pool", [bkk], [bkk2], "tensor_tensor", out=kk2[:, 0:tb], in0=kk[:, 0:tb], in1=kk[:, 0:tb], op=ALU.mult)
        pss, bpss = lora_ps()
        fw.op("pe", [bkk2, bob1], [bpss], "matmul", pss[:, 0:tb], lhsT=ob1[:], rhs=kk2[:, 0:tb], start=True, stop=True)
        fw.op("act", [bpss], [brin], "activation", out=rin[:, 0:tb], in_=pss[:, 0:tb], func=AF.Sqrt)
        fw.op("dve", [brin], [brin], "tensor_scalar", out=rin[:, 0:tb], in0=rin[:, 0:tb], scalar1=1e-12, scalar2=None,
              op0=ALU.max)
        fw.op("dve", [brin], [brin], "reciprocal", out=rin[:, 0:tb], in_=rin[:, 0:tb])
        fw.op("pool", [bkk, brin], [bkkn], "tensor_tensor", out=kkn[:, 0:tb], in0=kk[:, 0:tb], in1=rin[:, 0:tb], op=ALU.mult)
        tt, btt = T["tt"]; khat, bkhat = T["khat"]; bt, bbt = T["bt"]
        fw.op("dve", [bad, bpv, bpd], [btt], "tensor_scalar", out=tt[:, 0:tb], in0=ad[:, 0:tb], scalar1=pv[:, 10:11],
              scalar2=pd[:, 10:11], op0=ALU.mult, op1=ALU.add)
        fw.op("pool", [bzk, btt], [bkhat], "tensor_tensor", out=khat[:, 0:tb], in0=zk[:, 0:tb], in1=tt[:, 0:tb], op=ALU.mult)
        fw.op("pool", [bkkn, bad], [bbt], "tensor_tensor", out=bt[:, 0:tb], in0=kkn[:, 0:tb], in1=ad[:, 0:tb], op=ALU.mult)
        c1, bc1 = T["c1"]; c0, bc0 = T["c0"]; cr, bcr = T["cr"]; c2, bc2 = T["c2"]
        fw.op("dve", [bsm, bsg], [bc1], "tensor_tensor_scan", out=c1[:, 0:tb], data0=sm[:, 0:tb], data1=sg[:, 0:tb],
              initial=0.0, op0=ALU.mult, op1=ALU.add)
        fw.op("pool", [bc1, bsg], [bc0], "tensor_tensor", out=c0[:, 0:tb], in0=c1[:, 0:tb], in1=sg[:, 0:tb], op=ALU.subtract)
        c1v = c1[:, 0:tb].rearrange("p (c j) -> p c j", j=CH)
        totb = c1v[:, :, CH - 1:CH].to_broadcast([128, nch, CH])
        fw.op("dve", [bc1], [bcr], "tensor_tensor", out=cr[:, 0:tb].rearrange("p (c j) -> p c j", j=CH), in0=totb,
              in1=c1v, op=ALU.subtract)
        if d == 0:
            incl, bincl, excl, bexcl, hn, bhn = c1, bc1, c0, bc0, cr, bcr
        else:
            fw.op("pool", [bcr, bsg], [bc2], "tensor_tensor", out=c2[:, 0:tb], in0=cr[:, 0:tb], in1=sg[:, 0:tb], op=ALU.add)
            incl, bincl, excl, bexcl, hn, bhn = c2, bc2, cr, bcr, c0, bc0
        ei, bei = T["ei"]; ee, bee = T["ee"]; en, ben = T["en"]; eh, beh = T["eh"]
        fw.op("act", [bincl], [bei], "activation", out=ei[:, 0:tb], in_=incl[:, 0:tb], func=AF.Exp, scale=LAM)
        fw.op("act", [bexcl], [bee], "activation", out=ee[:, 0:tb], in_=excl[:, 0:tb], func=AF.Exp, scale=LAM)
        fw.op("act", [bincl], [ben], "activation", out=en[:, 0:tb], in_=incl[:, 0:tb], func=AF.Exp, scale=-LAM)
        fw.op("act", [bhn], [beh], "activation", out=eh[:, 0:tb], in_=hn[:, 0:tb], func=AF.Exp, scale=LAM)
        fw.op("act", [bc1], [bgC], "activation", out=gC[:, 0:nch], in_=c1v[:, :, CH - 1], func=AF.Exp, scale=LAM)
        v3 = lambda ap: ap[:, 0:tb].rearrange("p (c j) -> p c j", j=CH)
        fw.op("dve", [bkkn, bee], [bAR], "scalar_tensor_tensor", out=AR[:, 0:nch, 0, :], in0=v3(kkn), scalar=-1.0,
              in1=v3(ee), op0=ALU.mult, op1=ALU.mult)
        fw.op("pool", [bzr, bei], [bAR], "tensor_tensor", out=AR[:, 0:nch, 1, :], in0=v3(zr), in1=v3(ei), op=ALU.mult)
        for (nm, x0, bx0, e0, be0) in (("Bt", bt, bbt, en, ben), ("Kt", khat, bkhat, en, ben), ("Bh", bt, bbt, eh, beh),
                                       ("Kh", khat, bkhat, eh, beh)):
            eng = "dve" if nm in ("Bt", "Bh") else "pool"
            fw.op(eng, [bx0, be0], [T[nm][1]], "tensor_tensor", out=T[nm][0][:, 0:tb], in0=x0[:, 0:tb], in1=e0[:, 0:tb],
                  op=ALU.mult)
        for q in range(5):
            dump("zs%d" % q, zs[q][0][:, 0:tb], zs[q][1], [128, tb])
        for nm in ("sg", "ad", "kkn", "khat", "bt", "c1", "c0", "cr", "ei", "ee", "en", "eh", "Bt", "Kt", "Bh", "Kh"):
            dump(nm, T[nm][0][:, 0:tb], T[nm][1], [128, tb])
        dump("AR", AR[:, 0:nch, :, :], bAR, [128, nch, 2, CH])
        dump("gC", gC[:, 0:nch], bgC, [128, nch])

    def chunk(c, d, ycol):
        cs = slice(c * CH, (c + 1) * CH)
        Bt, bBt = T["Bt"]; Kt, bKt = T["Kt"]; Bh, bBh = T["Bh"]; Kh, bKh = T["Kh"]
        zv, bzv = zs[2]
        pa, bpa = pA
        for h in range(2):
            hs = slice(h * 64, (h + 1) * 64)
            arf = AR[hs, c, :, :].rearrange("p a j -> p (a j)")
            fw.op("pe", [bBt, bAR], [bpa], "matmul", pa[hs, 0:128], lhsT=Bt[hs, cs], rhs=arf, start=True, stop=True)
            fw.op("pe", [bKt, bAR], [bpa], "matmul", pa[hs, 128:256], lhsT=Kt[hs, cs], rhs=arf, start=True, stop=True)
            fw.op("pe", [bBt, bAR], [bpa], "matmul", pa[hs, 256:320], lhsT=AR[hs, c, 0, :], rhs=Bt[hs, cs],
                  start=True, stop=True)
        M, bM = Msb[cnt["m"] % 2]; cnt["m"] += 1
        fw.op("dve", [bpa, bmk], [bM], "tensor_tensor", out=M[:], in0=pa[:, 0:320].rearrange("p (a j) -> p a j", j=CH),
              in1=mk[:, d, :, :], op=ALU.mult)
        dump("M", M[:], bM, [128, 5, CH])
        if dbg == 21:
            return
        P, bP = PP[cnt["x"] % 2]
        X, bX = XX[cnt["x"] % 2]
        cnt["x"] += 1
        fw.op("pool", [bM, bid2], [bX], "tensor_tensor", out=X[:], in0=M[:, 0, :], in1=id2[:], op=ALU.add)
        pi_, bpi = pI
        curP = (M, bM, 0, 4)
        for i in range(1, 6):
            t_, bt_, ip, ipt = curP
            for h in range(2):
                hs = slice(h * 64, (h + 1) * 64)
                if i < 5:
                    fw.op("pe", [bt_], [bpi], "matmul", pi_[hs, 0:64], lhsT=t_[hs, ipt, :], rhs=t_[hs, ip, :],
                          start=True, stop=True)
                fw.op("pe", [bt_], [bpi], "matmul", pi_[hs, 64:128], lhsT=t_[hs, ip, :], rhs=t_[hs, ipt, :],
                      start=True, stop=True)
            Pn, bPn = PP[cnt["x"] % 2]; cnt["x"] += 1
            if i < 5:
                fw.op("act", [bpi], [bPn], "activation", out=Pn[:], in_=pi_[:, 0:128].rearrange("p (a j) -> p a j", j=CH),
                      func=AF.Copy)
            else:
                fw.op("act", [bpi], [bPn], "activation", out=Pn[:, 1, :], in_=pi_[:, 64:128], func=AF.Copy)
            for h in range(2):
                hs = slice(h * 64, (h + 1) * 64)
                fw.op("pe", [bPn, bX], [bpi], "matmul", pi_[hs, 128:192], lhsT=Pn[hs, 1, :], rhs=X[hs, :],
                      start=True, stop=True)
            Xn, bXn = XX[cnt["x"] % 2]
            fw.op("dve", [bpi, bX], [bXn], "tensor_tensor", out=Xn[:], in0=pi_[:, 128:192], in1=X[:], op=ALU.add)
            X, bX = Xn, bXn
            curP = (Pn, bPn, 0, 1)
        dump("X", X[:], bX, [128, CH])
        if dbg == 22:
            return
        ptt, bptt = pTt
        for h in range(2):
            hs = slice(h * 64, (h + 1) * 64)
            fw.op("pe", [bzv, bid], [bptt], "matmul", ptt[hs, 0:64], lhsT=zv[hs, cs], rhs=ident[hs, hs], start=True, stop=True)
            fw.op("pe", [bBh, bid], [bptt], "matmul", ptt[hs, 64:128], lhsT=Bh[hs, cs], rhs=ident[hs, hs], start=True, stop=True)
            fw.op("pe", [bKh, bid], [bptt], "matmul", ptt[hs, 128:192], lhsT=Kh[hs, cs], rhs=ident[hs, hs], start=True, stop=True)
        tm, btm = TM[cnt["t"] % 2]; cnt["t"] += 1
        fw.op("act", [bptt], [btm], "activation", out=tm[:], in_=ptt[:, 0:192].rearrange("p (a j) -> p a j", j=CH),
              func=AF.Copy)
        dump("tm", tm[:], btm, [128, 3, CH])
        if dbg == 23:
            return
        S0, bS0 = ST[cnt["st"] % 2]
        S1, bS1 = ST[(cnt["st"] + 1) % 2]
        cnt["st"] += 1
        ps, bps = pSt[cnt["s"] % 2]; cnt["s"] += 1
        for h in range(2):
            hs = slice(h * 64, (h + 1) * 64)
            fw.op("pe", [bAR, bS0], [bps], "matmul", ps[hs, 0:64], lhsT=AR[hs, c, 0, :], rhs=S0[hs, :], start=True, stop=False)
            fw.op("pe", [bM, btm], [bps], "matmul", ps[hs, 0:64], lhsT=M[hs, 2, :], rhs=tm[hs, 0, :], start=False, stop=True)
        fw.op("act", [bps], [bW1], "activation", out=W1[:], in_=ps[:, 0:64], func=AF.Copy)
        for h in range(2):
            hs = slice(h * 64, (h + 1) * 64)
            fw.op("pe", [bX, bW1], [bps], "matmul", ps[hs, 64:128], lhsT=X[hs, :], rhs=W1[hs, :], start=True, stop=True)
        fw.op("act", [bps], [bU], "activation", out=U[:], in_=ps[:, 64:128], func=AF.Copy)
        for h in range(2):
            hs = slice(h * 64, (h + 1) * 64)
            fw.op("pe", [btm, bU], [bps], "matmul", ps[hs, 128:192], lhsT=tm[hs, 1, :], rhs=U[hs, :], start=True, stop=False)
            fw.op("pe", [btm], [bps], "matmul", ps[hs, 128:192], lhsT=tm[hs, 2, :], rhs=tm[hs, 0, :], start=False, stop=True)
        for h in range(2):
            hs = slice(h * 64, (h + 1) * 64)
            fw.op("pe", [bS0, bAR], [bps], "matmul", ps[hs, 192:256], lhsT=S0[hs, :], rhs=AR[hs, c, 1, :], start=True, stop=False)
            fw.op("pe", [bU, bM], [bps], "matmul", ps[hs, 192:256], lhsT=U[hs, :], rhs=M[hs, 1, :], start=False, stop=False)
            fw.op("pe", [btm, bM], [bps], "matmul", ps[hs, 192:256], lhsT=tm[hs, 0, :], rhs=M[hs, 3, :], start=False, stop=True)
        fw.op("dve", [bS0, bgC, bps], [bS1], "scalar_tensor_tensor", out=S1[:], in0=S0[:], scalar=gC[:, c:c + 1],
              in1=ps[:, 128:192], op0=ALU.mult, op1=ALU.add)
        yb, byb = T["yb"]
        fw.op("act", [bps], [byb], "activation", out=yb[:, cs], in_=ps[:, 192:256], func=AF.Copy)
        dump("W1", W1[:], bW1, [128, CH])
        dump("U", U[:], bU, [128, CH])
        dump("S1", S1[:], bS1, [128, CH])
        dump("ybc", yb[:, cs], byb, [128, CH])

    def readout(t0, tb):
        (zr, bzr), (zk, bzk), (zv, bzv), (zw, bzw), (zg, bzg) = zs
        yb, byb = T["yb"]; yfb, byfb = T["yfb"]; yy, byy = T["yy"]; yc, byc = T["yc"]; yc2, byc2 = T["yc2"]
        rs2, brs2 = T["rs2"]; sgd, bsgd = T["sgd"]; ks, bks = T["ks"]; pr, bpr = T["pr"]; af, baf = T["af"]
        ad, bad = T["ad"]
        fw.dma("sp", [byf_dram], [byfb], yfb[:, 0:tb], yf_o[:, t0:t0 + tb])
        fw.op("dve", [byb, byfb], [byy], "tensor_tensor", out=yy[:, 0:tb], in0=yb[:, 0:tb], in1=yfb[:, 0:tb], op=ALU.add)
        pm, bpm = lora_ps()
        fw.op("pe", [byy, bob64], [bpm], "matmul", pm[:, 0:tb], lhsT=ob64[:], rhs=yy[:, 0:tb], start=True, stop=True)
        fw.op("dve", [byy, bpm], [byc], "tensor_tensor", out=yc[:, 0:tb], in0=yy[:, 0:tb], in1=pm[:, 0:tb], op=ALU.subtract)
        fw.op("pool", [byc], [byc2], "tensor_tensor", out=yc2[:, 0:tb], in0=yc[:, 0:tb], in1=yc[:, 0:tb], op=ALU.mult)
        pvv, bpvv = lora_ps()
        fw.op("pe", [byc2, bob64], [bpvv], "matmul", pvv[:, 0:tb], lhsT=ob64[:], rhs=yc2[:, 0:tb], start=True, stop=True)
        fw.op("dve", [bpvv], [brs2], "tensor_scalar", out=rs2[:, 0:tb], in0=pvv[:, 0:tb], scalar1=GN_EPS, scalar2=None,
              op0=ALU.add)
        fw.op("act", [brs2], [brs2], "activation", out=rs2[:, 0:tb], in_=rs2[:, 0:tb], func=AF.Sqrt)
        fw.op("dve", [brs2], [brs2], "reciprocal", out=rs2[:, 0:tb], in_=rs2[:, 0:tb])
        fw.op("pool", [byc, brs2], [byc], "tensor_tensor", out=yc[:, 0:tb], in0=yc[:, 0:tb], in1=rs2[:, 0:tb], op=ALU.mult)
        fw.op("dve", [byc, bpv], [byc], "tensor_scalar", out=yc[:, 0:tb], in0=yc[:, 0:tb], scalar1=pv[:, 12:13],
              scalar2=pv[:, 13:14], op0=ALU.mult, op1=ALU.add)
        pa, bpa = lora_ps()
        fw.op("pe", [bzw, ba2], [bpa], "matmul", pa[:, 0:tb], lhsT=a2s[64:128, 0, :], rhs=zw[64:128, 0:tb],
              start=True, stop=True)
        fw.op("act", [bpa, bpv], [baf], "activation", out=af[:, 0:tb], in_=pa[:, 0:tb], func=AF.Sigmoid, bias=pv[:, 7:8])
        fw.op("pool", [baf, bad], [baf], "tensor_tensor", out=af[:, 0:tb], in0=af[:, 0:tb], in1=ad[:, 0:tb], op=ALU.add)
        fw.op("dve", [baf, bpv, bpd], [baf], "tensor_scalar", out=af[:, 0:tb], in0=af[:, 0:tb], scalar1=pv[:, 10:11],
              scalar2=pd[:, 11:12], op0=ALU.mult, op1=ALU.add)
        fw.op("pool", [bzk, baf], [bks], "tensor_tensor", out=ks[:, 0:tb], in0=zk[:, 0:tb], in1=af[:, 0:tb], op=ALU.mult)
        fw.op("dve", [bzr, bpv, bks], [bpr], "scalar_tensor_tensor", out=pr[:, 0:tb], in0=zr[:, 0:tb], scalar=pv[:, 11:12],
              in1=ks[:, 0:tb], op0=ALU.mult, op1=ALU.mult)
        pb, bpb = lora_ps()
        fw.op("pe", [bpr, bob1], [bpb], "matmul", pb[:, 0:tb], lhsT=ob1[:], rhs=pr[:, 0:tb], start=True, stop=True)
        fw.op("dve", [bpb, bzv], [bks], "tensor_tensor", out=ks[:, 0:tb], in0=pb[:, 0:tb], in1=zv[:, 0:tb], op=ALU.mult)
        fw.op("pool", [bks, byc], [byc], "tensor_tensor", out=yc[:, 0:tb], in0=yc[:, 0:tb], in1=ks[:, 0:tb], op=ALU.add)
        fw.op("act", [bzg], [bsgd], "activation", out=sgd[:, 0:tb], in_=zg[:, 0:tb], func=AF.Sigmoid)
        pg, bpg = lora_ps()
        fw.op("pe", [bsgd, bg2], [bpg], "matmul", pg[:, 0:tb], lhsT=g2s[:], rhs=sgd[:, 0:tb], start=True, stop=True)
        fw.op("dve", [bpg, byc], [bob_out], "tensor_tensor", out=ob_out[:, 0:tb], in0=pg[:, 0:tb], in1=yc[:, 0:tb], op=ALU.mult)
        fw.dma("sp", [bob_out], [], yd_o[:, t0:t0 + tb], ob_out[:, 0:tb], is_output=True)

    byf_dram = Buf("yf_dram")
    lat_blocks = [(t0, min(TBK, NLAT - t0), 0, NLAT) for t0 in range(0, NLAT, TBK)]
    ctx_blocks = [(NLAT + t0, min(TBK, NCTX - t0), NLAT, NTOT) for t0 in range(0, NCTX, TBK)]
    for d in range(2):
        S0, bS0 = ST[cnt["st"] % 2]
        fw.op("pool", [], [bS0], "memset", S0[:], 0.0)
        if d == 0:
            order = ctx_blocks + lat_blocks
        else:
            order = ctx_blocks[::-1] + lat_blocks[::-1]
        for (t0, tb, lo, hi) in order:
            prep(t0, tb, lo, hi, d)
            nch = tb // CH
            chs = range(nch) if d == 0 else range(nch - 1, -1, -1)
            for c in chs:
                if dbg >= 2:
                    chunk(c, d, None)
            if dbg != 3:
                continue
            if d == 0:
                yb, byb = T["yb"]
                fw.dma("sp", [byb], [byf_dram], yf_o[:, t0:t0 + tb], yb[:, 0:tb], is_output=True)
            else:
                readout(t0, tb)
    fw.finish()
    return nc


def build_kbr2(NLAT, NCTX, dbg=3, dumps=None):
    NTOT = NLAT + NCTX
    TBK = 512
    nc = bass.Bass("TRN2", target_bir_lowering=False)
    z_d = nc.dram_tensor("z", [5, 128, NTOT], F32, kind="ExternalInput").ap()
    pv_d = nc.dram_tensor("pv", [128, 14], F32, kind="ExternalInput").ap()
    w2_d = nc.dram_tensor("w2", [2, 64, 128], F32, kind="ExternalInput").ap()
    a2_d = nc.dram_tensor("a2", [2, 64, 128], F32, kind="ExternalInput").ap()
    g2_d = nc.dram_tensor("g2", [128, 128], F32, kind="ExternalInput").ap()
    mk_d = nc.dram_tensor("mk", [2, 128, 5, 64], F32, kind="ExternalInput").ap()
    ob1_d = nc.dram_tensor("ob1", [128, 128], F32, kind="ExternalInput").ap()
    sm_d = nc.dram_tensor("sm", [128, 512], F32, kind="ExternalInput").ap()
    id2_d = nc.dram_tensor("id2", [128, 64], F32, kind="ExternalInput").ap()
    yd_o = nc.dram_tensor("yd_o", [128, NTOT], BF16, kind="ExternalOutput").ap()
    yf_o = nc.dram_tensor("yf_o", [128, NTOT], F32, kind="ExternalOutput").ap()
    fw = FW(nc)

    def sb(name, shape, dt=F32):
        return SBT(nc, name, shape, dt), Buf(name)

    pv, bpv = sb("pv_s", [128, 14])
    pd, bpd = sb("pd_s", [128, 16])
    w2s, bw2 = sb("w2_s", [64, 2, 128])
    a2s, ba2 = sb("a2_s", [128, 2, 128])
    g2s, bg2 = sb("g2_s", [128, 128])
    mk, bmk = sb("mk_s", [128, 2, 5, 64])
    ob1, bob1 = sb("ob1_s", [128, 128])
    ob64, bob64 = sb("ob64_s", [128, 128])
    sm, bsm = sb("sm_s", [128, 512])
    id2, bid2 = sb("id2_s", [128, 64])
    ident, bid = make_ident(fw, nc)
    fw.dma("sp", [], [bpv], pv[:], pv_d[:, :])
    fw.dma("sp", [], [bw2], w2s[:], w2_d.rearrange("d k n -> k d n"))
    fw.dma("sp", [], [ba2], a2s[64:128, :, :], a2_d.rearrange("d k n -> k d n"))
    fw.dma("sp", [], [bg2], g2s[:], g2_d[:, :])
    fw.dma("act", [], [bmk], mk[:], mk_d.rearrange("d p a f -> p d a f"))
    fw.dma("act", [], [bob1], ob1[:], ob1_d[:, :])
    fw.dma("act", [], [bsm], sm[:], sm_d[:, :])
    fw.dma("act", [], [bid2], id2[:], id2_d[:, :])
    fw.op("dve", [bob1], [bob64], "tensor_scalar", out=ob64[:], in0=ob1[:], scalar1=1.0 / 64, scalar2=None, op0=ALU.mult)
    fw.op("dve", [bpv], [bpd], "tensor_scalar", out=pd[:, 0:5], in0=pv[:, 0:5], scalar1=-1.0, scalar2=1.0,
          op0=ALU.mult, op1=ALU.add)
    fw.op("dve", [bpv], [bpd], "tensor_scalar", out=pd[:, 5:10], in0=pv[:, 0:5], scalar1=0.5, scalar2=None, op0=ALU.mult)
    fw.op("dve", [bpv], [bpd], "tensor_scalar", out=pd[:, 10:11], in0=pv[:, 10:11], scalar1=-1.0, scalar2=1.0,
          op0=ALU.mult, op1=ALU.add)
    fw.op("dve", [bpv], [bpd], "tensor_scalar", out=pd[:, 11:12], in0=pv[:, 10:11], scalar1=-2.0, scalar2=2.0,
          op0=ALU.mult, op1=ALU.add)

    NZ = 2
    zt = [[sb("zt%d_%d" % (i, q), [128, TBK + 2]) for q in range(5)] for i in range(NZ)]
    zsP = [[sb("zs%d_%d" % (par, q), [128, TBK]) for q in range(5)] for par in range(2)]
    adP = [sb("adP%d" % par, [128, TBK]) for par in range(2)]
    ybP = [sb("ybP%d" % par, [128, TBK]) for par in range(2)]
    names = ["tmp", "aa", "tws", "sg", "ad", "af", "kk", "kk2", "rin", "kkn", "tt", "khat", "bt", "c1", "c0", "cr", "c2",
             "ei", "ee", "en", "eh", "Bt", "Kt", "Bh", "Kh", "yb", "yfb", "yy", "yc", "yc2", "rs2", "sgd", "ks", "pr"]
    T = {n: sb("r_" + n, [128, TBK]) for n in names}
    ARP = [sb("AR%d" % par, [128, TBK // CH, 2, CH]) for par in range(2)]
    gCP = [sb("gC%d" % par, [128, TBK // CH]) for par in range(2)]
    ob_out, bob_out = sb("ob_out", [128, TBK], BF16)
    NCK = TBK // CH
    MP = [[sb("M%d_%d" % (par, c), [128, 5, CH]) for c in range(NCK)] for par in range(2)]
    XP = [[sb("X%d_%d" % (par, c), [128, CH]) for c in range(NCK)] for par in range(2)]
    TMP = [[sb("TM%d_%d" % (par, c), [128, 3, CH]) for c in range(NCK)] for par in range(2)]
    PPc = [[sb("PP%d_%d" % (c, i), [128, 2, CH]) for i in range(2)] for c in range(NCK)]
    W1, bW1 = sb("W1", [128, CH])
    U, bU = sb("U", [128, CH])
    ST = [sb("ST%d" % i, [128, CH]) for i in range(2)]
    pL = [(PST(nc, "pL%d" % i, [128, TBK], F32), Buf(excl=True)) for i in range(2)]
    pAs = [(PST(nc, "pA%d" % i, [128, 512], F32), Buf(excl=True)) for i in range(2)]
    pIs = [(PST(nc, "pI%d" % i, [128, 512], F32), Buf(excl=True)) for i in range(2)]
    pSt = [(PST(nc, "pSt%d" % i, [128, 512], F32), Buf(excl=True)) for i in range(2)]
    cnt = {"z": 0, "l": 0, "m": 0, "x": 0, "t": 0, "s": 0, "st": 0, "a": 0, "i": 0}

    def lora_ps():
        p = pL[cnt["l"] % 2]; cnt["l"] += 1
        return p

    dumped = set()

    def dump(name, ap, b, shape):
        if dumps is None or name in dumped:
            return
        dumped.add(name)
        dt_ = nc.dram_tensor("dbg_" + name, list(shape), F32, kind="ExternalOutput").ap()
        fw.dma("sp", [b], [], dt_, ap, is_output=True)
        dumps.append(name)

    def prep(t0, tb, lo, hi, d, par):
        nch = tb // CH
        zs = zsP[par]
        AR, bAR = ARP[par]
        gC, bgC = gCP[par]
        zi = zt[cnt["z"] % NZ]; cnt["z"] += 1
        a = max(lo, t0 - 1); b = min(hi, t0 + tb + 1)
        for q in range(5):
            z, bz = zi[q]
            if t0 - 1 < lo:
                fw.op("pool", [], [bz], "memset", z[:, 0:1], 0.0)
            if t0 + tb + 1 > hi:
                fw.op("pool", [], [bz], "memset", z[:, tb + 1:tb + 2], 0.0)
            fw.dma("sp" if q % 2 == 0 else "act", [], [bz], z[:, a - (t0 - 1):b - (t0 - 1)], z_d[q, :, a:b])
            tmp, btmp = T["tmp"]; aa, baa = T["aa"]
            fw.op("pool", [bz], [btmp], "tensor_tensor", out=tmp[:, 0:tb], in0=z[:, 0:tb], in1=z[:, 2:tb + 2], op=ALU.add)
            fw.op("dve", [bz, bpd], [baa], "tensor_scalar", out=aa[:, 0:tb], in0=z[:, 1:tb + 1], scalar1=pd[:, q:q + 1],
                  scalar2=None, op0=ALU.mult)
            fw.op("dve", [btmp, baa, bpd], [zs[q][1]], "scalar_tensor_tensor", out=zs[q][0][:, 0:tb], in0=tmp[:, 0:tb],
                  scalar=pd[:, 5 + q:6 + q], in1=aa[:, 0:tb], op0=ALU.mult, op1=ALU.add)
        (zr, bzr), (zk, bzk), (zv, bzv), (zw, bzw), (zg, bzg) = zs
        tws, btws = T["tws"]
        fw.op("act", [bzw], [btws], "activation", out=tws[0:64, 0:tb], in_=zw[0:64, 0:tb], func=AF.Tanh)
        pw, bpw = lora_ps()
        fw.op("pe", [btws, bw2], [bpw], "matmul", pw[:, 0:tb], lhsT=w2s[:, d, :], rhs=tws[0:64, 0:tb], start=True, stop=True)
        sg, bsg = T["sg"]
        fw.op("act", [bpw, bpv], [bsg], "activation", out=sg[:, 0:tb], in_=pw[:, 0:tb], func=AF.Sigmoid,
              bias=pv[:, 5 + d:6 + d])
        pa, bpa = lora_ps()
        fw.op("pe", [bzw, ba2], [bpa], "matmul", pa[:, 0:tb], lhsT=a2s[64:128, d, :], rhs=zw[64:128, 0:tb],
              start=True, stop=True)
        ad, bad = adP[par]
        fw.op("act", [bpa, bpv], [bad], "activation", out=ad[:, 0:tb], in_=pa[:, 0:tb], func=AF.Sigmoid,
              bias=pv[:, 7 + d:8 + d])
        kk, bkk = T["kk"]; kk2, bkk2 = T["kk2"]; rin, brin = T["rin"]; kkn, bkkn = T["kkn"]
        fw.op("dve", [bzk, bpv], [bkk], "tensor_scalar", out=kk[:, 0:tb], in0=zk[:, 0:tb], scalar1=pv[:, 9:10],
              scalar2=None, op0=ALU.mult)
        fw.op("pool", [bkk], [bkk2], "tensor_tensor", out=kk2[:, 0:tb], in0=kk[:, 0:tb], in1=kk[:, 0:tb], op=ALU.mult)
        pss, bpss = lora_ps()
        fw.op("pe", [bkk2, bob1], [bpss], "matmul", pss[:, 0:tb], lhsT=ob1[:], rhs=kk2[:, 0:tb], start=True, stop=True)
        fw.op("act", [bpss], [brin], "activation", out=rin[:, 0:tb], in_=pss[:, 0:tb], func=AF.Sqrt)
        fw.op("dve", [brin], [brin], "tensor_scalar", out=rin[:, 0:tb], in0=rin[:, 0:tb], scalar1=1e-12, scalar2=None,
              op0=ALU.max)
        fw.op("dve", [brin], [brin], "reciprocal", out=rin[:, 0:tb], in_=rin[:, 0:tb])
        fw.op("pool", [bkk, brin], [bkkn], "tensor_tensor", out=kkn[:, 0:tb], in0=kk[:, 0:tb], in1=rin[:, 0:tb], op=ALU.mult)
        tt, btt = T["tt"]; khat, bkhat = T["khat"]; bt, bbt = T["bt"]
        fw.op("dve", [bad, bpv, bpd], [btt], "tensor_scalar", out=tt[:, 0:tb], in0=ad[:, 0:tb], scalar1=pv[:, 10:11],
              scalar2=pd[:, 10:11], op0=ALU.mult, op1=ALU.add)
        fw.op("pool", [bzk, btt], [bkhat], "tensor_tensor", out=khat[:, 0:tb], in0=zk[:, 0:tb], in1=tt[:, 0:tb], op=ALU.mult)
        fw.op("pool", [bkkn, bad], [bbt], "tensor_tensor", out=bt[:, 0:tb], in0=kkn[:, 0:tb], in1=ad[:, 0:tb], op=ALU.mult)
        c1, bc1 = T["c1"]; c0, bc0 = T["c0"]; cr, bcr = T["cr"]; c2, bc2 = T["c2"]
        fw.op("dve", [bsm, bsg], [bc1], "tensor_tensor_scan", out=c1[:, 0:tb], data0=sm[:, 0:tb], data1=sg[:, 0:tb],
              initial=0.0, op0=ALU.mult, op1=ALU.add)
        fw.op("pool", [bc1, bsg], [bc0], "tensor_tensor", out=c0[:, 0:tb], in0=c1[:, 0:tb], in1=sg[:, 0:tb], op=ALU.subtract)
        c1v = c1[:, 0:tb].rearrange("p (c j) -> p c j", j=CH)
        totb = c1v[:, :, CH - 1:CH].to_broadcast([128, nch, CH])
        fw.op("dve", [bc1], [bcr], "tensor_tensor", out=cr[:, 0:tb].rearrange("p (c j) -> p c j", j=CH), in0=totb,
              in1=c1v, op=ALU.subtract)
        if d == 0:
            incl, bincl, excl, bexcl, hn, bhn = c1, bc1, c0, bc0, cr, bcr
        else:
            fw.op("pool", [bcr, bsg], [bc2], "tensor_tensor", out=c2[:, 0:tb], in0=cr[:, 0:tb], in1=sg[:, 0:tb], op=ALU.add)
            incl, bincl, excl, bexcl, hn, bhn = c2, bc2, cr, bcr, c0, bc0
        ei, bei = T["ei"]; ee, bee = T["ee"]; en, ben = T["en"]; eh, beh = T["eh"]
        fw.op("act", [bincl], [bei], "activation", out=ei[:, 0:tb], in_=incl[:, 0:tb], func=AF.Exp, scale=LAM)
        fw.op("act", [bexcl], [bee], "activation", out=ee[:, 0:tb], in_=excl[:, 0:tb], func=AF.Exp, scale=LAM)
        fw.op("act", [bincl], [ben], "activation", out=en[:, 0:tb], in_=incl[:, 0:tb], func=AF.Exp, scale=-LAM)
        fw.op("act", [bhn], [beh], "activation", out=eh[:, 0:tb], in_=hn[:, 0:tb], func=AF.Exp, scale=LAM)
        fw.op("act", [bc1], [bgC], "activation", out=gC[:, 0:nch], in_=c1v[:, :, CH - 1], func=AF.Exp, scale=LAM)
        v3 = lambda ap: ap[:, 0:tb].rearrange("p (c j) -> p c j", j=CH)
        fw.op("dve", [bkkn, bee], [bAR], "scalar_tensor_tensor", out=AR[:, 0:nch, 0, :], in0=v3(kkn), scalar=-1.0,
              in1=v3(ee), op0=ALU.mult, op1=ALU.mult)
        fw.op("pool", [bzr, bei], [bAR], "tensor_tensor", out=AR[:, 0:nch, 1, :], in0=v3(zr), in1=v3(ei), op=ALU.mult)
        for (nm, x0, bx0, e0, be0) in (("Bt", bt, bbt, en, ben), ("Kt", khat, bkhat, en, ben), ("Bh", bt, bbt, eh, beh),
                                       ("Kh", khat, bkhat, eh, beh)):
            eng = "dve" if nm in ("Bt", "Bh") else "pool"
            fw.op(eng, [bx0, be0], [T[nm][1]], "tensor_tensor", out=T[nm][0][:, 0:tb], in0=x0[:, 0:tb], in1=e0[:, 0:tb],
                  op=ALU.mult)

    def stage1(par, c, d):
        cs = slice(c * CH, (c + 1) * CH)
        AR, bAR = ARP[par]
        Bt, bBt = T["Bt"]; Kt, bKt = T["Kt"]
        pa, bpa = pAs[cnt["a"] % 2]; cnt["a"] += 1
        for h in range(2):
            hs = slice(h * 64, (h + 1) * 64)
            arf = AR[hs, c, :, :].rearrange("p a j -> p (a j)")
            fw.op("pe", [bBt, bAR], [bpa], "matmul", pa[hs, 0:128], lhsT=Bt[hs, cs], rhs=arf, start=True, stop=True)
            fw.op("pe", [bKt, bAR], [bpa], "matmul", pa[hs, 128:256], lhsT=Kt[hs, cs], rhs=arf, start=True, stop=True)
            fw.op("pe", [bBt, bAR], [bpa], "matmul", pa[hs, 256:320], lhsT=AR[hs, c, 0, :], rhs=Bt[hs, cs],
                  start=True, stop=True)
        M, bM = MP[par][c]
        fw.op("dve", [bpa, bmk], [bM], "tensor_tensor", out=M[:], in0=pa[:, 0:320].rearrange("p (a j) -> p a j", j=CH),
              in1=mk[:, d, :, :], op=ALU.mult)
        X, bX = XP[par][c]
        fw.op("pool", [bM, bid2], [bX], "tensor_tensor", out=X[:], in0=M[:, 0, :], in1=id2[:], op=ALU.add)

    def inv_round(par, chunks, i):
        for c in chunks:
            if i == 1:
                t_, bt_, ip, ipt = MP[par][c][0], MP[par][c][1], 0, 4
            else:
                t_, bt_ = PPc[c][(i - 1) % 2]
                ip, ipt = 0, 1
            pi_, bpi = pIs[cnt["i"] % 2]; cnt["i"] += 1
            for h in range(2):
                hs = slice(h * 64, (h + 1) * 64)
                if i < 5:
                    fw.op("pe", [bt_], [bpi], "matmul", pi_[hs, 0:64], lhsT=t_[hs, ipt, :], rhs=t_[hs, ip, :],
                          start=True, stop=True)
                fw.op("pe", [bt_], [bpi], "matmul", pi_[hs, 64:128], lhsT=t_[hs, ip, :], rhs=t_[hs, ipt, :],
                      start=True, stop=True)
            Pn, bPn = PPc[c][i % 2]
            if i < 5:
                fw.op("act", [bpi], [bPn], "activation", out=Pn[:], in_=pi_[:, 0:128].rearrange("p (a j) -> p a j", j=CH),
                      func=AF.Copy)
            else:
                fw.op("act", [bpi], [bPn], "activation", out=Pn[:, 1, :], in_=pi_[:, 64:128], func=AF.Copy)
        for c in chunks:
            Pn, bPn = PPc[c][i % 2]
            X, bX = XP[par][c]
            pi_, bpi = pIs[cnt["i"] % 2]; cnt["i"] += 1
            for h in range(2):
                hs = slice(h * 64, (h + 1) * 64)
                fw.op("pe", [bPn, bX], [bpi], "matmul", pi_[hs, 128:192], lhsT=Pn[hs, 1, :], rhs=X[hs, :],
                      start=True, stop=True)
            fw.op("dve", [bpi, bX], [bX], "tensor_tensor", out=X[:], in0=pi_[:, 128:192], in1=X[:], op=ALU.add)

    def transposes(par, c):
        cs = slice(c * CH, (c + 1) * CH)
        Bh, bBh = T["Bh"]; Kh, bKh = T["Kh"]
        zv, bzv = zsP[par][2]
        ptt, bptt = pAs[cnt["a"] % 2]; cnt["a"] += 1
        for h in range(2):
            hs = slice(h * 64, (h + 1) * 64)
            fw.op("pe", [bzv, bid], [bptt], "matmul", ptt[hs, 0:64], lhsT=zv[hs, cs], rhs=ident[hs, hs], start=True, stop=True)
            fw.op("pe", [bBh, bid], [bptt], "matmul", ptt[hs, 64:128], lhsT=Bh[hs, cs], rhs=ident[hs, hs], start=True, stop=True)
            fw.op("pe", [bKh, bid], [bptt], "matmul", ptt[hs, 128:192], lhsT=Kh[hs, cs], rhs=ident[hs, hs], start=True, stop=True)
        tm, btm = TMP[par][c]
        fw.op("act", [bptt], [btm], "activation", out=tm[:], in_=ptt[:, 0:192].rearrange("p (a j) -> p a j", j=CH),
              func=AF.Copy)

    def pre_slices(par, nch, d):
        chunks = list(range(nch))
        half = (nch + 1) // 2
        sl = [lambda: [stage1(par, c, d) for c in chunks[:half]], lambda: [stage1(par, c, d) for c in chunks[half:]]]
        for i in range(1, 6):
            sl.append(lambda i=i: inv_round(par, chunks, i))
        sl.append(lambda: [transposes(par, c) for c in chunks])
        return sl

    def state(par, c, d):
        cs = slice(c * CH, (c + 1) * CH)
        AR, bAR = ARP[par]
        gC, bgC = gCP[par]
        M, bM = MP[par][c]
        X, bX = XP[par][c]
        tm, btm = TMP[par][c]
        S0, bS0 = ST[cnt["st"] % 2]
        S1, bS1 = ST[(cnt["st"] + 1) % 2]
        cnt["st"] += 1
        ps, bps = pSt[cnt["s"] % 2]; cnt["s"] += 1
        for h in range(2):
            hs = slice(h * 64, (h + 1) * 64)
            fw.op("pe", [bAR, bS0], [bps], "matmul", ps[hs, 0:64], lhsT=AR[hs, c, 0, :], rhs=S0[hs, :], start=True, stop=False)
            fw.op("pe", [bM, btm], [bps], "matmul", ps[hs, 0:64], lhsT=M[hs, 2, :], rhs=tm[hs, 0, :], start=False, stop=True)
        fw.op("act", [bps], [bW1], "activation", out=W1[:], in_=ps[:, 0:64], func=AF.Copy)
        for h in range(2):
            hs = slice(h * 64, (h + 1) * 64)
            fw.op("pe", [bX, bW1], [bps], "matmul", ps[hs, 64:128], lhsT=X[hs, :], rhs=W1[hs, :], start=True, stop=True)
        fw.op("act", [bps], [bU], "activation", out=U[:], in_=ps[:, 64:128], func=AF.Copy)
        for h in range(2):
            hs = slice(h * 64, (h + 1) * 64)
            fw.op("pe", [btm, bU], [bps], "matmul", ps[hs, 128:192], lhsT=tm[hs, 1, :], rhs=U[hs, :], start=True, stop=False)
            fw.op("pe", [btm], [bps], "matmul", ps[hs, 128:192], lhsT=tm[hs, 2, :], rhs=tm[hs, 0, :], start=False, stop=True)
        for h in range(2):
            hs = slice(h * 64, (h + 1) * 64)
            fw.op("pe", [bS0, bAR], [bps], "matmul", ps[hs, 192:256], lhsT=S0[hs, :], rhs=AR[hs, c, 1, :], start=True, stop=False)
            fw.op("pe", [bU, bM], [bps], "matmul", ps[hs, 192:256], lhsT=U[hs, :], rhs=M[hs, 1, :], start=False, stop=False)
            fw.op("pe", [btm, bM], [bps], "matmul", ps[hs, 192:256], lhsT=tm[hs, 0, :], rhs=M[hs, 3, :], start=False, stop=True)
        fw.op("dve", [bS0, bgC, bps], [bS1], "scalar_tensor_tensor", out=S1[:], in0=S0[:], scalar=gC[:, c:c + 1],
              in1=ps[:, 128:192], op0=ALU.mult, op1=ALU.add)
        yb, byb = ybP[par]
        fw.op("act", [bps], [byb], "activation", out=yb[:, cs], in_=ps[:, 192:256], func=AF.Copy)

    def readout(t0, tb, par):
        zs = zsP[par]
        (zr, bzr), (zk, bzk), (zv, bzv), (zw, bzw), (zg, bzg) = zs
        yb, byb = ybP[par]; yfb, byfb = T["yfb"]; yy, byy = T["yy"]; yc, byc = T["yc"]; yc2, byc2 = T["yc2"]
        rs2, brs2 = T["rs2"]; sgd, bsgd = T["sgd"]; ks, bks = T["ks"]; pr, bpr = T["pr"]; af, baf = T["af"]
        ad, bad = adP[par]
        fw.dma("sp", [byf_dram], [byfb], yfb[:, 0:tb], yf_o[:, t0:t0 + tb])
        fw.op("dve", [byb, byfb], [byy], "tensor_tensor", out=yy[:, 0:tb], in0=yb[:, 0:tb], in1=yfb[:, 0:tb], op=ALU.add)
        pm, bpm = lora_ps()
        fw.op("pe", [byy, bob64], [bpm], "matmul", pm[:, 0:tb], lhsT=ob64[:], rhs=yy[:, 0:tb], start=True, stop=True)
        fw.op("dve", [byy, bpm], [byc], "tensor_tensor", out=yc[:, 0:tb], in0=yy[:, 0:tb], in1=pm[:, 0:tb], op=ALU.subtract)
        fw.op("pool", [byc], [byc2], "tensor_tensor", out=yc2[:, 0:tb], in0=yc[:, 0:tb], in1=yc[:, 0:tb], op=ALU.mult)
        pvv, bpvv = lora_ps()
        fw.op("pe", [byc2, bob64], [bpvv], "matmul", pvv[:, 0:tb], lhsT=ob64[:], rhs=yc2[:, 0:tb], start=True, stop=True)
        fw.op("dve", [bpvv], [brs2], "tensor_scalar", out=rs2[:, 0:tb], in0=pvv[:, 0:tb], scalar1=GN_EPS, scalar2=None,
              op0=ALU.add)
        fw.op("act", [brs2], [brs2], "activation", out=rs2[:, 0:tb], in_=rs2[:, 0:tb], func=AF.Sqrt)
        fw.op("dve", [brs2], [brs2], "reciprocal", out=rs2[:, 0:tb], in_=rs2[:, 0:tb])
        fw.op("pool", [byc, brs2], [byc], "tensor_tensor", out=yc[:, 0:tb], in0=yc[:, 0:tb], in1=rs2[:, 0:tb], op=ALU.mult)
        fw.op("dve", [byc, bpv], [byc], "tensor_scalar", out=yc[:, 0:tb], in0=yc[:, 0:tb], scalar1=pv[:, 12:13],
              scalar2=pv[:, 13:14], op0=ALU.mult, op1=ALU.add)
        pa, bpa = lora_ps()
        fw.op("pe", [bzw, ba2], [bpa], "matmul", pa[:, 0:tb], lhsT=a2s[64:128, 0, :], rhs=zw[64:128, 0:tb],
              start=True, stop=True)
        fw.op("act", [bpa, bpv], [baf], "activation", out=af[:, 0:tb], in_=pa[:, 0:tb], func=AF.Sigmoid, bias=pv[:, 7:8])
        fw.op("pool", [baf, bad], [baf], "tensor_tensor", out=af[:, 0:tb], in0=af[:, 0:tb], in1=ad[:, 0:tb], op=ALU.add)
        fw.op("dve", [baf, bpv, bpd], [baf], "tensor_scalar", out=af[:, 0:tb], in0=af[:, 0:tb], scalar1=pv[:, 10:11],
              scalar2=pd[:, 11:12], op0=ALU.mult, op1=ALU.add)
        fw.op("pool", [bzk, baf], [bks], "tensor_tensor", out=ks[:, 0:tb], in0=zk[:, 0:tb], in1=af[:, 0:tb], op=ALU.mult)
        fw.op("dve", [bzr, bpv, bks], [bpr], "scalar_tensor_tensor", out=pr[:, 0:tb], in0=zr[:, 0:tb], scalar=pv[:, 11:12],
              in1=ks[:, 0:tb], op0=ALU.mult, op1=ALU.mult)
        pb, bpb = lora_ps()
        fw.op("pe", [bpr, bob1], [bpb], "matmul", pb[:, 0:tb], lhsT=ob1[:], rhs=pr[:, 0:tb], start=True, stop=True)
        fw.op("dve", [bpb, bzv], [bks], "tensor_tensor", out=ks[:, 0:tb], in0=pb[:, 0:tb], in1=zv[:, 0:tb], op=ALU.mult)
        fw.op("pool", [bks, byc], [byc], "tensor_tensor", out=yc[:, 0:tb], in0=yc[:, 0:tb], in1=ks[:, 0:tb], op=ALU.add)
        fw.op("act", [bzg], [bsgd], "activation", out=sgd[:, 0:tb], in_=zg[:, 0:tb], func=AF.Sigmoid)
        pg, bpg = lora_ps()
        fw.op("pe", [bsgd, bg2], [bpg], "matmul", pg[:, 0:tb], lhsT=g2s[:], rhs=sgd[:, 0:tb], start=True, stop=True)
        fw.op("dve", [bpg, byc], [bob_out], "tensor_tensor", out=ob_out[:, 0:tb], in0=pg[:, 0:tb], in1=yc[:, 0:tb], op=ALU.mult)
        fw.dma("sp", [bob_out], [], yd_o[:, t0:t0 + tb], ob_out[:, 0:tb], is_output=True)

    byf_dram = Buf("yf_dram")
    lat_blocks = [(t0, min(TBK, NLAT - t0), 0, NLAT) for t0 in range(0, NLAT, TBK)]
    ctx_blocks = [(NLAT + t0, min(TBK, NCTX - t0), NLAT, NTOT) for t0 in range(0, NCTX, TBK)]
    for d in range(2):
        S0, bS0 = ST[cnt["st"] % 2]
        fw.op("pool", [], [bS0], "memset", S0[:], 0.0)
        if d == 0:
            order = ctx_blocks + lat_blocks
        else:
            order = ctx_blocks[::-1] + lat_blocks[::-1]
        t0, tb, lo, hi = order[0]
        prep(t0, tb, lo, hi, d, 0)
        for f in pre_slices(0, tb // CH, d):
            f()
        for n, (t0, tb, lo, hi) in enumerate(order):
            par = n % 2
            sl = []
            if n + 1 < len(order):
                t0n, tbn, lon, hin = order[n + 1]
                prep(t0n, tbn, lon, hin, d, par ^ 1)
                sl = pre_slices(par ^ 1, tbn // CH, d)
            nch = tb // CH
            chs = list(range(nch)) if d == 0 else list(range(nch - 1, -1, -1))
            per = -(-len(sl) // nch) if sl else 0
            k = 0
            for c in chs:
                state(par, c, d)
                for _ in range(per):
                    if k < len(sl):
                        sl[k](); k += 1
            while k < len(sl):
                sl[k](); k += 1
            if d == 0:
                yb, byb = ybP[par]
                fw.dma("sp", [byb], [byf_dram], yf_o[:, t0:t0 + tb], yb[:, 0:tb], is_output=True)
            else:
                readout(t0, tb, par)
    fw.finish()
    return nc


def natten_patterns(rows):
    def rs(r):
        return int(np.clip(r - 4, 0, rows - 8))
    pats = {}
    per_pair = []
    nkl = rows // 2
    for r in range(0, rows, 2):
        ws = rs(r)
        kb0 = ws // 2
        nblk = min(5, nkl - kb0)
        key = (rs(r) - r, rs(r + 1) - (r + 1), nblk)
        if key not in pats:
            pats[key] = (len(pats), r)
        per_pair.append((pats[key][0], kb0, nblk))
    return pats, per_pair


def natten_bias_tables(rpb2, rows):
    pats, _ = natten_patterns(rows)
    def rs(r):
        return int(np.clip(r - 4, 0, rows - 8))
    tab = np.full((len(pats), 2, 128, 5, 128), -30000.0, np.float32)
    p = np.arange(128)
    f = np.arange(128)
    kc = (p % 64)[:, None]
    c = (f % 64)[None, :]
    csc = np.clip(c - 8, 0, 48)
    colok = (kc >= csc) & (kc < csc + 16)
    dcol = kc - c + 15
    for key, (pid, r) in pats.items():
        ws = rs(r)
        for blk in range(key[2]):
            keyrow = (ws + 2 * blk + p // 64)[:, None]
            qrow = (r + f // 64)[None, :]
            rsq = np.clip(qrow - 4, 0, rows - 8)
            ok = colok & (keyrow >= rsq) & (keyrow < rsq + 8)
            drow = keyrow - qrow + 7
            idx_r = np.clip(drow, 0, 14)
            idx_c = np.clip(dcol, 0, 30)
            for hh in range(2):
                vals = rpb2[hh][idx_r, idx_c]
                tab[pid, hh, :, blk, :] = np.where(ok, vals, np.float32(-30000.0))
    return tab


def build_kbn(NLAT, NCTX):
    NTOT = NLAT + NCTX
    NKB = NTOT // 128
    NKL = NLAT // 128
    rows = NLAT // 64
    pats, per_pair = natten_patterns(rows)
    NP = len(pats)
    PC = 2048
    nc = bass.Bass("TRN2", target_bir_lowering=False)
    q_d = nc.dram_tensor("q", [128, NTOT], BF16, kind="ExternalInput").ap()
    k_d = nc.dram_tensor("k", [128, NTOT], BF16, kind="ExternalInput").ap()
    v_d = [nc.dram_tensor("v%d" % m, [128, NKB * 65], BF16, kind="ExternalInput").ap() for m in range(2)]
    tab_d = nc.dram_tensor("tab", [NP, 2, 128, 5, 128], F32, kind="ExternalInput").ap()
    sel_d = nc.dram_tensor("sel", [65, 64], F32, kind="ExternalInput").ap()
    y_o = nc.dram_tensor("y_o", [2, 64, NTOT], BF16, kind="ExternalOutput").ap()
    fw = FW(nc)
    npc = max(1, NLAT // PC)
    pcs = [(i * PC, PC if i < npc - 1 else NTOT - i * PC) for i in range(npc)]
    Q = SBT(nc, "Q", [128, NTOT], BF16); bQ = [Buf() for _ in pcs]
    K = SBT(nc, "K", [128, NTOT], BF16); bK = [Buf() for _ in pcs]
    V = [SBT(nc, "V%d" % m, [128, NKB, 65], BF16) for m in range(2)]; bV = [[Buf() for _ in pcs] for m in range(2)]
    E = SBT(nc, "E", [128, NP * 2, 5, 128], F32); bE = Buf()
    sel = SBT(nc, "sel_s", [65, 64], F32); bsel = Buf()
    fw.dma("sp", [], [bsel], sel[:], sel_d[:, :])
    for pid in range(NP):
        for hh in range(2):
            fw.dma("sp" if hh == 0 else "act", [], [bE], E[:, pid * 2 + hh, :, :], tab_d[pid, hh])
    for pid in range(NP):
        fw.op("act", [bE], [bE], "activation", out=E[:, pid * 2:pid * 2 + 2, :, :], in_=E[:, pid * 2:pid * 2 + 2, :, :],
              func=AF.Exp)
    for pi, (c0, cw) in enumerate(pcs):
        fw.dma("sp", [], [bK[pi]], K[:, c0:c0 + cw], k_d[:, c0:c0 + cw])
        fw.dma("act", [], [bQ[pi]], Q[:, c0:c0 + cw], q_d[:, c0:c0 + cw])
        kb0 = c0 // 128; kbn = cw // 128
        for m in range(2):
            fw.dma("pool", [], [bV[m][pi]], V[m][:, kb0:kb0 + kbn, :],
                   v_d[m][:, kb0 * 65:(kb0 + kbn) * 65].rearrange("p (k d) -> p k d", d=65))

    def pc_of(col):
        return min(col // PC, npc - 1)

    pS = [PST(nc, "pS%d" % i, [128, 8, 128], F32) for i in range(2)]; bpS = [Buf(excl=True) for _ in range(2)]
    pO = [PST(nc, "pO%d" % i, [128, 512], F32) for i in range(2)]; bpO = [Buf(excl=True), Buf(excl=True)]
    pB = PST(nc, "pB", [128, 512], F32); bpB = Buf(excl=True)
    PTf = [SBT(nc, "PTf%d" % i, [128, 5, 128], F32) for i in range(2)]; bPTf = [Buf(), Buf()]
    PTb = [SBT(nc, "PTb%d" % i, [128, 7, 128], BF16) for i in range(2)]; bPTb = [Buf(), Buf()]
    osb = [SBT(nc, "osb%d" % i, [65, 512], F32) for i in range(2)]; bosb = [Buf(), Buf()]
    rden = SBT(nc, "rden", [64, 512], F32); brden = Buf()
    yo = [SBT(nc, "yo%d" % i, [64, 512], BF16) for i in range(2)]; byo = [Buf(), Buf()]
    cnt = {"s": 0, "o": 0}
    nctxb = NCTX // 128

    def epilogue(po, bpo, oi, hh, q0, qn):
        ob = osb[oi % 2]; bob = bosb[oi % 2]
        fw.op("act", [bpo], [bob], "activation", out=ob[:, 0:qn], in_=po[0:65, 0:qn], func=AF.Copy)
        fw.op("pe", [bob, bsel], [bpB], "matmul", pB[0:64, 0:qn], lhsT=sel[:], rhs=ob[:, 0:qn], start=True, stop=True)
        fw.op("dve", [bpB], [brden], "reciprocal", out=rden[:, 0:qn], in_=pB[0:64, 0:qn])
        y = yo[oi % 2]; by = byo[oi % 2]
        fw.op("dve", [bob, brden], [by], "tensor_tensor", out=y[:, 0:qn], in0=ob[0:64, 0:qn], in1=rden[:, 0:qn],
              op=ALU.mult)
        fw.dma("sp", [by], [], y_o[hh, :, q0:q0 + qn], y[:, 0:qn], is_output=True)

    for hh in range(2):
        hs = slice(hh * 64, (hh + 1) * 64)
        npairs = rows // 2
        for g0 in range(0, npairs, 4):
            oi = cnt["o"]; cnt["o"] += 1
            po = pO[oi % 2]; bpo = bpO[oi % 2]
            gn = min(4, npairs - g0)
            for sl in range(gn):
                pr = g0 + sl
                pid, kb0, nblk = per_pair[pr]
                q0 = pr * 128
                si = cnt["s"]; cnt["s"] += 1
                ps = pS[si % 2]; bps = bpS[si % 2]
                ptf = PTf[si % 2]; bptf = bPTf[si % 2]
                ptb = PTb[si % 2]; bptb = bPTb[si % 2]
                bq = bQ[pc_of(q0)]
                for blk in range(nblk):
                    kb = kb0 + blk
                    fw.op("pe", [bK[pc_of(kb * 128)], bq], [bps], "matmul", ps[:, blk, :],
                          lhsT=K[hs, kb * 128:(kb + 1) * 128], rhs=Q[hs, q0:q0 + 128], start=True, stop=True)
                for i in range(nctxb):
                    kb = NKL + i
                    fw.op("pe", [bK[pc_of(kb * 128)], bq], [bps], "matmul", ps[:, 5 + i, :],
                          lhsT=K[hs, kb * 128:(kb + 1) * 128], rhs=Q[hs, q0:q0 + 128], start=True, stop=True)
                fw.op("act", [bps], [bptf], "activation", out=ptf[:, 0:nblk, :], in_=ps[:, 0:nblk, :], func=AF.Exp,
                      scale=SCALE)
                fw.op("act", [bps], [bptb], "activation", out=ptb[:, 5:5 + nctxb, :], in_=ps[:, 5:5 + nctxb, :],
                      func=AF.Exp, scale=SCALE)
                fw.op("dve", [bptf, bE], [bptb], "tensor_tensor", out=ptb[:, 0:nblk, :], in0=ptf[:, 0:nblk, :],
                      in1=E[:, pid * 2 + hh, 0:nblk, :], op=ALU.mult)
                nmm = nblk + nctxb
                n = 0
                for blk in range(nblk):
                    kb = kb0 + blk
                    fw.op("pe", [bptb, bV[hh][pc_of(kb * 128)]], [bpo], "matmul", po[0:65, sl * 128:(sl + 1) * 128],
                          lhsT=V[hh][:, kb, 0:65], rhs=ptb[:, blk, :], start=(n == 0), stop=(n == nmm - 1))
                    n += 1
                for i in range(nctxb):
                    kb = NKL + i
                    fw.op("pe", [bptb, bV[hh][pc_of(kb * 128)]], [bpo], "matmul", po[0:65, sl * 128:(sl + 1) * 128],
                          lhsT=V[hh][:, kb, 0:65], rhs=ptb[:, 5 + i, :], start=(n == 0), stop=(n == nmm - 1))
                    n += 1
            epilogue(po, bpo, oi, hh, g0 * 128, gn * 128)
        oi = cnt["o"]; cnt["o"] += 1
        po = pO[oi % 2]; bpo = bpO[oi % 2]
        si = cnt["s"]; cnt["s"] += 1
        ps = pS[si % 2]; bps = bpS[si % 2]
        ptb = PTb[si % 2]; bptb = bPTb[si % 2]
        for i in range(nctxb):
            kb = NKL + i
            for qi in range(nctxb):
                fw.op("pe", [bK[pc_of(kb * 128)], bQ[pc_of(NLAT)]], [bps], "matmul", ps[:, i * nctxb + qi, :],
                      lhsT=K[hs, kb * 128:(kb + 1) * 128], rhs=Q[hs, NLAT + qi * 128:NLAT + (qi + 1) * 128],
                      start=True, stop=True)
        fw.op("act", [bps], [bptb], "activation", out=ptb[:, 0:nctxb * nctxb, :], in_=ps[:, 0:nctxb * nctxb, :],
              func=AF.Exp, scale=SCALE)
        for qi in range(nctxb):
            for i in range(nctxb):
                kb = NKL + i
                fw.op("pe", [bptb, bV[hh][pc_of(kb * 128)]], [bpo], "matmul", po[0:65, qi * 128:(qi + 1) * 128],
                      lhsT=V[hh][:, kb, 0:65], rhs=ptb[:, i * nctxb + qi, :], start=(i == 0), stop=(i == nctxb - 1))
        epilogue(po, bpo, oi, hh, NLAT, NCTX)
    fw.finish()
    return nc


def build_k0():
    NCOL = 3072
    nc = bass.Bass("TRN2", target_bir_lowering=False)
    cT_d = nc.dram_tensor("cT", [128, 8, 3], F32, kind="ExternalInput").ap()
    wm_d = nc.dram_tensor("wm", [D, NCOL], F32, kind="ExternalInput").ap()
    bm_d = nc.dram_tensor("bm", [3, NCOL], F32, kind="ExternalInput").ap()
    mo_d = nc.dram_tensor("mo", [3, NCOL], F32, kind="ExternalOutput").ap()
    fw = FW(nc)
    cT = SBT(nc, "cT_s", [128, 8, 3], F32); bc = Buf()
    sT = SBT(nc, "sT_s", [128, 8, 3], F32); bs = Buf()
    wm = SBT(nc, "wm_s", [128, 8, NCOL], F32); bw = [Buf() for _ in range(8)]
    bm = SBT(nc, "bm_s", [3, NCOL], F32); bb = Buf()
    mo = SBT(nc, "mo_s", [3, NCOL], F32); bmo = Buf()
    ps = [PST(nc, "ps%d" % i, [128, 512], F32) for i in range(2)]; bps = [Buf(excl=True), Buf(excl=True)]
    fw.dma("sp", [], [bc], cT[:], cT_d[:, :, :])
    fw.dma("sp", [], [bb], bm[:], bm_d[:, :])
    for k in range(8):
        fw.dma("sp" if k % 2 == 0 else "act", [], [bw[k]], wm[:, k, :], wm_d[k * 128:(k + 1) * 128, :])
    fw.op("act", [bc], [bs], "activation", out=sT[:], in_=cT[:], func=AF.Silu)
    for cb in range(NCOL // 512):
        p = ps[cb % 2]; bp = bps[cb % 2]
        for k in range(8):
            fw.op("pe", [bs, bw[k]], [bp], "matmul", p[0:3, :], lhsT=sT[:, k, :], rhs=wm[:, k, cb * 512:(cb + 1) * 512],
                  start=(k == 0), stop=(k == 7))
        fw.op("dve", [bp, bb], [bmo], "tensor_tensor", out=mo[:, cb * 512:(cb + 1) * 512], in0=p[0:3, :],
              in1=bm[:, cb * 512:(cb + 1) * 512], op=ALU.add)
    fw.dma("sp", [bmo], [], mo_d[:, :], mo[:], is_output=True)
    fw.finish()
    return nc


_NC_CACHE = {}


def _get_nc(key, builder):
    if key not in _NC_CACHE:
        _NC_CACHE[key] = builder()
    return _NC_CACHE[key]


def _run(key, builder, in_maps):
    nc = _get_nc(key, builder)
    res = run_bass_kernel_spmd(nc, in_maps, core_ids=list(range(NCORES)))
    return res.results


def _fmcol(v):
    return np.asarray(v, np.float32).reshape(8, 128).T


def _c(a):
    return np.ascontiguousarray(a)


def kernel(x, c, ctx, c_ctx, w_mod, b_mod, norm_mix, norm_ffn, w_in_even, w_out_even, q_norm_a, k_norm_a,
           sink_b, w_in_odd, w_out_odd, rpb_c, shift_mu, decay_w0, decay_w2, iclr_a0, iclr_a2, gate_g2,
           k_k, k_a, r_k, ln_x_w, ln_x_b, w_ffn_in, w_ffn_out, norm_out):
    f32 = lambda a: np.asarray(a, np.float32)
    x = f32(x); c = f32(c); ctx = f32(ctx); c_ctx = f32(c_ctx)
    w_mod = f32(w_mod); b_mod = f32(b_mod)
    B = 2
    NQ = 4
    TL = SEQ // NQ
    NTC = TL + CTX
    NTOT = SEQ + CTX
    cvec = np.stack([c[0], c[1], c_ctx], 1)
    cT = _c(cvec.reshape(8, 128, 3).transpose(1, 0, 2))
    ims = []
    for i in range(NCORES):
        l, hf = i // 2, i % 2
        ims.append({"cT": cT, "wm": _c(w_mod[l][:, hf * 3072:(hf + 1) * 3072]),
                    "bm": _c(np.broadcast_to(b_mod[l][hf * 3072:(hf + 1) * 3072], (3, 3072)))})
    r0 = _run("k0", build_k0, ims)
    mods = np.zeros((DEPTH, 3, 6 * D), np.float32)
    for i in range(NCORES):
        l, hf = i // 2, i % 2
        mods[l][:, hf * 3072:(hf + 1) * 3072] = r0[i]["mo"]

    def mv(l, g, which):
        return mods[l, g, which * D:(which + 1) * D]

    xs = [[_c(x[b, q * TL:(q + 1) * TL]) for q in range(NQ)] for b in range(B)]
    cx = [_c(ctx[b]) for b in range(B)]
    rm, ob = rope_consts()
    masks = band_masks()
    sel = sel_const()
    mk, ob1, sm, id2 = rwkv_consts()
    out = None
    for l in range(DEPTH):
        even = (l % 2 == 0)
        li = l // 2
        ims = []
        for b in range(B):
            for q in range(NQ):
                fmv = _c(np.concatenate([_fmcol(norm_mix[l]), _fmcol(mv(l, b, 1)), _fmcol(mv(l, b, 0)),
                                         _fmcol(mv(l, 2, 1)), _fmcol(mv(l, 2, 0))], 1))
                d = {"x": _c(np.concatenate([xs[b][q], cx[b]], 0)), "fmv": fmv}
                if even:
                    d["w_in"] = f32(w_in_even[li])
                    pos = np.concatenate([np.arange(TL) + q * TL, np.zeros(CTX, np.int64)])
                    isc = np.arange(NTC) >= TL
                    d["cs"] = rope_tables(pos, isc)
                    d["gv"] = _c(np.stack([np.tile(f32(q_norm_a[li]), 2), np.tile(f32(k_norm_a[li]), 2)], 1))
                    d["rm"] = rm
                    d["ob"] = ob
                else:
                    d["w_in"] = f32(w_in_odd[li])
                ims.append(d)
        ra = _run("ka_even" if even else "ka_odd", lambda: build_ka(TL, CTX, even), ims)

        def gather_fm(name, b):
            parts = [ra[b * NQ + q][name][:, :, :TL] for q in range(NQ)] + [ra[b * NQ][name][:, :, TL:]]
            return np.concatenate(parts, axis=2)

        def gather_tm(name, b):
            parts = [ra[b * NQ + q][name][:TL] for q in range(NQ)] + [ra[b * NQ][name][TL:]]
            return np.concatenate(parts, axis=0)

        yT = [np.zeros((D, NTOT), NPBF) for _ in range(B)]
        if even:
            ims = []
            for b in range(B):
                qk = gather_fm("qk_o", b)
                v = gather_tm("v_o", b)
                for j in range(4):
                    kvh = j // 2
                    ka = qk[4][kvh * 64:(kvh + 1) * 64]
                    kb = qk[9][kvh * 64:(kvh + 1) * 64]
                    ims.append({"q0": _c(qk[j]), "k0": _c(np.concatenate([ka, ka], 0)),
                                "v0": vaug_layout(_c(v[:, kvh * 64:(kvh + 1) * 64])),
                                "q1": _c(qk[5 + j]), "k1": _c(np.concatenate([kb, kb], 0)),
                                "v1": vaug_layout(_c(v[:, 128 + kvh * 64:128 + (kvh + 1) * 64])),
                                "mask": masks, "sel": sel,
                                "sink": _c(np.broadcast_to(f32(sink_b[li])[2 * j:2 * j + 2], (64, 2)))})
            rb = _run("kbe", lambda: build_kbe(SEQ, CTX), ims)
            for b in range(B):
                for j in range(4):
                    yo = rb[b * 4 + j]["y_o"]
                    for hh in range(2):
                        h = 2 * j + hh
                        yT[b][h * 64:(h + 1) * 64] = yo[hh]
                        yT[b][512 + h * 64:512 + (h + 1) * 64] = yo[2 + hh]
        else:
            imn = []
            imr = []
            hcs = [slice(j * 128, (j + 1) * 128) for j in range(4)]
            mu = f32(shift_mu[li])
            for b in range(B):
                qk = gather_fm("qk_o", b)
                v = gather_tm("v_o", b)
                zd = gather_fm("zd_o", b)
                for j in range(4):
                    hc = hcs[j]
                    imn.append({"q": _c(qk[j]), "k": _c(qk[4 + j]),
                                "v0": vaug_layout(_c(v[:, (2 * j) * 64:(2 * j + 1) * 64])),
                                "v1": vaug_layout(_c(v[:, (2 * j + 1) * 64:(2 * j + 2) * 64])),
                                "tab": natten_bias_tables(f32(rpb_c[li])[2 * j:2 * j + 2], SEQ // 64), "sel": sel})
                    pv = np.stack([mu[j * 128:(j + 1) * 128], mu[512 + j * 128:512 + (j + 1) * 128],
                                   mu[1024 + j * 128:1024 + (j + 1) * 128], mu[1536:1664], mu[1664:1792],
                                   f32(decay_w0[li])[0][hc], f32(decay_w0[li])[1][hc], f32(iclr_a0[li])[0][hc],
                                   f32(iclr_a0[li])[1][hc], f32(k_k[li])[hc], f32(k_a[li])[hc],
                                   f32(r_k[li]).reshape(-1)[hc], f32(ln_x_w[li])[hc], f32(ln_x_b[li])[hc]], 1)
                    imr.append({"z": _c(np.stack([zd[j], zd[4 + j], zd[8 + j], zd[12], zd[13]])),
                                "pv": _c(pv.astype(np.float32)), "w2": _c(f32(decay_w2[li])[:, :, hc]),
                                "a2": _c(f32(iclr_a2[li])[:, :, hc]), "g2": _c(f32(gate_g2[li])[:, hc]),
                                "mk": mk, "ob1": ob1, "sm": sm, "id2": id2})
            rn = _run("kbn", lambda: build_kbn(SEQ, CTX), imn)
            rr = _run("kbr", lambda: build_kbr2(SEQ, CTX), imr)
            for b in range(B):
                for j in range(4):
                    yo = rn[b * 4 + j]["y_o"]
                    for hh in range(2):
                        h = 2 * j + hh
                        yT[b][h * 64:(h + 1) * 64] = yo[hh]
                    yT[b][512 + j * 128:512 + (j + 1) * 128] = rr[b * 4 + j]["yd_o"]
        final = (l == DEPTH - 1)
        w_o = f32(w_out_even[li]) if even else f32(w_out_odd[li])
        ims = []
        for b in range(B):
            for q in range(NQ):
                grep = np.zeros((2, 2, 128, D), np.float32)
                grep[0, 0] = mv(l, b, 2)[None]
                grep[0, 1] = mv(l, b, 5)[None]
                grep[1, 0] = mv(l, 2, 2)[None]
                grep[1, 1] = mv(l, 2, 5)[None]
                fmv = _c(np.concatenate([_fmcol(norm_ffn[l]), _fmcol(mv(l, b, 4)), _fmcol(mv(l, b, 3)),
                                         _fmcol(mv(l, 2, 4)), _fmcol(mv(l, 2, 3))], 1))
                yTc = _c(np.concatenate([yT[b][:, q * TL:(q + 1) * TL], yT[b][:, SEQ:]], 1))
                ims.append({"x": _c(np.concatenate([xs[b][q], cx[b]], 0)), "yT": yTc, "w_out": w_o,
                            "w_fi": f32(w_ffn_in[l]), "w_fo": f32(w_ffn_out[l]), "grep": grep, "fmv": fmv,
                            "norep": _c(np.broadcast_to(f32(norm_out), (128, D)))})
        rc = _run("kc_final" if final else "kc", lambda: build_kc(TL, CTX, final), ims)
        for b in range(B):
            for q in range(NQ):
                xo = rc[b * NQ + q]["xo"]
                xs[b][q] = _c(xo[:TL])
            cx[b] = _c(rc[b * NQ]["xo"][TL:])
    out = np.stack([np.concatenate(xs[b], 0) for b in range(B)], 0).astype(np.float32)
    return out
```

```python
import numpy as np
import ml_dtypes
import concourse.bass as bass
import concourse.mybir as mybir
from concourse.bass_utils import run_bass_kernel_spmd

F32 = mybir.dt.float32
BF16 = mybir.dt.bfloat16
AF = mybir.ActivationFunctionType
ALU = mybir.AluOpType
AX = mybir.AxisListType
NPBF = ml_dtypes.bfloat16

D = 1024
SEQ = 16384
CTX = 256
DEPTH = 4
HID = 2816
NCORES = 8
RMS_EPS = 1e-6


class Buf:
    __slots__ = ("w", "r", "name", "excl")

    def __init__(self, name="", excl=False):
        self.w = None
        self.r = {}
        self.name = name
        self.excl = excl


class FW:
    NDMA = 24

    def __init__(self, nc):
        self.nc = nc
        self.engs = {"pe": nc.tensor, "act": nc.scalar, "dve": nc.vector, "pool": nc.gpsimd, "sp": nc.sync}
        self.esem = {}
        self.ecnt = {}
        for e in ("pe", "act", "dve", "pool"):
            self.esem[e] = nc.alloc_semaphore("es_" + e)
            self.ecnt[e] = 0
        self.waited = {e: {} for e in self.engs}
        self.dsems = [nc.alloc_semaphore("ds%d" % i) for i in range(self.NDMA)]
        self.dcnt = [0] * self.NDMA
        self.dnext = 0
        self.semid = {}
        self.out_toks = []
        self.nwaits = 0
        self.prog = {e: [] for e in self.engs}

    def _sid(self, sem):
        return id(sem)

    def _wait(self, e, tok):
        sem, val = tok
        k = self._sid(sem)
        if self.waited[e].get(k, 0) >= val:
            return
        self.prog[e].append(("w", sem, val))
        if e == "pe" and sem is not self.esem["pe"]:
            self.prog[e].append(("w", sem, val))
        self.nwaits += 1
        self.waited[e][k] = val

    def _deps(self, e, reads, writes):
        own = self.esem.get(e)
        for t in reads:
            if t.w is not None:
                if not (e == "pe" and t.w[0] is own):
                    self._wait(e, t.w)
            if t.excl:
                for k, tok in t.r.items():
                    if tok[0] is not own:
                        self._wait(e, tok)
        for t in writes:
            if t.w is not None:
                if not (e == "pe" and t.w[0] is own):
                    self._wait(e, t.w)
            for k, tok in t.r.items():
                if e == "pe" and tok[0] is own:
                    continue
                self._wait(e, tok)

    def _post(self, tok, reads, writes):
        k = self._sid(tok[0])
        for t in reads:
            o = t.r.get(k)
            if o is None or o[1] < tok[1]:
                t.r[k] = tok
        for t in writes:
            t.w = tok
            t.r = {}

    def fence(self, e):
        if self.ecnt[e] > 0:
            self._wait(e, (self.esem[e], self.ecnt[e]))

    def op(self, e, reads, writes, meth, *args, **kw):
        self._deps(e, reads, writes)
        self.ecnt[e] += 1
        self.prog[e].append(("i", (lambda en, m=meth, a=args, k=kw: getattr(en, m)(*a, **k)), self.esem[e], 1))
        self._post((self.esem[e], self.ecnt[e]), reads, writes)

    def dma(self, e, reads, writes, out, in_, is_output=False):
        j = self.dnext
        self.dnext = (j + 1) % self.NDMA
        if self.dcnt[j] > 0:
            self._wait(e, (self.dsems[j], self.dcnt[j]))
        self._deps(e, reads, writes)
        self.prog[e].append(("i", (lambda en, o=out, i=in_: en.dma_start(out=o, in_=i)), self.dsems[j], 16))
        self.dcnt[j] += 16
        tok = (self.dsems[j], self.dcnt[j])
        self._post(tok, reads, writes)
        if is_output:
            self.out_toks.append(tok)

    def dma_fn(self, e, reads, writes, fn):
        j = self.dnext
        self.dnext = (j + 1) % self.NDMA
        if self.dcnt[j] > 0:
            self._wait(e, (self.dsems[j], self.dcnt[j]))
        self._deps(e, reads, writes)
        self.prog[e].append(("i", fn, self.dsems[j], 16))
        self.dcnt[j] += 16
        tok = (self.dsems[j], self.dcnt[j])
        self._post(tok, reads, writes)

    def raw(self, e, fn):
        self.prog[e].append(("r", fn))

    def cc(self, kind, reads, writes, ins, outs, replica_groups):
        e = "pool"
        j = self.dnext
        self.dnext = (j + 1) % self.NDMA
        if self.dcnt[j] > 0:
            self._wait(e, (self.dsems[j], self.dcnt[j]))
        self._deps(e, reads, writes)
        self.prog[e].append(("i", (lambda en, k=kind, i=ins, o=outs, rg=replica_groups: en.collective_compute(
            k, ALU.bypass, replica_groups=rg, ins=i, outs=o)), self.dsems[j], 16))
        self.dcnt[j] += 16
        tok = (self.dsems[j], self.dcnt[j])
        self._post(tok, reads, writes)

    def barrier(self):
        toks = [(self.dsems[j], self.dcnt[j]) for j in range(self.NDMA) if self.dcnt[j] > 0]
        toks += [(self.esem[e], self.ecnt[e]) for e in ("pe", "act", "dve", "pool") if self.ecnt[e] > 0]
        for e in self.engs:
            for tok in toks:
                if tok[0] is self.esem.get(e):
                    continue
                self._wait(e, tok)

    def finish(self):
        for tok in self.out_toks:
            self._wait("sp", tok)
        for j in range(self.NDMA):
            if self.dcnt[j] > 0:
                self._wait("sp", (self.dsems[j], self.dcnt[j]))
        for e in ("pe", "act", "dve", "pool"):
            if self.ecnt[e] > 0:
                self._wait("sp", (self.esem[e], self.ecnt[e]))
        prog = self.prog

        def replay(eng, items):
            for it in items:
                if it[0] == "w":
                    eng.wait_ge(it[1], it[2])
                elif it[0] == "r":
                    it[1](eng)
                else:
                    it[1](eng).then_inc(it[2], it[3])

        with self.nc.Block() as block:
            @block.sync
            def _(eng):
                replay(eng, prog["sp"])

            @block.tensor
            def _(eng):
                replay(eng, prog["pe"])

            @block.scalar
            def _(eng):
                replay(eng, prog["act"])

            @block.vector
            def _(eng):
                replay(eng, prog["dve"])

            @block.gpsimd
            def _(eng):
                replay(eng, prog["pool"])


def SBT(nc, name, shape, dt):
    return nc.alloc_sbuf_tensor(name, shape, dt)


def PST(nc, name, shape, dt):
    return nc.alloc_psum_tensor(name, shape, dt)


def load_w_bf16(fw, nc, w_ap, dst, bdst, K, N, stg, bstg, eng_cast="pool", col0=0, ncols=None, dcol0=0):
    if ncols is None:
        ncols = N
    SW = stg[0].shape[1]
    q = 0
    for k in range(K // 128):
        for c0 in range(0, ncols, SW):
            cw = min(SW, ncols - c0)
            s = q % len(stg)
            fw.dma("sp" if q % 2 == 0 else "act", [], [bstg[s]], stg[s][:, 0:cw],
                   w_ap[k * 128:(k + 1) * 128, col0 + c0:col0 + c0 + cw])
            fw.op(eng_cast, [bstg[s]], [bdst],
                  "tensor_copy", out=dst[:, k, dcol0 + c0:dcol0 + c0 + cw],
                                                                 in_=stg[s][:, 0:cw])
            q += 1


def emit_rstd(fw, ss, bss, rstd, brstd, n):
    fw.op("dve", [bss], [brstd], "tensor_scalar", out=rstd, in0=ss, scalar1=1.0 / n, scalar2=RMS_EPS,
                                                           op0=ALU.mult, op1=ALU.add)
    fw.op("act", [brstd], [brstd], "activation", out=rstd, in_=rstd, func=AF.Sqrt)
    fw.op("dve", [brstd], [brstd], "reciprocal", out=rstd, in_=rstd)


def make_ident(fw, nc, name="ident"):
    ident = SBT(nc, name, [128, 128], F32)
    b = Buf()
    fw.op("pool", [], [b], "memset", ident[:], 1.0)
    fw.op("pool", [b], [b], "affine_select", out=ident[:], in_=ident[:], pattern=[[-1, 128]],
                                                      compare_op=ALU.is_equal, fill=0.0, base=0,
                                                      channel_multiplier=1)
    return ident, b


def build_kc(NT_LAT, NT_CTX, final):
    NT = NT_LAT + NT_CTX
    TB = 256
    nc = bass.Bass("TRN2", target_bir_lowering=False)
    x = nc.dram_tensor("x", [NT, D], F32, kind="ExternalInput").ap()
    yT = nc.dram_tensor("yT", [D, NT], BF16, kind="ExternalInput").ap()
    w_out = nc.dram_tensor("w_out", [D, D], F32, kind="ExternalInput").ap()
    w_fi = nc.dram_tensor("w_fi", [D, 2 * HID], F32, kind="ExternalInput").ap()
    w_fo = nc.dram_tensor("w_fo", [HID, D], F32, kind="ExternalInput").ap()
    grep = nc.dram_tensor("grep", [2, 2, 128, D], F32, kind="ExternalInput").ap()
    fmv = nc.dram_tensor("fmv", [128, 40], F32, kind="ExternalInput").ap()
    norep = nc.dram_tensor("norep", [128, D], F32, kind="ExternalInput").ap()
    xo = nc.dram_tensor("xo", [NT, D], F32, kind="ExternalOutput").ap()
    fw = FW(nc)
    NJ = HID // 128
    wout = SBT(nc, "wout", [128, 8, D], BF16); bwout = Buf()
    wfi = SBT(nc, "wfi", [128, 8, 2 * HID], BF16); bwfi = Buf()
    wfo = SBT(nc, "wfo", [128, NJ, D], BF16); bwfo = Buf()
    stg = [SBT(nc, "stg%d" % i, [128, 1024], F32) for i in range(2)]; bstg = [Buf(), Buf()]
    g1 = SBT(nc, "g1", [128, D], F32); bg1 = Buf()
    g2 = SBT(nc, "g2", [128, D], F32); bg2 = Buf()
    fm = SBT(nc, "fm", [128, 40], F32); bfm = Buf()
    gs = SBT(nc, "gs", [128, 16], F32); bgs = Buf()
    yTs = SBT(nc, "yTs", [128, 8, TB], BF16); byT = Buf()
    xin = SBT(nc, "xin", [128, D], F32); bxin = Buf()
    x1 = [SBT(nc, "x1_%d" % i, [128, D], F32) for i in range(TB // 128)]; bx1 = [Buf() for _ in x1]
    xn = SBT(nc, "xn", [128, D], F32); bxn = Buf()
    hT = SBT(nc, "hT", [128, 8, TB], BF16); bhT = Buf()
    aT = SBT(nc, "aT", [128, NJ, TB], BF16); baT = Buf()
    gsb = [SBT(nc, "gsb%d" % i, [128, TB], F32) for i in range(2)]; bgsb = [Buf(), Buf()]
    ss = SBT(nc, "ss", [128, 1], F32); bss = Buf()
    rstd = SBT(nc, "rstd", [128, 1], F32); brstd = Buf()
    pmm = [PST(nc, "pmm%d" % i, [128, D], F32) for i in range(2)]; bpmm = [Buf(), Buf()]
    ptr = PST(nc, "ptr", [128, 8, 128], F32); bptr = Buf()
    pgu = [PST(nc, "pgu%d" % i, [128, 2, TB], F32) for i in range(2)]; bpgu = [Buf() for _ in range(2)]
    ident, bid = make_ident(fw, nc)

    fw.dma("sp", [], [bfm], fm[:], fmv[:, :])
    load_w_bf16(fw, nc, w_out, wout, bwout, D, D, stg, bstg)
    load_w_bf16(fw, nc, w_fi, wfi, bwfi, D, 2 * HID, stg, bstg)
    load_w_bf16(fw, nc, w_fo, wfo, bwfo, HID, D, stg, bstg)
    if final:
        nrep = SBT(nc, "nrep", [128, D], F32); bnrep = Buf()
        fw.dma("sp", [], [bnrep], nrep[:], norep[:, :])

    blocks = [(t0, min(TB, NT_LAT - t0), 0) for t0 in range(0, NT_LAT, TB)]
    blocks += [(NT_LAT + t0, min(TB, NT_CTX - t0), 1) for t0 in range(0, NT_CTX, TB)]
    cur_g = -1
    pi = 0
    for (t0, tb, g) in blocks:
        if g != cur_g:
            cur_g = g
            fw.dma("sp", [], [bg1], g1[:], grep[g, 0])
            fw.dma("sp", [], [bg2], g2[:], grep[g, 1])
            sc = fm[:, 8 + 16 * g:16 + 16 * g]
            fw.op("dve", [bfm], [bgs], "scalar_tensor_tensor", out=gs[:, 0:8], in0=sc, scalar=1.0, in1=fm[:, 0:8], op0=ALU.add, op1=ALU.mult)
        shc = 16 + 16 * g
        nt = tb // 128
        fw.dma("sp", [], [byT], yTs[:, :, 0:tb], yT.rearrange("(c p) t -> p c t", p=128)[:, :, t0:t0 + tb])
        for i in range(nt):
            fw.dma("act", [], [bxin], xin[:], x[t0 + i * 128:t0 + (i + 1) * 128, :])
            po = pmm[pi % 2]; bpo = bpmm[pi % 2]; pi += 1
            for cb in range(2):
                for k in range(8):
                    fw.op("pe", [byT, bwout], [bpo], "matmul", po[:, cb * 512:(cb + 1) * 512], lhsT=yTs[:, k, i * 128:(i + 1) * 128],
                        rhs=wout[:, k, cb * 512:(cb + 1) * 512], start=(k == 0), stop=(k == 7))
            fw.op("dve", [bpo, bg1], [bx1[i]], "tensor_tensor", out=x1[i][:], in0=po[:], in1=g1[:], op=ALU.mult)
            fw.op("dve", [bx1[i], bxin], [bx1[i]], "tensor_tensor", out=x1[i][:], in0=x1[i][:], in1=xin[:], op=ALU.add)
            fw.op("act", [bx1[i]], [bxn, bss], "activation", out=xn[:], in_=x1[i][:], func=AF.Square, accum_out=ss[:])
            emit_rstd(fw, ss[:], bss, rstd[:], brstd, D)
            fw.op("act", [bx1[i], brstd], [bxn], "activation", out=xn[:], in_=x1[i][:], func=AF.Copy, scale=rstd[:, 0:1])
            for c in range(8):
                fw.op("pe", [bxn, bid], [bptr], "transpose", out=ptr[:, c, :], in_=xn[:, c * 128:(c + 1) * 128], identity=ident[:])
            for c in range(8):
                if c % 2 == 0:
                    fw.op("dve", [bptr, bgs, bfm], [bhT], "tensor_scalar", out=hT[:, c, i * 128:(i + 1) * 128], in0=ptr[:, c, :], scalar1=gs[:, c:c + 1],
                        scalar2=fm[:, shc + c:shc + c + 1], op0=ALU.mult, op1=ALU.add)
                else:
                    fw.op("act", [bptr, bgs, bfm], [bhT], "activation", out=hT[:, c, i * 128:(i + 1) * 128], in_=ptr[:, c, :], func=AF.Identity,
                        scale=gs[:, c:c + 1], bias=fm[:, shc + c:shc + c + 1])
        for j in range(NJ):
            pg = pgu[j % 2][:, 0, :]; bpg = bpgu[j % 2]
            pu = pgu[j % 2][:, 1, :]; bpu = bpgu[j % 2]
            for k in range(8):
                fw.op("pe", [bhT, bwfi], [bpg], "matmul", pg[:, 0:tb], lhsT=wfi[:, k, j * 128:(j + 1) * 128], rhs=hT[:, k, 0:tb],
                    start=(k == 0), stop=(k == 7))
            for k in range(8):
                fw.op("pe", [bhT, bwfi], [bpu], "matmul", pu[:, 0:tb], lhsT=wfi[:, k, HID + j * 128:HID + (j + 1) * 128], rhs=hT[:, k, 0:tb],
                    start=(k == 0), stop=(k == 7))
            gb = gsb[j % 2]; bgb = bgsb[j % 2]
            fw.op("act", [bpg], [bgb], "activation", out=gb[:, 0:tb], in_=pg[:, 0:tb], func=AF.Silu)
            fw.op("dve", [bgb, bpu], [baT], "tensor_tensor", out=aT[:, j, 0:tb], in0=pu[:, 0:tb], in1=gb[:, 0:tb], op=ALU.mult)
        for i in range(nt):
            pf = pmm[pi % 2]; bpf = bpmm[pi % 2]; pi += 1
            for cb in range(2):
                for j in range(NJ):
                    fw.op("pe", [baT, bwfo], [bpf], "matmul", pf[:, cb * 512:(cb + 1) * 512], lhsT=aT[:, j, i * 128:(i + 1) * 128],
                        rhs=wfo[:, j, cb * 512:(cb + 1) * 512], start=(j == 0), stop=(j == NJ - 1))
            fw.op("dve", [bpf, bg2], [bxn], "tensor_tensor", out=xn[:], in0=pf[:], in1=g2[:], op=ALU.mult)
            fw.op("dve", [bxn, bx1[i]], [bxn], "tensor_tensor", out=xn[:], in0=xn[:], in1=x1[i][:], op=ALU.add)
            if final:
                fw.op("act", [bxn], [bx1[i], bss], "activation", out=x1[i][:], in_=xn[:], func=AF.Square, accum_out=ss[:])
                emit_rstd(fw, ss[:], bss, rstd[:], brstd, D)
                fw.op("dve", [bxn, brstd, bnrep], [bxn], "scalar_tensor_tensor", out=xn[:], in0=xn[:], scalar=rstd[:, 0:1], in1=nrep[:], op0=ALU.mult, op1=ALU.mult)
            fw.dma("sp", [bxn], [], xo[t0 + i * 128:t0 + (i + 1) * 128, :], xn[:], is_output=True)
    fw.finish()
    return nc


EV_FM = [(0, "a"), (128, "a"), (256, "a"), (384, "a"), (512, "ak"), (768, "b"), (896, "b"), (1024, "b"),
         (1152, "b"), (1280, "b")]
EV_TM = [(640, 128), (1408, 128)]
OD_FM_BF = [(c * 128, "c") for c in range(8)]
OD_FM_F32 = [(1536 + c * 128, "z") for c in range(14)]
OD_TM = [(1024, 512)]


def build_ka(NT_LAT, NT_CTX, even):
    NT = NT_LAT + NT_CTX
    TB = 512
    NIN = 1536 if even else 3328
    nc = bass.Bass("TRN2", target_bir_lowering=False)
    x = nc.dram_tensor("x", [NT, D], F32, kind="ExternalInput").ap()
    w_in = nc.dram_tensor("w_in", [D, NIN], F32, kind="ExternalInput").ap()
    fmv = nc.dram_tensor("fmv", [128, 40], F32, kind="ExternalInput").ap()
    fw = FW(nc)
    if even:
        fm_list = EV_FM
        tm_list = EV_TM
        gv_d = nc.dram_tensor("gv", [128, 2], F32, kind="ExternalInput").ap()
        rm_d = nc.dram_tensor("rm", [128, 128], F32, kind="ExternalInput").ap()
        ob_d = nc.dram_tensor("ob", [128, 128], F32, kind="ExternalInput").ap()
        cs_d = nc.dram_tensor("cs", [2, 128, NT], F32, kind="ExternalInput").ap()
        qk_o = nc.dram_tensor("qk_o", [len(fm_list), 128, NT], BF16, kind="ExternalOutput").ap()
        NTM = 256
    else:
        fm_list = OD_FM_BF + OD_FM_F32
        tm_list = OD_TM
        qk_o = nc.dram_tensor("qk_o", [8, 128, NT], BF16, kind="ExternalOutput").ap()
        zd_o = nc.dram_tensor("zd_o", [14, 128, NT], F32, kind="ExternalOutput").ap()
        NTM = 512
    v_o = nc.dram_tensor("v_o", [NT, NTM], BF16, kind="ExternalOutput").ap()

    win = SBT(nc, "win", [128, 8, NIN], BF16); bwin = Buf()
    stg = [SBT(nc, "stg%d" % i, [128, 1024], F32) for i in range(2)]; bstg = [Buf(), Buf()]
    fm = SBT(nc, "fm", [128, 40], F32); bfm = Buf()
    gs = SBT(nc, "gs", [128, 8], F32); bgs = Buf()
    xin = [SBT(nc, "xin%d" % i, [128, D], F32) for i in range(2)]; bxin = [Buf(), Buf()]
    xn = SBT(nc, "xn", [128, D], F32); bxn = Buf()
    hT = SBT(nc, "hT", [128, 8, TB], BF16); bhT = Buf()
    ss = SBT(nc, "ss", [128, 1], F32); bss = Buf()
    rstd = SBT(nc, "rstd", [128, 1], F32); brstd = Buf()
    ptr = PST(nc, "ptr", [128, 8, 128], F32); bptr = Buf()
    pfm = [PST(nc, "pfm%d" % i, [128, TB], F32) for i in range(2)]; bpfm = [Buf(), Buf()]
    paux = [PST(nc, "paux%d" % i, [128, TB], F32) for i in range(2)]; bpaux = [Buf(), Buf()]
    ptm = [PST(nc, "ptm%d" % i, [128, 512], F32) for i in range(2)]; bptm = [Buf(), Buf()]
    ofm = [SBT(nc, "ofm%d" % i, [128, TB], F32) for i in range(3)]; bofm = [Buf() for _ in range(3)]
    ofb = [SBT(nc, "ofb%d" % i, [128, TB], BF16) for i in range(3)]; bofb = [Buf() for _ in range(3)]
    otm = [SBT(nc, "otm%d" % i, [128, NTM], BF16) for i in range(2)]; botm = [Buf(), Buf()]
    ident, bid = make_ident(fw, nc)
    fw.dma("sp", [], [bfm], fm[:], fmv[:, :])
    if even:
        gv = SBT(nc, "gv_s", [128, 2], F32); bgv = Buf()
        rm = SBT(nc, "rm_s", [128, 128], F32); brm = Buf()
        ob = SBT(nc, "ob_s", [128, 128], F32); bob = Buf()
        cst = SBT(nc, "cst", [128, 2, TB], F32); bcst = Buf()
        sq = SBT(nc, "sq", [128, TB], F32); bsq = Buf()
        rs = SBT(nc, "rs", [128, TB], F32); brs = Buf()
        t1 = SBT(nc, "t1", [128, TB], F32); bt1 = Buf()
        fw.dma("sp", [], [bgv], gv[:], gv_d[:, :])
        fw.dma("sp", [], [brm], rm[:], rm_d[:, :])
        fw.dma("sp", [], [bob], ob[:], ob_d[:, :])
    load_w_bf16(fw, nc, w_in, win, bwin, D, NIN, stg, bstg)
    blocks = [(t0, min(TB, NT_LAT - t0), 0) for t0 in range(0, NT_LAT, TB)]
    blocks += [(NT_LAT + t0, min(TB, NT_CTX - t0), 1) for t0 in range(0, NT_CTX, TB)]
    cur_g = -1
    xi = 0
    fi = 0
    oi = 0
    ti = 0
    for (t0, tb, g) in blocks:
        if g != cur_g:
            cur_g = g
            fw.op("dve", [bfm], [bgs], "scalar_tensor_tensor", out=gs[:, 0:8], in0=fm[:, 8 + 16 * g:16 + 16 * g],
                  scalar=1.0, in1=fm[:, 0:8], op0=ALU.add, op1=ALU.mult)
        shc = 16 + 16 * g
        nt = tb // 128
        if even:
            fw.dma("act", [], [bcst], cst[:, :, 0:tb], cs_d.rearrange("a p t -> p a t")[:, :, t0:t0 + tb])
        for i in range(nt):
            xt = xin[xi % 2]; bxt = bxin[xi % 2]; xi += 1
            fw.dma("sp", [], [bxt], xt[:], x[t0 + i * 128:t0 + (i + 1) * 128, :])
            fw.op("act", [bxt], [bxn, bss], "activation", out=xn[:], in_=xt[:], func=AF.Square, accum_out=ss[:])
            emit_rstd(fw, ss[:], bss, rstd[:], brstd, D)
            fw.op("act", [bxt, brstd], [bxn], "activation", out=xn[:], in_=xt[:], func=AF.Copy, scale=rstd[:, 0:1])
            for c in range(8):
                fw.op("pe", [bxn, bid], [bptr], "transpose", out=ptr[:, c, :], in_=xn[:, c * 128:(c + 1) * 128],
                      identity=ident[:])
            for c in range(8):
                if c % 2 == 0:
                    fw.op("dve", [bptr, bgs, bfm], [bhT], "tensor_scalar", out=hT[:, c, i * 128:(i + 1) * 128],
                          in0=ptr[:, c, :], scalar1=gs[:, c:c + 1], scalar2=fm[:, shc + c:shc + c + 1],
                          op0=ALU.mult, op1=ALU.add)
                else:
                    fw.op("act", [bptr, bgs, bfm], [bhT], "activation", out=hT[:, c, i * 128:(i + 1) * 128],
                          in_=ptr[:, c, :], func=AF.Identity, scale=gs[:, c:c + 1],
                          bias=fm[:, shc + c:shc + c + 1])
        for ci, (col, kind) in enumerate(fm_list):
            pf = pfm[fi % 2]; bpf = bpfm[fi % 2]; fi += 1
            for k in range(8):
                fw.op("pe", [bhT, bwin], [bpf], "matmul", pf[:, 0:tb], lhsT=win[:, k, col:col + 128],
                      rhs=hT[:, k, 0:tb], start=(k == 0), stop=(k == 7))
            if kind == "c":
                o = ofb[oi % 3]; bo = bofb[oi % 3]; oi += 1
                if ci % 2 == 0:
                    fw.op("act", [bpf], [bo], "activation", out=o[:, 0:tb], in_=pf[:, 0:tb], func=AF.Copy)
                else:
                    fw.op("dve", [bpf], [bo], "tensor_copy", out=o[:, 0:tb], in_=pf[:, 0:tb])
                fw.dma("sp", [bo], [], qk_o[ci, :, t0:t0 + tb], o[:, 0:tb], is_output=True)
            elif kind == "z":
                o = ofm[oi % 3]; bo = bofm[oi % 3]; oi += 1
                if ci % 2 == 0:
                    fw.op("act", [bpf], [bo], "activation", out=o[:, 0:tb], in_=pf[:, 0:tb], func=AF.Copy)
                else:
                    fw.op("dve", [bpf], [bo], "tensor_copy", out=o[:, 0:tb], in_=pf[:, 0:tb])
                fw.dma("sp", [bo], [], zd_o[ci - 8, :, t0:t0 + tb], o[:, 0:tb], is_output=True)
            else:
                pa = paux[fi % 2]; bpa = bpaux[fi % 2]
                if kind in ("a", "ak"):
                    gcol = 0 if kind == "a" else 1
                    fw.op("act", [bpf], [bsq], "activation", out=sq[:, 0:tb], in_=pf[:, 0:tb], func=AF.Square)
                    fw.op("pe", [bsq, bob], [bpa], "matmul", pa[:, 0:tb], lhsT=ob[:], rhs=sq[:, 0:tb],
                          start=True, stop=True)
                    fw.op("dve", [bpa], [brs], "tensor_scalar", out=rs[:, 0:tb], in0=pa[:, 0:tb], scalar1=RMS_EPS,
                          scalar2=None, op0=ALU.add)
                    fw.op("act", [brs], [brs], "activation", out=rs[:, 0:tb], in_=rs[:, 0:tb], func=AF.Sqrt)
                    fw.op("dve", [brs], [brs], "reciprocal", out=rs[:, 0:tb], in_=rs[:, 0:tb])
                    fw.op("dve", [bpf, brs, bgv], [bt1], "scalar_tensor_tensor", out=t1[:, 0:tb], in0=pf[:, 0:tb],
                          scalar=gv[:, gcol:gcol + 1], in1=rs[:, 0:tb], op0=ALU.mult, op1=ALU.mult)
                else:
                    fw.op("act", [bpf], [bt1], "activation", out=t1[:, 0:tb], in_=pf[:, 0:tb], func=AF.Copy)
                pr = paux[(fi + 1) % 2]; bpr = bpaux[(fi + 1) % 2]
                fw.op("pe", [bt1, brm], [bpr], "matmul", pr[:, 0:tb], lhsT=rm[:], rhs=t1[:, 0:tb],
                      start=True, stop=True)
                fw.op("dve", [bpr, bcst], [bsq], "tensor_tensor", out=sq[:, 0:tb], in0=pr[:, 0:tb],
                      in1=cst[:, 1, 0:tb], op=ALU.mult)
                fw.op("pool", [bt1, bcst], [bt1], "tensor_tensor", out=t1[:, 0:tb], in0=t1[:, 0:tb],
                      in1=cst[:, 0, 0:tb], op=ALU.mult)
                o = ofb[oi % 3]; bo = bofb[oi % 3]; oi += 1
                fw.op("dve", [bt1, bsq], [bo], "tensor_tensor", out=o[:, 0:tb], in0=t1[:, 0:tb], in1=sq[:, 0:tb],
                      op=ALU.add)
                fw.dma("sp", [bo], [], qk_o[ci, :, t0:t0 + tb], o[:, 0:tb], is_output=True)
        for i in range(nt):
            pt = ptm[ti % 2]; bpt = bptm[ti % 2]
            o = otm[ti % 2]; bo = botm[ti % 2]; ti += 1
            oc = 0
            for (col, wd) in tm_list:
                for k in range(8):
                    fw.op("pe", [bhT, bwin], [bpt], "matmul", pt[:, oc:oc + wd], lhsT=hT[:, k, i * 128:(i + 1) * 128],
                          rhs=win[:, k, col:col + wd], start=(k == 0 and oc == 0), stop=(k == 7),
                          skip_group_check=True)
                oc += wd
            fw.op("act", [bpt], [bo], "activation", out=o[:], in_=pt[:, 0:NTM], func=AF.Copy)
            fw.dma("pool", [bo], [], v_o[t0 + i * 128:t0 + (i + 1) * 128, :], o[:], is_output=True)
    fw.finish()
    return nc


def rope_tables(pos_tok, is_ctx):
    t = pos_tok.astype(np.int64)
    row = (t // 64).astype(np.float32)
    colp = (t % 64).astype(np.float32)
    inv = (10000.0 ** (-np.arange(16, dtype=np.float32) / 16)).astype(np.float32)
    ang = np.concatenate([row[:, None] * inv, colp[:, None] * inv], axis=-1)
    cos = np.cos(ang).astype(np.float32)
    sin = np.sin(ang).astype(np.float32)
    cos[is_ctx] = 1.0
    sin[is_ctx] = 0.0
    cosf = np.concatenate([cos, cos, cos, cos], axis=1).T
    sinf = np.concatenate([sin, sin, sin, sin], axis=1).T
    return np.ascontiguousarray(np.stack([cosf, sinf]).astype(np.float32))


def rope_consts():
    rm = np.zeros((128, 128), np.float32)
    for hb in (0, 64):
        for m in range(32):
            rm[hb + m + 32, hb + m] = -1.0
            rm[hb + m, hb + m + 32] = 1.0
    ob = np.zeros((128, 128), np.float32)
    ob[0:64, 0:64] = 1.0 / 64
    ob[64:128, 64:128] = 1.0 / 64
    return rm, ob


SCALE = 64 ** -0.5


def band_masks():
    m = np.zeros((6, 128, 512), np.float32)
    p = np.arange(128)[:, None]
    f = np.arange(512)[None, :]
    for r in range(6):
        m[r] = (np.abs((r - 1) * 128 + p - f) <= 128)
    return m.astype(NPBF)


def build_kbe(NLAT, NCTX):
    NTOT = NLAT + NCTX
    NKB = NTOT // 128
    NKL = NLAT // 128
    QB = 512
    PC = 2048
    nc = bass.Bass("TRN2", target_bir_lowering=False)
    q_d = [nc.dram_tensor("q%d" % m, [128, NTOT], BF16, kind="ExternalInput").ap() for m in range(2)]
    k_d = [nc.dram_tensor("k%d" % m, [128, NTOT], BF16, kind="ExternalInput").ap() for m in range(2)]
    v_d = [nc.dram_tensor("v%d" % m, [128, NKB * 65], BF16, kind="ExternalInput").ap() for m in range(2)]
    mask_d = nc.dram_tensor("mask", [6, 128, 512], BF16, kind="ExternalInput").ap()
    sink_d = nc.dram_tensor("sink", [64, 2], F32, kind="ExternalInput").ap()
    sel_d = nc.dram_tensor("sel", [65, 64], F32, kind="ExternalInput").ap()
    y_o = nc.dram_tensor("y_o", [4, 64, NTOT], BF16, kind="ExternalOutput").ap()
    fw = FW(nc)
    npc = max(1, NLAT // PC)
    pcs = [(i * PC, PC if i < npc - 1 else NTOT - i * PC) for i in range(npc)]
    Q = []; K = []; V = []; bQ = []; bK = []; bV = []
    for m in range(2):
        Q.append(SBT(nc, "Q%d" % m, [128, NTOT], BF16)); bQ.append([Buf() for _ in pcs])
        K.append(SBT(nc, "K%d" % m, [128, NTOT], BF16)); bK.append([Buf() for _ in pcs])
        V.append(SBT(nc, "V%d" % m, [128, NKB, 65], BF16)); bV.append([Buf() for _ in pcs])
    msk = SBT(nc, "msk", [128, 6, 512], BF16); bmsk = Buf()
    snk = SBT(nc, "snk", [64, 2], F32); bsnk = Buf()
    sel = SBT(nc, "sel_s", [65, 64], F32); bsel = Buf()
    fw.dma("sp", [], [bmsk], msk[:], mask_d.rearrange("r p f -> p r f"))
    fw.dma("sp", [], [bsnk], snk[:], sink_d[:, :])
    fw.dma("sp", [], [bsel], sel[:], sel_d[:, :])
    fw.op("act", [bsnk], [bsnk], "activation", out=snk[:], in_=snk[:], func=AF.Exp)
    for m in range(2):
        for pi, (c0, cw) in enumerate(pcs):
            fw.dma("sp", [], [bK[m][pi]], K[m][:, c0:c0 + cw], k_d[m][:, c0:c0 + cw])
            fw.dma("act", [], [bQ[m][pi]], Q[m][:, c0:c0 + cw], q_d[m][:, c0:c0 + cw])
            kb0 = c0 // 128; kbn = cw // 128
            fw.dma("pool", [], [bV[m][pi]], V[m][:, kb0:kb0 + kbn, :],
                   v_d[m][:, kb0 * 65:(kb0 + kbn) * 65].rearrange("p (k d) -> p k d", d=65))

    def pc_of(col):
        return min(col // PC, npc - 1)

    NPB = 4
    pS = [PST(nc, "pS%d" % i, [128, QB], F32) for i in range(NPB)]; bpS = [Buf(excl=True) for _ in range(NPB)]
    pO = [PST(nc, "pO%d" % i, [128, QB], F32) for i in range(2)]; bpO = [Buf(excl=True), Buf(excl=True)]
    pB = PST(nc, "pB", [128, QB], F32); bpB = Buf(excl=True)
    PT = [SBT(nc, "PT%d" % i, [128, QB], BF16) for i in range(NPB)]; bPT = [Buf() for _ in range(NPB)]
    PM = [SBT(nc, "PM%d" % i, [128, QB], BF16) for i in range(4)]; bPM = [Buf() for _ in range(4)]
    osb = [SBT(nc, "osb%d" % i, [65, QB], F32) for i in range(2)]; bosb = [Buf(), Buf()]
    rden = SBT(nc, "rden", [64, QB], F32); brden = Buf()
    yo = [SBT(nc, "yo%d" % i, [64, QB], BF16) for i in range(2)]; byo = [Buf(), Buf()]
    cnt = {"s": 0, "o": 0, "m": 0}

    def attn2(m, q0, qn, kbl, use_sink):
        bq = bQ[m][pc_of(q0)]

        def s_stage(kb, mi):
            pss = []
            for hh in range(2):
                hs = slice(hh * 64, (hh + 1) * 64)
                si = cnt["s"]; cnt["s"] += 1
                ps = pS[si % NPB]; bps = bpS[si % NPB]
                fw.op("pe", [bK[m][pc_of(kb * 128)], bq], [bps], "matmul", ps[:, 0:qn],
                      lhsT=K[m][hs, kb * 128:(kb + 1) * 128], rhs=Q[m][hs, q0:q0 + qn], start=True, stop=True)
                pss.append((ps, bps, PT[si % NPB], bPT[si % NPB]))
            srcs = []
            for hh in range(2):
                ps, bps, pt, bpt = pss[hh]
                fw.op("act", [bps], [bpt], "activation", out=pt[:, 0:qn], in_=ps[:, 0:qn], func=AF.Exp, scale=SCALE)
                src = pt; bsrc = bpt
                if mi is not None:
                    mm = cnt["m"]; cnt["m"] += 1
                    pm = PM[mm % 4]; bpm = bPM[mm % 4]
                    fw.op("dve", [bpt, bmsk], [bpm], "tensor_tensor", out=pm[:, 0:qn], in0=pt[:, 0:qn],
                          in1=msk[:, mi, 0:qn], op=ALU.mult)
                    src = pm; bsrc = bpm
                srcs.append((src, bsrc))
            return srcs

        def pv_stage(n, kb, srcs):
            for hh in range(2):
                src, bsrc = srcs[hh]
                po = pO[hh]; bpo = bpO[hh]
                fw.op("pe", [bsrc, bV[m][pc_of(kb * 128)]], [bpo], "matmul", po[0:65, 0:qn], lhsT=V[m][:, kb, 0:65],
                      rhs=src[:, 0:qn], start=(n == 0), stop=(n == len(kbl) - 1))

        prev = None
        for n, (kb, mi) in enumerate(kbl):
            srcs = s_stage(kb, mi)
            if prev is not None:
                pv_stage(*prev)
            prev = (n, kb, srcs)
        pv_stage(*prev)
        for hh in range(2):
            po = pO[hh]; bpo = bpO[hh]
            oi = cnt["o"]; cnt["o"] += 1
            ob = osb[oi % 2]; bob = bosb[oi % 2]
            fw.op("act", [bpo], [bob], "activation", out=ob[:, 0:qn], in_=po[0:65, 0:qn], func=AF.Copy)
            fw.op("pe", [bob, bsel], [bpB], "matmul", pB[0:64, 0:qn], lhsT=sel[:], rhs=ob[:, 0:qn], start=True, stop=True)
            if use_sink:
                fw.op("dve", [bpB, bsnk], [brden], "tensor_scalar", out=rden[:, 0:qn], in0=pB[0:64, 0:qn],
                      scalar1=snk[:, hh:hh + 1], scalar2=None, op0=ALU.add)
                fw.op("dve", [brden], [brden], "reciprocal", out=rden[:, 0:qn], in_=rden[:, 0:qn])
            else:
                fw.op("dve", [bpB], [brden], "reciprocal", out=rden[:, 0:qn], in_=pB[0:64, 0:qn])
            y = yo[oi % 2]; by = byo[oi % 2]
            fw.op("dve", [bob, brden], [by], "tensor_tensor", out=y[:, 0:qn], in0=ob[0:64, 0:qn], in1=rden[:, 0:qn],
                  op=ALU.mult)
            fw.dma("sp", [by], [], y_o[m * 2 + hh, :, q0:q0 + qn], y[:, 0:qn], is_output=True)

    ctx_kb = [(NKL + i, None) for i in range(NCTX // 128)]
    for qb in range(NLAT // QB):
        kbl = []
        for r in range(6):
            kb = 4 * qb + r - 1
            if 0 <= kb < NKL:
                kbl.append((kb, r))
        attn2(1, qb * QB, QB, kbl + ctx_kb, True)
    attn2(1, NLAT, NCTX, ctx_kb, True)
    for qb in range(NLAT // QB):
        attn2(0, qb * QB, QB, [(kb, None) for kb in range(NKL)] + ctx_kb, False)
    attn2(0, NLAT, NCTX, ctx_kb, False)
    fw.finish()
    return nc


def vaug_layout(v):
    ntot = v.shape[0]
    nkb = ntot // 128
    o = np.ones((128, nkb, 65), v.dtype)
    o[:, :, 0:64] = v.reshape(nkb, 128, 64).transpose(1, 0, 2)
    return np.ascontiguousarray(o.reshape(128, nkb * 65))


def sel_const():
    s = np.zeros((65, 64), np.float32)
    s[64, :] = 1.0
    return s


GN_EPS = 64e-5
LAM = -float(np.exp(-0.5))
CH = 64


def rwkv_consts():
    p = np.arange(128)[:, None] % 64
    f = np.arange(64)[None, :]
    su = (f > p).astype(np.float32)
    sl = (f < p).astype(np.float32)
    iu = (f >= p).astype(np.float32)
    il = (f <= p).astype(np.float32)
    mk = np.zeros((2, 128, 5, 64), np.float32)
    mk[0] = np.stack([su, iu, su, iu, sl], 1)
    mk[1] = np.stack([sl, il, sl, il, su], 1)
    ob1 = np.zeros((128, 128), np.float32)
    ob1[:64, :64] = 1.0
    ob1[64:, 64:] = 1.0
    sm = np.ones((128, 512), np.float32)
    sm[:, ::64] = 0.0
    id2 = (p == f).astype(np.float32)
    return mk, ob1, sm, id2


def build_kbr(NLAT, NCTX, dbg=3, dumps=None):
    NTOT = NLAT + NCTX
    TBK = 512
    nc = bass.Bass("TRN2", target_bir_lowering=False)
    z_d = nc.dram_tensor("z", [5, 128, NTOT], F32, kind="ExternalInput").ap()
    pv_d = nc.dram_tensor("pv", [128, 14], F32, kind="ExternalInput").ap()
    w2_d = nc.dram_tensor("w2", [2, 64, 128], F32, kind="ExternalInput").ap()
    a2_d = nc.dram_tensor("a2", [2, 64, 128], F32, kind="ExternalInput").ap()
    g2_d = nc.dram_tensor("g2", [128, 128], F32, kind="ExternalInput").ap()
    mk_d = nc.dram_tensor("mk", [2, 128, 5, 64], F32, kind="ExternalInput").ap()
    ob1_d = nc.dram_tensor("ob1", [128, 128], F32, kind="ExternalInput").ap()
    sm_d = nc.dram_tensor("sm", [128, 512], F32, kind="ExternalInput").ap()
    id2_d = nc.dram_tensor("id2", [128, 64], F32, kind="ExternalInput").ap()
    yd_o = nc.dram_tensor("yd_o", [128, NTOT], BF16, kind="ExternalOutput").ap()
    yf_o = nc.dram_tensor("yf_o", [128, NTOT], F32, kind="ExternalOutput").ap()
    fw = FW(nc)

    def sb(name, shape, dt=F32):
        return SBT(nc, name, shape, dt), Buf(name)

    pv, bpv = sb("pv_s", [128, 14])
    pd, bpd = sb("pd_s", [128, 16])
    w2s, bw2 = sb("w2_s", [64, 2, 128])
    a2s, ba2 = sb("a2_s", [128, 2, 128])
    g2s, bg2 = sb("g2_s", [128, 128])
    mk, bmk = sb("mk_s", [128, 2, 5, 64])
    ob1, bob1 = sb("ob1_s", [128, 128])
    ob64, bob64 = sb("ob64_s", [128, 128])
    sm, bsm = sb("sm_s", [128, 512])
    id2, bid2 = sb("id2_s", [128, 64])
    ident, bid = make_ident(fw, nc)
    fw.dma("sp", [], [bpv], pv[:], pv_d[:, :])
    fw.dma("sp", [], [bw2], w2s[:], w2_d.rearrange("d k n -> k d n"))
    fw.dma("sp", [], [ba2], a2s[64:128, :, :], a2_d.rearrange("d k n -> k d n"))
    fw.dma("sp", [], [bg2], g2s[:], g2_d[:, :])
    fw.dma("act", [], [bmk], mk[:], mk_d.rearrange("d p a f -> p d a f"))
    fw.dma("act", [], [bob1], ob1[:], ob1_d[:, :])
    fw.dma("act", [], [bsm], sm[:], sm_d[:, :])
    fw.dma("act", [], [bid2], id2[:], id2_d[:, :])
    fw.op("dve", [bob1], [bob64], "tensor_scalar", out=ob64[:], in0=ob1[:], scalar1=1.0 / 64, scalar2=None, op0=ALU.mult)
    fw.op("dve", [bpv], [bpd], "tensor_scalar", out=pd[:, 0:5], in0=pv[:, 0:5], scalar1=-1.0, scalar2=1.0,
          op0=ALU.mult, op1=ALU.add)
    fw.op("dve", [bpv], [bpd], "tensor_scalar", out=pd[:, 5:10], in0=pv[:, 0:5], scalar1=0.5, scalar2=None, op0=ALU.mult)
    fw.op("dve", [bpv], [bpd], "tensor_scalar", out=pd[:, 10:11], in0=pv[:, 10:11], scalar1=-1.0, scalar2=1.0,
          op0=ALU.mult, op1=ALU.add)
    fw.op("dve", [bpv], [bpd], "tensor_scalar", out=pd[:, 11:12], in0=pv[:, 10:11], scalar1=-2.0, scalar2=2.0,
          op0=ALU.mult, op1=ALU.add)

    NZ = 2
    zt = [[sb("zt%d_%d" % (i, q), [128, TBK + 2]) for q in range(5)] for i in range(NZ)]
    zs = [sb("zs%d" % q, [128, TBK]) for q in range(5)]
    names = ["tmp", "aa", "tws", "sg", "ad", "af", "kk", "kk2", "rin", "kkn", "tt", "khat", "bt", "c1", "c0", "cr", "c2",
             "ei", "ee", "en", "eh", "Bt", "Kt", "Bh", "Kh", "yb", "yfb", "yy", "yc", "yc2", "rs2", "sgd", "ks", "pr"]
    T = {n: sb("r_" + n, [128, TBK]) for n in names}
    AR, bAR = sb("AR", [128, TBK // CH, 2, CH])
    gC, bgC = sb("gC", [128, TBK // CH])
    ob_out, bob_out = sb("ob_out", [128, TBK], BF16)
    Msb = [sb("Msb%d" % i, [128, 5, CH]) for i in range(2)]
    PP = [sb("PP%d" % i, [128, 2, CH]) for i in range(2)]
    XX = [sb("XX%d" % i, [128, CH]) for i in range(2)]
    TM = [sb("TM%d" % i, [128, 3, CH]) for i in range(2)]
    W1, bW1 = sb("W1", [128, CH])
    U, bU = sb("U", [128, CH])
    ST = [sb("ST%d" % i, [128, CH]) for i in range(2)]
    pL = [(PST(nc, "pL%d" % i, [128, TBK], F32), Buf(excl=True)) for i in range(2)]
    pA = (PST(nc, "pA", [128, 512], F32), Buf(excl=True))
    pI = (PST(nc, "pI", [128, 512], F32), Buf(excl=True))
    pTt = (PST(nc, "pTt", [128, 512], F32), Buf(excl=True))
    pSt = [(PST(nc, "pSt%d" % i, [128, 512], F32), Buf(excl=True)) for i in range(2)]
    cnt = {"z": 0, "l": 0, "m": 0, "x": 0, "t": 0, "s": 0, "st": 0}

    def lora_ps():
        p = pL[cnt["l"] % 2]; cnt["l"] += 1
        return p

    dumped = set()

    def dump(name, ap, b, shape):
        if dumps is None or name in dumped:
            return
        dumped.add(name)
        dt_ = nc.dram_tensor("dbg_" + name, list(shape), F32, kind="ExternalOutput").ap()
        fw.dma("sp", [b], [], dt_, ap, is_output=True)
        dumps.append(name)

    def prep(t0, tb, lo, hi, d):
        nch = tb // CH
        zi = zt[cnt["z"] % NZ]; cnt["z"] += 1
        a = max(lo, t0 - 1); b = min(hi, t0 + tb + 1)
        for q in range(5):
            z, bz = zi[q]
            if t0 - 1 < lo:
                fw.op("pool", [], [bz], "memset", z[:, 0:1], 0.0)
            if t0 + tb + 1 > hi:
                fw.op("pool", [], [bz], "memset", z[:, tb + 1:tb + 2], 0.0)
            fw.dma("sp" if q % 2 == 0 else "act", [], [bz], z[:, a - (t0 - 1):b - (t0 - 1)], z_d[q, :, a:b])
            tmp, btmp = T["tmp"]; aa, baa = T["aa"]
            fw.op("pool", [bz], [btmp], "tensor_tensor", out=tmp[:, 0:tb], in0=z[:, 0:tb], in1=z[:, 2:tb + 2], op=ALU.add)
            fw.op("dve", [bz, bpd], [baa], "tensor_scalar", out=aa[:, 0:tb], in0=z[:, 1:tb + 1], scalar1=pd[:, q:q + 1],
                  scalar2=None, op0=ALU.mult)
            fw.op("dve", [btmp, baa, bpd], [zs[q][1]], "scalar_tensor_tensor", out=zs[q][0][:, 0:tb], in0=tmp[:, 0:tb],
                  scalar=pd[:, 5 + q:6 + q], in1=aa[:, 0:tb], op0=ALU.mult, op1=ALU.add)
        (zr, bzr), (zk, bzk), (zv, bzv), (zw, bzw), (zg, bzg) = zs
        tws, btws = T["tws"]
        fw.op("act", [bzw], [btws], "activation", out=tws[0:64, 0:tb], in_=zw[0:64, 0:tb], func=AF.Tanh)
        pw, bpw = lora_ps()
        fw.op("pe", [btws, bw2], [bpw], "matmul", pw[:, 0:tb], lhsT=w2s[:, d, :], rhs=tws[0:64, 0:tb], start=True, stop=True)
        sg, bsg = T["sg"]
        fw.op("act", [bpw, bpv], [bsg], "activation", out=sg[:, 0:tb], in_=pw[:, 0:tb], func=AF.Sigmoid,
              bias=pv[:, 5 + d:6 + d])
        pa, bpa = lora_ps()
        fw.op("pe", [bzw, ba2], [bpa], "matmul", pa[:, 0:tb], lhsT=a2s[64:128, d, :], rhs=zw[64:128, 0:tb],
              start=True, stop=True)
        ad, bad = T["ad"]
        fw.op("act", [bpa, bpv], [bad], "activation", out=ad[:, 0:tb], in_=pa[:, 0:tb], func=AF.Sigmoid,
              bias=pv[:, 7 + d:8 + d])
        kk, bkk = T["kk"]; kk2, bkk2 = T["kk2"]; rin, brin = T["rin"]; kkn, bkkn = T["kkn"]
        fw.op("dve", [bzk, bpv], [bkk], "tensor_scalar", out=kk[:, 0:tb], in0=zk[:, 0:tb], scalar1=pv[:, 9:10],
              scalar2=None, op0=ALU.mult)
        fw.op("pool", [bkk], [bkk2], "tensor_tensor", out=kk2[:, 0:tb], in0=kk[:, 0:tb], in1=kk[:, 0:tb], op=ALU.mult)
        pss, bpss = lora_ps()
        fw.op("pe", [bkk2, bob1], [bpss], "matmul", pss[:, 0:tb], lhsT=ob1[:], rhs=kk2[:, 0:tb], start=True, stop=True)
        fw.op("act", [bpss], [brin], "activation", out=rin[:, 0:tb], in_=pss[:, 0:tb], func=AF.Sqrt)
        fw.op("dve", [brin], [brin], "tensor_scalar", out=rin[:, 0:tb], in0=rin[:, 0:tb], scalar1=1e-12, scalar2=None,
              op0=ALU.max)
        fw.op("dve", [brin], [brin], "reciprocal", out=rin[:, 0:tb], in_=rin[:, 0:tb])
        fw.op("pool", [bkk, brin], [bkkn], "tensor_tensor", out=kkn[:, 0:tb], in0=kk[:, 0:tb], in1=rin[:, 0:tb], op=ALU.mult)
        tt, btt = T["tt"]; khat, bkhat = T["khat"]; bt, bbt = T["bt"]
        fw.op("dve", [bad, bpv, bpd], [btt], "tensor_scalar", out=tt[:, 0:tb], in0=ad[:, 0:tb], scalar1=pv[:, 10:11],
              scalar2=pd[:, 10:11], op0=ALU.mult, op1=ALU.add)
        fw.op("pool", [bzk, btt], [bkhat], "tensor_tensor", out=khat[:, 0:tb], in0=zk[:, 0:tb], in1=tt[:, 0:tb], op=ALU.mult)
        fw.op("pool", [bkkn, bad], [bbt], "tensor_tensor", out=bt[:, 0:tb], in0=kkn[:, 0:tb], in1=ad[:, 0:tb], op=ALU.mult)
        c1, bc1 = T["c1"]; c0, bc0 = T["c0"]; cr, bcr = T["cr"]; c2, bc2 = T["c2"]
        fw.op("dve", [bsm, bsg], [bc1], "tensor_tensor_scan", out=c1[:, 0:tb], data0=sm[:, 0:tb], data1=sg[:, 0:tb],
              initial=0.0, op0=ALU.mult, op1=ALU.add)
        fw.op("pool", [bc1, bsg], [bc0], "tensor_tensor", out=c0[:, 0:tb], in0=c1[:, 0:tb], in1=sg[:, 0:tb], op=ALU.subtract)
        c1v = c1[:, 0:tb].rearrange("p (c j) -> p c j", j=CH)
        totb = c1v[:, :, CH - 1:CH].to_broadcast([128, nch, CH])
        fw.op("dve", [bc1], [bcr], "tensor_tensor", out=cr[:, 0:tb].rearrange("p (c j) -> p c j", j=CH), in0=totb,
              in1=c1v, op=ALU.subtract)
        if d == 0:
            incl, bincl, excl, bexcl, hn, bhn = c1, bc1, c0, bc0, cr, bcr
        else:
            fw.op("pool", [bcr, bsg], [bc2], "tensor_tensor", out=c2[:, 0:tb], in0=cr[:, 0:tb], in1=sg[:, 0:tb], op=ALU.add)
            incl, bincl, excl, bexcl, hn, bhn = c2, bc2, cr, bcr, c0, bc0
        ei, bei = T["ei"]; ee, bee = T["ee"]; en, ben = T["en"]; eh, beh = T["eh"]
        fw.op("act", [bincl], [bei], "activation", out=ei[:, 0:tb], in_=incl[:, 0:tb], func=AF.Exp, scale=LAM)
        fw.op("act", [bexcl], [bee], "activation", out=ee[:, 0:tb], in_=excl[:, 0:tb], func=AF.Exp, scale=LAM)
        fw.op("act", [bincl], [ben], "activation", out=en[:, 0:tb], in_=incl[:, 0:tb], func=AF.Exp, scale=-LAM)
        fw.op("act", [bhn], [beh], "activation", out=eh[:, 0:tb], in_=hn[:, 0:tb], func=AF.Exp, scale=LAM)
        fw.op("act", [bc1], [bgC], "activation", out=gC[:, 0:nch], in_=c1v[:, :, CH - 1], func=AF.Exp, scale=LAM)
        v3 = lambda ap: ap[:, 0:tb].rearrange("p (c j) -> p c j", j=CH)
        fw.op("dve", [bkkn, bee], [bAR], "scalar_tensor_tensor", out=AR[:, 0:nch, 0, :], in0=v3(kkn), scalar=-1.0,
              in1=v3(ee), op0=ALU.mult, op1=ALU.mult)
        fw.op("pool", [bzr, bei], [bAR], "tensor_tensor", out=AR[:, 0:nch, 1, :], in0=v3(zr), in1=v3(ei), op=ALU.mult)
        for (nm, x0, bx0, e0, be0) in (("Bt", bt, bbt, en, ben), ("Kt", khat, bkhat, en, ben), ("Bh", bt, bbt, eh, beh),
                                       ("Kh", khat, bkhat, eh, beh)):
            eng = "dve" if nm in ("Bt", "Bh") else "pool"
            fw.op(eng, [bx0, be0], [T[nm][1]], "tensor_tensor", out=T[nm][0][:, 0:tb], in0=x0[:, 0:tb], in1=e0[:, 0:tb],
                  op=ALU.mult)
        for q in range(5):
            dump("zs%d" % q, zs[q][0][:, 0:tb], zs[q][1], [128, tb])
        for nm in ("sg", "ad", "kkn", "khat", "bt", "c1", "c0", "cr", "ei", "ee", "en", "eh", "Bt", "Kt", "Bh", "Kh"):
            dump(nm, T[nm][0][:, 0:tb], T[nm][1], [128, tb])
        dump("AR", AR[:, 0:nch, :, :], bAR, [128, nch, 2, CH])
        dump("gC", gC[:, 0:nch], bgC, [128, nch])

    def chunk(c, d, ycol):
        cs = slice(c * CH, (c + 1) * CH)
        Bt, bBt = T["Bt"]; Kt, bKt = T["Kt"]; Bh, bBh = T["Bh"]; Kh, bKh = T["Kh"]
        zv, bzv = zs[2]
        pa, bpa = pA
        for h in range(2):
            hs = slice(h * 64, (h + 1) * 64)
            arf = AR[hs, c, :, :].rearrange("p a j -> p (a j)")
            fw.op("pe", [bBt, bAR], [bpa], "matmul", pa[hs, 0:128], lhsT=Bt[hs, cs], rhs=arf, start=True, stop=True)
            fw.op("pe", [bKt, bAR], [bpa], "matmul", pa[hs, 128:256], lhsT=Kt[hs, cs], rhs=arf, start=True, stop=True)
            fw.op("pe", [bBt, bAR], [bpa], "matmul", pa[hs, 256:320], lhsT=AR[hs, c, 0, :], rhs=Bt[hs, cs],
                  start=True, stop=True)
        M, bM = Msb[cnt["m"] % 2]; cnt["m"] += 1
        fw.op("dve", [bpa, bmk], [bM], "tensor_tensor", out=M[:], in0=pa[:, 0:320].rearrange("p (a j) -> p a j", j=CH),
              in1=mk[:, d, :, :], op=ALU.mult)
        dump("M", M[:], bM, [128, 5, CH])
        if dbg == 21:
            return
        P, bP = PP[cnt["x"] % 2]
        X, bX = XX[cnt["x"] % 2]
        cnt["x"] += 1
        fw.op("pool", [bM, bid2], [bX], "tensor_tensor", out=X[:], in0=M[:, 0, :], in1=id2[:], op=ALU.add)
        pi_, bpi = pI
        curP = (M, bM, 0, 4)
        for i in range(1, 6):
            t_, bt_, ip, ipt = curP
            for h in range(2):
                hs = slice(h * 64, (h + 1) * 64)
                if i < 5:
                    fw.op("pe", [bt_], [bpi], "matmul", pi_[hs, 0:64], lhsT=t_[hs, ipt, :], rhs=t_[hs, ip, :],
                          start=True, stop=True)
                fw.op("pe", [bt_], [bpi], "matmul", pi_[hs, 64:128], lhsT=t_[hs, ip, :], rhs=t_[hs, ipt, :],
                      start=True, stop=True)
            Pn, bPn = PP[cnt["x"] % 2]; cnt["x"] += 1
            if i < 5:
                fw.op("act", [bpi], [bPn], "activation", out=Pn[:], in_=pi_[:, 0:128].rearrange("p (a j) -> p a j", j=CH),
                      func=AF.Copy)
            else:
                fw.op("act", [bpi], [bPn], "activation", out=Pn[:, 1, :], in_=pi_[:, 64:128], func=AF.Copy)
            for h in range(2):
                hs = slice(h * 64, (h + 1) * 64)
                fw.op("pe", [bPn, bX], [bpi], "matmul", pi_[hs, 128:192], lhsT=Pn[hs, 1, :], rhs=X[hs, :],
                      start=True, stop=True)
            Xn, bXn = XX[cnt["x"] % 2]
            fw.op("dve", [bpi, bX], [bXn], "tensor_tensor", out=Xn[:], in0=pi_[:, 128:192], in1=X[:], op=ALU.add)
            X, bX = Xn, bXn
            curP = (Pn, bPn, 0, 1)
        dump("X", X[:], bX, [128, CH])
        if dbg == 22:
            return
        ptt, bptt = pTt
        for h in range(2):
            hs = slice(h * 64, (h + 1) * 64)
            fw.op("pe", [bzv, bid], [bptt], "matmul", ptt[hs, 0:64], lhsT=zv[hs, cs], rhs=ident[hs, hs], start=True, stop=True)
            fw.op("pe", [bBh, bid], [bptt], "matmul", ptt[hs, 64:128], lhsT=Bh[hs, cs], rhs=ident[hs, hs], start=True, stop=True)
            fw.op("pe", [bKh, bid], [bptt], "matmul", ptt[hs, 128:192], lhsT=Kh[hs, cs], rhs=ident[hs, hs], start=True, stop=True)
        tm, btm = TM[cnt["t"] % 2]; cnt["t"] += 1
        fw.op("act", [bptt], [btm], "activation", out=tm[:], in_=ptt[:, 0:192].rearrange("p (a j) -> p a j", j=CH),
              func=AF.Copy)
        dump("tm", tm[:], btm, [128, 3, CH])
        if dbg == 23:
            return
        S0, bS0 = ST[cnt["st"] % 2]
        S1, bS1 = ST[(cnt["st"] + 1) % 2]
        cnt["st"] += 1
        ps, bps = pSt[cnt["s"] % 2]; cnt["s"] += 1
        for h in range(2):
            hs = slice(h * 64, (h + 1) * 64)
            fw.op("pe", [bAR, bS0], [bps], "matmul", ps[hs, 0:64], lhsT=AR[hs, c, 0, :], rhs=S0[hs, :], start=True, stop=False)
            fw.op("pe", [bM, btm], [bps], "matmul", ps[hs, 0:64], lhsT=M[hs, 2, :], rhs=tm[hs, 0, :], start=False, stop=True)
        fw.op("act", [bps], [bW1], "activation", out=W1[:], in_=ps[:, 0:64], func=AF.Copy)
        for h in range(2):
            hs = slice(h * 64, (h + 1) * 64)
            fw.op("pe", [bX, bW1], [bps], "matmul", ps[hs, 64:128], lhsT=X[hs, :], rhs=W1[hs, :], start=True, stop=True)
        fw.op("act", [bps], [bU], "activation", out=U[:], in_=ps[:, 64:128], func=AF.Copy)
        for h in range(2):
            hs = slice(h * 64, (h + 1) * 64)
            fw.op("pe", [btm, bU], [bps], "matmul", ps[hs, 128:192], lhsT=tm[hs, 1, :], rhs=U[hs, :], start=True, stop=False)
            fw.op("pe", [btm], [bps], "matmul", ps[hs, 128:192], lhsT=tm[hs, 2, :], rhs=tm[hs, 0, :], start=False, stop=True)
        for h in range(2):
            hs = slice(h * 64, (h + 1) * 64)
            fw.op("pe", [bS0, bAR], [bps], "matmul", ps[hs, 192:256], lhsT=S0[hs, :], rhs=AR[hs, c, 1, :], start=True, stop=False)
            fw.op("pe", [bU, bM], [bps], "matmul", ps[hs, 192:256], lhsT=U[hs, :], rhs=M[hs, 1, :], start=False, stop=False)
            fw.op("pe", [btm, bM], [bps], "matmul", ps[hs, 192:256], lhsT=tm[hs, 0, :], rhs=M[hs, 3, :], start=False, stop=True)
        fw.op("dve", [bS0, bgC, bps], [bS1], "scalar_tensor_tensor", out=S1[:], in0=S0[:], scalar=gC[:, c:c + 1],
              in1=ps[:, 128:192], op0=ALU.mult, op1=ALU.add)
        yb, byb = T["yb"]
        fw.op("act", [bps], [byb], "activation", out=yb[:, cs], in_=ps[:, 192:256], func=AF.Copy)
        dump("W1", W1[:], bW1, [128, CH])
        dump("U", U[:], bU, [128, CH])
        dump("S1", S1[:], bS1, [128, CH])
        dump("ybc", yb[:, cs], byb, [128, CH])

    def readout(t0, tb):
        (zr, bzr), (zk, bzk), (zv, bzv), (zw, bzw), (zg, bzg) = zs
        yb, byb = T["yb"]; yfb, byfb = T["yfb"]; yy, byy = T["yy"]; yc, byc = T["yc"]; yc2, byc2 = T["yc2"]
        rs2, brs2 = T["rs2"]; sgd, bsgd = T["sgd"]; ks, bks = T["ks"]; pr, bpr = T["pr"]; af, baf = T["af"]
        ad, bad = T["ad"]
        fw.dma("sp", [byf_dram], [byfb], yfb[:, 0:tb], yf_o[:, t0:t0 + tb])
        fw.op("dve", [byb, byfb], [byy], "tensor_tensor", out=yy[:, 0:tb], in0=yb[:, 0:tb], in1=yfb[:, 0:tb], op=ALU.add)
        pm, bpm = lora_ps()
        fw.op("pe", [byy, bob64], [bpm], "matmul", pm[:, 0:tb], lhsT=ob64[:], rhs=yy[:, 0:tb], start=True, stop=True)
        fw.op("dve", [byy, bpm], [byc], "tensor_tensor", out=yc[:, 0:tb], in0=yy[:, 0:tb], in1=pm[:, 0:tb], op=ALU.subtract)
        fw.op("pool", [byc], [byc2], "tensor_tensor", out=yc2[:, 0:tb], in0=yc[:, 0:tb], in1=yc[:, 0:tb], op=ALU.mult)
        pvv, bpvv = lora_ps()
        fw.op("pe", [byc2, bob64], [bpvv], "matmul", pvv[:, 0:tb], lhsT=ob64[:], rhs=yc2[:, 0:tb], start=True, stop=True)
        fw.op("dve", [bpvv], [brs2], "tensor_scalar", out=rs2[:, 0:tb], in0=pvv[:, 0:tb], scalar1=GN_EPS, scalar2=None,
              op0=ALU.add)
        fw.op("act", [brs2], [brs2], "activation", out=rs2[:, 0:tb], in_=rs2[:, 0:tb], func=AF.Sqrt)
        fw.op("dve", [brs2], [brs2], "reciprocal", out=rs2[:, 0:tb], in_=rs2[:, 0:tb])
        fw.op("pool", [byc, brs2], [byc], "tensor_tensor", out=yc[:, 0:tb], in0=yc[:, 0:tb], in1=rs2[:, 0:tb], op=ALU.mult)
        fw.op("dve", [byc, bpv], [byc], "tensor_scalar", out=yc[:, 0:tb], in0=yc[:, 0:tb], scalar1=pv[:, 12:13],
              scalar2=pv[:, 13:14], op0=ALU.mult, op1=ALU.add)
        pa, bpa = lora_ps()
        fw.op("pe", [bzw, ba2], [bpa], "matmul", pa[:, 0:tb], lhsT=a2s[64:128, 0, :], rhs=zw[64:128, 0:tb],
              start=True, stop=True)
        fw.op("act", [bpa, bpv], [baf], "activation", out=af[:, 0:tb], in_=pa[:, 0:tb], func=AF.Sigmoid, bias=pv[:, 7:8])
        fw.op("pool", [baf, bad], [baf], "tensor_tensor", out=af[:, 0:tb], in0=af[:, 0:tb], in1=ad[:, 0:tb], op=ALU.add)
        fw.op("dve", [baf, bpv, bpd], [baf], "tensor_scalar", out=af[:, 0:tb], in0=af[:, 0:tb], scalar1=pv[:, 10:11],
              scalar2=pd[:, 11:12], op0=ALU.mult, op1=ALU.add)
        fw.op("pool", [bzk, baf], [bks], "tensor_tensor", out=ks[:, 0:tb], in0=zk[:, 0:tb], in1=af[:, 0:tb], op=ALU.mult)
        fw.op("dve", [bzr, bpv, bks], [bpr], "scalar_tensor_tensor", out=pr[:, 0:tb], in0=zr[:, 0:tb], scalar=pv[:, 11:12],
              in1=ks[:, 0:tb], op0=ALU.mult, op1=ALU.mult)
        pb, bpb = lora_ps()
        fw.op("pe", [bpr, bob1], [bpb], "matmul", pb[:, 0:tb], lhsT=ob1[:], rhs=pr[:, 0:tb], start=True, stop=True)
        fw.op("dve", [bpb, bzv], [bks], "tensor_tensor", out=ks[:, 0:tb], in0=pb[:, 0:tb], in1=zv[:, 0:tb], op=ALU.mult)
        fw.op("pool", [bks, byc], [byc], "tensor_tensor", out=yc[:, 0:tb], in0=yc[:, 0:tb], in1=ks[:, 0:tb], op=ALU.add)
        fw.op("act", [bzg], [bsgd], "activation", out=sgd[:, 0:tb], in_=zg[:, 0:tb], func=AF.Sigmoid)
        pg, bpg = lora_ps()
        fw.op("pe", [bsgd, bg2], [bpg], "matmul", pg[:, 0:tb], lhsT=g2s[:], rhs=sgd[:, 0:tb], start=True, stop=True)
        fw.op("dve", [bpg, byc], [bob_out], "tensor_tensor", out=ob_out[:, 0:tb], in0=pg[:, 0:tb], in1=yc[:, 0:tb], op=ALU.mult)
        fw.dma("sp", [bob_out], [], yd_o[:, t0:t0 + tb], ob_out[:, 0:tb], is_output=True)

    byf_dram = Buf("yf_dram")
    lat_blocks = [(t0, min(TBK, NLAT - t0), 0, NLAT) for t0 in range(0, NLAT, TBK)]
    ctx_blocks = [(NLAT + t0, min(TBK, NCTX - t0), NLAT, NTOT) for t0 in range(0, NCTX, TBK)]
    for d in range(2):
        S0, bS0 = ST[cnt["st"] % 2]
        fw.op("pool", [], [bS0], "memset", S0[:], 0.0)
        if d == 0:
            order = ctx_blocks + lat_blocks
        else:
            order = ctx_blocks[::-1] + lat_blocks[::-1]
        for (t0, tb, lo, hi) in order:
            prep(t0, tb, lo, hi, d)
            nch = tb // CH
            chs = range(nch) if d == 0 else range(nch - 1, -1, -1)
            for c in chs:
                if dbg >= 2:
                    chunk(c, d, None)
            if dbg != 3:
                continue
            if d == 0:
                yb, byb = T["yb"]
                fw.dma("sp", [byb], [byf_dram], yf_o[:, t0:t0 + tb], yb[:, 0:tb], is_output=True)
            else:
                readout(t0, tb)
    fw.finish()
    return nc


def build_kbr2(NLAT, NCTX, dbg=3, dumps=None):
    NTOT = NLAT + NCTX
    TBK = 512
    nc = bass.Bass("TRN2", target_bir_lowering=False)
    z_d = nc.dram_tensor("z", [5, 128, NTOT], F32, kind="ExternalInput").ap()
    pv_d = nc.dram_tensor("pv", [128, 14], F32, kind="ExternalInput").ap()
    w2_d = nc.dram_tensor("w2", [2, 64, 128], F32, kind="ExternalInput").ap()
    a2_d = nc.dram_tensor("a2", [2, 64, 128], F32, kind="ExternalInput").ap()
    g2_d = nc.dram_tensor("g2", [128, 128], F32, kind="ExternalInput").ap()
    mk_d = nc.dram_tensor("mk", [2, 128, 5, 64], F32, kind="ExternalInput").ap()
    ob1_d = nc.dram_tensor("ob1", [128, 128], F32, kind="ExternalInput").ap()
    sm_d = nc.dram_tensor("sm", [128, 512], F32, kind="ExternalInput").ap()
    id2_d = nc.dram_tensor("id2", [128, 64], F32, kind="ExternalInput").ap()
    yd_o = nc.dram_tensor("yd_o", [128, NTOT], BF16, kind="ExternalOutput").ap()
    yf_o = nc.dram_tensor("yf_o", [128, NTOT], F32, kind="ExternalOutput").ap()
    fw = FW(nc)

    def sb(name, shape, dt=F32):
        return SBT(nc, name, shape, dt), Buf(name)

    pv, bpv = sb("pv_s", [128, 14])
    pd, bpd = sb("pd_s", [128, 16])
    w2s, bw2 = sb("w2_s", [64, 2, 128])
    a2s, ba2 = sb("a2_s", [128, 2, 128])
    g2s, bg2 = sb("g2_s", [128, 128])
    mk, bmk = sb("mk_s", [128, 2, 5, 64])
    ob1, bob1 = sb("ob1_s", [128, 128])
    ob64, bob64 = sb("ob64_s", [128, 128])
    sm, bsm = sb("sm_s", [128, 512])
    id2, bid2 = sb("id2_s", [128, 64])
    ident, bid = make_ident(fw, nc)
    fw.dma("sp", [], [bpv], pv[:], pv_d[:, :])
    fw.dma("sp", [], [bw2], w2s[:], w2_d.rearrange("d k n -> k d n"))
    fw.dma("sp", [], [ba2], a2s[64:128, :, :], a2_d.rearrange("d k n -> k d n"))
    fw.dma("sp", [], [bg2], g2s[:], g2_d[:, :])
    fw.dma("act", [], [bmk], mk[:], mk_d.rearrange("d p a f -> p d a f"))
    fw.dma("act", [], [bob1], ob1[:], ob1_d[:, :])
    fw.dma("act", [], [bsm], sm[:], sm_d[:, :])
    fw.dma("act", [], [bid2], id2[:], id2_d[:, :])
    fw.op("dve", [bob1], [bob64], "tensor_scalar", out=ob64[:], in0=ob1[:], scalar1=1.0 / 64, scalar2=None, op0=ALU.mult)
    fw.op("dve", [bpv], [bpd], "tensor_scalar", out=pd[:, 0:5], in0=pv[:, 0:5], scalar1=-1.0, scalar2=1.0,
          op0=ALU.mult, op1=ALU.add)
    fw.op("dve", [bpv], [bpd], "tensor_scalar", out=pd[:, 5:10], in0=pv[:, 0:5], scalar1=0.5, scalar2=None, op0=ALU.mult)
    fw.op("dve", [bpv], [bpd], "tensor_scalar", out=pd[:, 10:11], in0=pv[:, 10:11], scalar1=-1.0, scalar2=1.0,
          op0=ALU.mult, op1=ALU.add)
    fw.op("dve", [bpv], [bpd], "tensor_scalar", out=pd[:, 11:12], in0=pv[:, 10:11], scalar1=-2.0, scalar2=2.0,
          op0=ALU.mult, op1=ALU.add)

    NZ = 2
    zt = [[sb("zt%d_%d" % (i, q), [128, TBK + 2]) for q in range(5)] for i in range(NZ)]
    zsP = [[sb("zs%d_%d" % (par, q), [128, TBK]) for q in range(5)] for par in range(2)]
    adP = [sb("adP%d" % par, [128, TBK]) for par in range(2)]
    ybP = [sb("ybP%d" % par, [128, TBK]) for par in range(2)]
    names = ["tmp", "aa", "tws", "sg", "ad", "af", "kk", "kk2", "rin", "kkn", "tt", "khat", "bt", "c1", "c0", "cr", "c2",
             "ei", "ee", "en", "eh", "Bt", "Kt", "Bh", "Kh", "yb", "yfb", "yy", "yc", "yc2", "rs2", "sgd", "ks", "pr"]
    T = {n: sb("r_" + n, [128, TBK]) for n in names}
    ARP = [sb("AR%d" % par, [128, TBK // CH, 2, CH]) for par in range(2)]
    gCP = [sb("gC%d" % par, [128, TBK // CH]) for par in range(2)]
    ob_out, bob_out = sb("ob_out", [128, TBK], BF16)
    NCK = TBK // CH
    MP = [[sb("M%d_%d" % (par, c), [128, 5, CH]) for c in range(NCK)] for par in range(2)]
    XP = [[sb("X%d_%d" % (par, c), [128, CH]) for c in range(NCK)] for par in range(2)]
    TMP = [[sb("TM%d_%d" % (par, c), [128, 3, CH]) for c in range(NCK)] for par in range(2)]
    PPc = [[sb("PP%d_%d" % (c, i), [128, 2, CH], BF16) for i in range(2)] for c in range(NCK)]
    Xb = [sb("Xb%d" % c, [128, CH], BF16) for c in range(NCK)]
    W1, bW1 = sb("W1", [128, CH])
    U, bU = sb("U", [128, CH])
    ST = [sb("ST%d" % i, [128, CH]) for i in range(2)]
    pL = [(PST(nc, "pL%d" % i, [128, TBK], F32), Buf(excl=True)) for i in range(2)]
    pAs = [(PST(nc, "pA%d" % i, [128, 512], F32), Buf(excl=True)) for i in range(2)]
    pIs = [(PST(nc, "pI%d" % i, [128, 512], F32), Buf(excl=True)) for i in range(2)]
    pSt = [(PST(nc, "pSt%d" % i, [128, 512], F32), Buf(excl=True)) for i in range(2)]
    cnt = {"z": 0, "l": 0, "m": 0, "x": 0, "t": 0, "s": 0, "st": 0, "a": 0, "i": 0}

    def lora_ps():
        p = pL[cnt["l"] % 2]; cnt["l"] += 1
        return p

    dumped = set()

    def dump(name, ap, b, shape):
        if dumps is None or name in dumped:
            return
        dumped.add(name)
        dt_ = nc.dram_tensor("dbg_" + name, list(shape), F32, kind="ExternalOutput").ap()
        fw.dma("sp", [b], [], dt_, ap, is_output=True)
        dumps.append(name)

    def prep(t0, tb, lo, hi, d, par):
        nch = tb // CH
        zs = zsP[par]
        AR, bAR = ARP[par]
        gC, bgC = gCP[par]
        zi = zt[cnt["z"] % NZ]; cnt["z"] += 1
        a = max(lo, t0 - 1); b = min(hi, t0 + tb + 1)
        for q in range(5):
            z, bz = zi[q]
            if t0 - 1 < lo:
                fw.op("pool", [], [bz], "memset", z[:, 0:1], 0.0)
            if t0 + tb + 1 > hi:
                fw.op("pool", [], [bz], "memset", z[:, tb + 1:tb + 2], 0.0)
            fw.dma("sp" if q % 2 == 0 else "act", [], [bz], z[:, a - (t0 - 1):b - (t0 - 1)], z_d[q, :, a:b])
            tmp, btmp = T["tmp"]; aa, baa = T["aa"]
            fw.op("pool", [bz], [btmp], "tensor_tensor", out=tmp[:, 0:tb], in0=z[:, 0:tb], in1=z[:, 2:tb + 2], op=ALU.add)
            fw.op("dve", [bz, bpd], [baa], "tensor_scalar", out=aa[:, 0:tb], in0=z[:, 1:tb + 1], scalar1=pd[:, q:q + 1],
                  scalar2=None, op0=ALU.mult)
            fw.op("dve", [btmp, baa, bpd], [zs[q][1]], "scalar_tensor_tensor", out=zs[q][0][:, 0:tb], in0=tmp[:, 0:tb],
                  scalar=pd[:, 5 + q:6 + q], in1=aa[:, 0:tb], op0=ALU.mult, op1=ALU.add)
        (zr, bzr), (zk, bzk), (zv, bzv), (zw, bzw), (zg, bzg) = zs
        tws, btws = T["tws"]
        fw.op("act", [bzw], [btws], "activation", out=tws[0:64, 0:tb], in_=zw[0:64, 0:tb], func=AF.Tanh)
        pw, bpw = lora_ps()
        fw.op("pe", [btws, bw2], [bpw], "matmul", pw[:, 0:tb], lhsT=w2s[:, d, :], rhs=tws[0:64, 0:tb], start=True, stop=True)
        sg, bsg = T["sg"]
        fw.op("act", [bpw, bpv], [bsg], "activation", out=sg[:, 0:tb], in_=pw[:, 0:tb], func=AF.Sigmoid,
              bias=pv[:, 5 + d:6 + d])
        pa, bpa = lora_ps()
        fw.op("pe", [bzw, ba2], [bpa], "matmul", pa[:, 0:tb], lhsT=a2s[64:128, d, :], rhs=zw[64:128, 0:tb],
              start=True, stop=True)
        ad, bad = adP[par]
        fw.op("act", [bpa, bpv], [bad], "activation", out=ad[:, 0:tb], in_=pa[:, 0:tb], func=AF.Sigmoid,
              bias=pv[:, 7 + d:8 + d])
        kk, bkk = T["kk"]; kk2, bkk2 = T["kk2"]; rin, brin = T["rin"]; kkn, bkkn = T["kkn"]
        fw.op("dve", [bzk, bpv], [bkk], "tensor_scalar", out=kk[:, 0:tb], in0=zk[:, 0:tb], scalar1=pv[:, 9:10],
              scalar2=None, op0=ALU.mult)
        fw.op("pool", [bkk], [bkk2], "tensor_tensor", out=kk2[:, 0:tb], in0=kk[:, 0:tb], in1=kk[:, 0:tb], op=ALU.mult)
        pss, bpss = lora_ps()
        fw.op("pe", [bkk2, bob1], [bpss], "matmul", pss[:, 0:tb], lhsT=ob1[:], rhs=kk2[:, 0:tb], start=True, stop=True)
        fw.op("act", [bpss], [brin], "activation", out=rin[:, 0:tb], in_=pss[:, 0:tb], func=AF.Sqrt)
        fw.op("dve", [brin], [brin], "tensor_scalar", out=rin[:, 0:tb], in0=rin[:, 0:tb], scalar1=1e-12, scalar2=None,
              op0=ALU.max)
        fw.op("dve", [brin], [brin], "reciprocal", out=rin[:, 0:tb], in_=rin[:, 0:tb])
        fw.op("pool", [bkk, brin], [bkkn], "tensor_tensor", out=kkn[:, 0:tb], in0=kk[:, 0:tb], in1=rin[:, 0:tb], op=ALU.mult)
        tt, btt = T["tt"]; khat, bkhat = T["khat"]; bt, bbt = T["bt"]
        fw.op("dve", [bad, bpv, bpd], [btt], "tensor_scalar", out=tt[:, 0:tb], in0=ad[:, 0:tb], scalar1=pv[:, 10:11],
              scalar2=pd[:, 10:11], op0=ALU.mult, op1=ALU.add)
        fw.op("pool", [bzk, btt], [bkhat], "tensor_tensor", out=khat[:, 0:tb], in0=zk[:, 0:tb], in1=tt[:, 0:tb], op=ALU.mult)
        fw.op("pool", [bkkn, bad], [bbt], "tensor_tensor", out=bt[:, 0:tb], in0=kkn[:, 0:tb], in1=ad[:, 0:tb], op=ALU.mult)
        c1, bc1 = T["c1"]; c0, bc0 = T["c0"]; cr, bcr = T["cr"]; c2, bc2 = T["c2"]
        fw.op("dve", [bsm, bsg], [bc1], "tensor_tensor_scan", out=c1[:, 0:tb], data0=sm[:, 0:tb], data1=sg[:, 0:tb],
              initial=0.0, op0=ALU.mult, op1=ALU.add)
        fw.op("pool", [bc1, bsg], [bc0], "tensor_tensor", out=c0[:, 0:tb], in0=c1[:, 0:tb], in1=sg[:, 0:tb], op=ALU.subtract)
        c1v = c1[:, 0:tb].rearrange("p (c j) -> p c j", j=CH)
        totb = c1v[:, :, CH - 1:CH].to_broadcast([128, nch, CH])
        fw.op("dve", [bc1], [bcr], "tensor_tensor", out=cr[:, 0:tb].rearrange("p (c j) -> p c j", j=CH), in0=totb,
              in1=c1v, op=ALU.subtract)
        if d == 0:
            incl, bincl, excl, bexcl, hn, bhn = c1, bc1, c0, bc0, cr, bcr
        else:
            fw.op("pool", [bcr, bsg], [bc2], "tensor_tensor", out=c2[:, 0:tb], in0=cr[:, 0:tb], in1=sg[:, 0:tb], op=ALU.add)
            incl, bincl, excl, bexcl, hn, bhn = c2, bc2, cr, bcr, c0, bc0
        ei, bei = T["ei"]; ee, bee = T["ee"]; en, ben = T["en"]; eh, beh = T["eh"]
        fw.op("act", [bincl], [bei], "activation", out=ei[:, 0:tb], in_=incl[:, 0:tb], func=AF.Exp, scale=LAM)
        fw.op("act", [bexcl], [bee], "activation", out=ee[:, 0:tb], in_=excl[:, 0:tb], func=AF.Exp, scale=LAM)
        fw.op("act", [bincl], [ben], "activation", out=en[:, 0:tb], in_=incl[:, 0:tb], func=AF.Exp, scale=-LAM)
        fw.op("act", [bhn], [beh], "activation", out=eh[:, 0:tb], in_=hn[:, 0:tb], func=AF.Exp, scale=LAM)
        fw.op("act", [bc1], [bgC], "activation", out=gC[:, 0:nch], in_=c1v[:, :, CH - 1], func=AF.Exp, scale=LAM)
        v3 = lambda ap: ap[:, 0:tb].rearrange("p (c j) -> p c j", j=CH)
        fw.op("dve", [bkkn, bee], [bAR], "scalar_tensor_tensor", out=AR[:, 0:nch, 0, :], in0=v3(kkn), scalar=-1.0,
              in1=v3(ee), op0=ALU.mult, op1=ALU.mult)
        fw.op("pool", [bzr, bei], [bAR], "tensor_tensor", out=AR[:, 0:nch, 1, :], in0=v3(zr), in1=v3(ei), op=ALU.mult)
        for (nm, x0, bx0, e0, be0) in (("Bt", bt, bbt, en, ben), ("Kt", khat, bkhat, en, ben), ("Bh", bt, bbt, eh, beh),
                                       ("Kh", khat, bkhat, eh, beh)):
            eng = "dve" if nm in ("Bt", "Bh") else "pool"
            fw.op(eng, [bx0, be0], [T[nm][1]], "tensor_tensor", out=T[nm][0][:, 0:tb], in0=x0[:, 0:tb], in1=e0[:, 0:tb],
                  op=ALU.mult)

    def stage1(par, c, d):
        cs = slice(c * CH, (c + 1) * CH)
        AR, bAR = ARP[par]
        Bt, bBt = T["Bt"]; Kt, bKt = T["Kt"]
        pa, bpa = pAs[cnt["a"] % 2]; cnt["a"] += 1
        for h in range(2):
            hs = slice(h * 64, (h + 1) * 64)
            arf = AR[hs, c, :, :].rearrange("p a j -> p (a j)")
            fw.op("pe", [bBt, bAR], [bpa], "matmul", pa[hs, 0:128], lhsT=Bt[hs, cs], rhs=arf, start=True, stop=True)
            fw.op("pe", [bKt, bAR], [bpa], "matmul", pa[hs, 128:256], lhsT=Kt[hs, cs], rhs=arf, start=True, stop=True)
            fw.op("pe", [bBt, bAR], [bpa], "matmul", pa[hs, 256:320], lhsT=AR[hs, c, 0, :], rhs=Bt[hs, cs],
                  start=True, stop=True)
        M, bM = MP[par][c]
        fw.op("dve", [bpa, bmk], [bM], "tensor_tensor", out=M[:], in0=pa[:, 0:320].rearrange("p (a j) -> p a j", j=CH),
              in1=mk[:, d, :, :], op=ALU.mult)
        X, bX = XP[par][c]
        fw.op("pool", [bM, bid2], [bX], "tensor_tensor", out=X[:], in0=M[:, 0, :], in1=id2[:], op=ALU.add)
        xb, bxb = Xb[c]
        fw.op("pool", [bX], [bxb], "tensor_copy", out=xb[:], in_=X[:])
        p0, bp0 = PPc[c][0]
        fw.op("act", [bM], [bp0], "activation", out=p0[:], in_=M[:, 0:5:4, :], func=AF.Copy)

    def inv_sq(par, chunks, i):
        for c in chunks:
            t_, bt_ = PPc[c][(i - 1) % 2]
            ip, ipt = 0, 1
            pi_, bpi = pIs[cnt["i"] % 2]; cnt["i"] += 1
            for h in range(2):
                hs = slice(h * 64, (h + 1) * 64)
                if i < 5:
                    fw.op("pe", [bt_], [bpi], "matmul", pi_[hs, 0:64], lhsT=t_[hs, ipt, :], rhs=t_[hs, ip, :],
                          start=True, stop=True)
                fw.op("pe", [bt_], [bpi], "matmul", pi_[hs, 64:128], lhsT=t_[hs, ip, :], rhs=t_[hs, ipt, :],
                      start=True, stop=True)
            Pn, bPn = PPc[c][i % 2]
            if i < 5:
                fw.op("act", [bpi], [bPn], "activation", out=Pn[:], in_=pi_[:, 0:128].rearrange("p (a j) -> p a j", j=CH),
                      func=AF.Copy)
            else:
                fw.op("act", [bpi], [bPn], "activation", out=Pn[:, 1, :], in_=pi_[:, 64:128], func=AF.Copy)

    def inv_px(par, chunks, i):
        for c in chunks:
            Pn, bPn = PPc[c][i % 2]
            X, bX = XP[par][c]
            xb, bxb = Xb[c]
            pi_, bpi = pIs[cnt["i"] % 2]; cnt["i"] += 1
            for h in range(2):
                hs = slice(h * 64, (h + 1) * 64)
                fw.op("pe", [bPn, bxb], [bpi], "matmul", pi_[hs, 128:192], lhsT=Pn[hs, 1, :], rhs=xb[hs, :],
                      start=True, stop=True)
            fw.op("dve", [bpi, bX], [bX], "tensor_tensor", out=X[:], in0=pi_[:, 128:192], in1=X[:], op=ALU.add)
            if i < 5:
                fw.op("pool", [bX], [bxb], "tensor_copy", out=xb[:], in_=X[:])

    def transposes(par, c):
        cs = slice(c * CH, (c + 1) * CH)
        Bh, bBh = T["Bh"]; Kh, bKh = T["Kh"]
        zv, bzv = zsP[par][2]
        ptt, bptt = pAs[cnt["a"] % 2]; cnt["a"] += 1
        for h in range(2):
            hs = slice(h * 64, (h + 1) * 64)
            fw.op("pe", [bzv, bid], [bptt], "matmul", ptt[hs, 0:64], lhsT=zv[hs, cs], rhs=ident[hs, hs], start=True, stop=True)
            fw.op("pe", [bBh, bid], [bptt], "matmul", ptt[hs, 64:128], lhsT=Bh[hs, cs], rhs=ident[hs, hs], start=True, stop=True)
            fw.op("pe", [bKh, bid], [bptt], "matmul", ptt[hs, 128:192], lhsT=Kh[hs, cs], rhs=ident[hs, hs], start=True, stop=True)
        tm, btm = TMP[par][c]
        fw.op("act", [bptt], [btm], "activation", out=tm[:], in_=ptt[:, 0:192].rearrange("p (a j) -> p a j", j=CH),
              func=AF.Copy)

    def pre_slices(par, nch, d):
        chunks = list(range(nch))
        grp = [chunks[k:k + 2] for k in range(0, nch, 2)]
        hv = [chunks[:(nch + 1) // 2], chunks[(nch + 1) // 2:]]
        sl = []
        for g in grp:
            sl.append(lambda g=g: [stage1(par, c, d) for c in g])
        for i in range(1, 6):
            for g in hv:
                if g:
                    sl.append(lambda i=i, g=g: inv_sq(par, g, i))
            for g in hv:
                if g:
                    sl.append(lambda i=i, g=g: inv_px(par, g, i))
        for g in grp:
            sl.append(lambda g=g: [transposes(par, c) for c in g])
        return sl

    def state(par, c, d):
        cs = slice(c * CH, (c + 1) * CH)
        AR, bAR = ARP[par]
        gC, bgC = gCP[par]
        M, bM = MP[par][c]
        X, bX = XP[par][c]
        tm, btm = TMP[par][c]
        S0, bS0 = ST[cnt["st"] % 2]
        S1, bS1 = ST[(cnt["st"] + 1) % 2]
        cnt["st"] += 1
        ps, bps = pSt[cnt["s"] % 2]; cnt["s"] += 1
        for h in range(2):
            hs = slice(h * 64, (h + 1) * 64)
            fw.op("pe", [bAR, bS0], [bps], "matmul", ps[hs, 0:64], lhsT=AR[hs, c, 0, :], rhs=S0[hs, :], start=True, stop=False)
            fw.op("pe", [bM, btm], [bps], "matmul", ps[hs, 0:64], lhsT=M[hs, 2, :], rhs=tm[hs, 0, :], start=False, stop=True)
        fw.op("act", [bps], [bW1], "activation", out=W1[:], in_=ps[:, 0:64], func=AF.Copy)
        yield
        for h in range(2):
            hs = slice(h * 64, (h + 1) * 64)
            fw.op("pe", [bX, bW1], [bps], "matmul", ps[hs, 64:128], lhsT=X[hs, :], rhs=W1[hs, :], start=True, stop=True)
        fw.op("act", [bps], [bU], "activation", out=U[:], in_=ps[:, 64:128], func=AF.Copy)
        yield
        for h in range(2):
            hs = slice(h * 64, (h + 1) * 64)
            fw.op("pe", [btm, bU], [bps], "matmul", ps[hs, 128:192], lhsT=tm[hs, 1, :], rhs=U[hs, :], start=True, stop=False)
            fw.op("pe", [btm], [bps], "matmul", ps[hs, 128:192], lhsT=tm[hs, 2, :], rhs=tm[hs, 0, :], start=False, stop=True)
        for h in range(2):
            hs = slice(h * 64, (h + 1) * 64)
            fw.op("pe", [bS0, bAR], [bps], "matmul", ps[hs, 192:256], lhsT=S0[hs, :], rhs=AR[hs, c, 1, :], start=True, stop=False)
            fw.op("pe", [bU, bM], [bps], "matmul", ps[hs, 192:256], lhsT=U[hs, :], rhs=M[hs, 1, :], start=False, stop=False)
            fw.op("pe", [btm, bM], [bps], "matmul", ps[hs, 192:256], lhsT=tm[hs, 0, :], rhs=M[hs, 3, :], start=False, stop=True)
        fw.op("dve", [bS0, bgC, bps], [bS1], "scalar_tensor_tensor", out=S1[:], in0=S0[:], scalar=gC[:, c:c + 1],
              in1=ps[:, 128:192], op0=ALU.mult, op1=ALU.add)
        yb, byb = ybP[par]
        fw.op("act", [bps], [byb], "activation", out=yb[:, cs], in_=ps[:, 192:256], func=AF.Copy)
        yield

    def readout(t0, tb, par):
        zs = zsP[par]
        (zr, bzr), (zk, bzk), (zv, bzv), (zw, bzw), (zg, bzg) = zs
        yb, byb = ybP[par]; yfb, byfb = T["yfb"]; yy, byy = T["yy"]; yc, byc = T["yc"]; yc2, byc2 = T["yc2"]
        rs2, brs2 = T["rs2"]; sgd, bsgd = T["sgd"]; ks, bks = T["ks"]; pr, bpr = T["pr"]; af, baf = T["af"]
        ad, bad = adP[par]
        fw.dma("sp", [byf_dram], [byfb], yfb[:, 0:tb], yf_o[:, t0:t0 + tb])
        fw.op("dve", [byb, byfb], [byy], "tensor_tensor", out=yy[:, 0:tb], in0=yb[:, 0:tb], in1=yfb[:, 0:tb], op=ALU.add)
        pm, bpm = lora_ps()
        fw.op("pe", [byy, bob64], [bpm], "matmul", pm[:, 0:tb], lhsT=ob64[:], rhs=yy[:, 0:tb], start=True, stop=True)
        fw.op("dve", [byy, bpm], [byc], "tensor_tensor", out=yc[:, 0:tb], in0=yy[:, 0:tb], in1=pm[:, 0:tb], op=ALU.subtract)
        fw.op("pool", [byc], [byc2], "tensor_tensor", out=yc2[:, 0:tb], in0=yc[:, 0:tb], in1=yc[:, 0:tb], op=ALU.mult)
        pvv, bpvv = lora_ps()
        fw.op("pe", [byc2, bob64], [bpvv], "matmul", pvv[:, 0:tb], lhsT=ob64[:], rhs=yc2[:, 0:tb], start=True, stop=True)
        fw.op("dve", [bpvv], [brs2], "tensor_scalar", out=rs2[:, 0:tb], in0=pvv[:, 0:tb], scalar1=GN_EPS, scalar2=None,
              op0=ALU.add)
        fw.op("act", [brs2], [brs2], "activation", out=rs2[:, 0:tb], in_=rs2[:, 0:tb], func=AF.Sqrt)
        fw.op("dve", [brs2], [brs2], "reciprocal", out=rs2[:, 0:tb], in_=rs2[:, 0:tb])
        fw.op("pool", [byc, brs2], [byc], "tensor_tensor", out=yc[:, 0:tb], in0=yc[:, 0:tb], in1=rs2[:, 0:tb], op=ALU.mult)
        fw.op("dve", [byc, bpv], [byc], "tensor_scalar", out=yc[:, 0:tb], in0=yc[:, 0:tb], scalar1=pv[:, 12:13],
              scalar2=pv[:, 13:14], op0=ALU.mult, op1=ALU.add)
        pa, bpa = lora_ps()
        fw.op("pe", [bzw, ba2], [bpa], "matmul", pa[:, 0:tb], lhsT=a2s[64:128, 0, :], rhs=zw[64:128, 0:tb],
              start=True, stop=True)
        fw.op("act", [bpa, bpv], [baf], "activation", out=af[:, 0:tb], in_=pa[:, 0:tb], func=AF.Sigmoid, bias=pv[:, 7:8])
        fw.op("pool", [baf, bad], [baf], "tensor_tensor", out=af[:, 0:tb], in0=af[:, 0:tb], in1=ad[:, 0:tb], op=ALU.add)
        fw.op("dve", [baf, bpv, bpd], [baf], "tensor_scalar", out=af[:, 0:tb], in0=af[:, 0:tb], scalar1=pv[:, 10:11],
              scalar2=pd[:, 11:12], op0=ALU.mult, op1=ALU.add)
        fw.op("pool", [bzk, baf], [bks], "tensor_tensor", out=ks[:, 0:tb], in0=zk[:, 0:tb], in1=af[:, 0:tb], op=ALU.mult)
        fw.op("dve", [bzr, bpv, bks], [bpr], "scalar_tensor_tensor", out=pr[:, 0:tb], in0=zr[:, 0:tb], scalar=pv[:, 11:12],
              in1=ks[:, 0:tb], op0=ALU.mult, op1=ALU.mult)
        pb, bpb = lora_ps()
        fw.op("pe", [bpr, bob1], [bpb], "matmul", pb[:, 0:tb], lhsT=ob1[:], rhs=pr[:, 0:tb], start=True, stop=True)
        fw.op("dve", [bpb, bzv], [bks], "tensor_tensor", out=ks[:, 0:tb], in0=pb[:, 0:tb], in1=zv[:, 0:tb], op=ALU.mult)
        fw.op("pool", [bks, byc], [byc], "tensor_tensor", out=yc[:, 0:tb], in0=yc[:, 0:tb], in1=ks[:, 0:tb], op=ALU.add)
        fw.op("act", [bzg], [bsgd], "activation", out=sgd[:, 0:tb], in_=zg[:, 0:tb], func=AF.Sigmoid)
        pg, bpg = lora_ps()
        fw.op("pe", [bsgd, bg2], [bpg], "matmul", pg[:, 0:tb], lhsT=g2s[:], rhs=sgd[:, 0:tb], start=True, stop=True)
        fw.op("dve", [bpg, byc], [bob_out], "tensor_tensor", out=ob_out[:, 0:tb], in0=pg[:, 0:tb], in1=yc[:, 0:tb], op=ALU.mult)
        fw.dma("sp", [bob_out], [], yd_o[:, t0:t0 + tb], ob_out[:, 0:tb], is_output=True)

    byf_dram = Buf("yf_dram")
    lat_blocks = [(t0, min(TBK, NLAT - t0), 0, NLAT) for t0 in range(0, NLAT, TBK)]
    ctx_blocks = [(NLAT + t0, min(TBK, NCTX - t0), NLAT, NTOT) for t0 in range(0, NCTX, TBK)]
    for d in range(2):
        S0, bS0 = ST[cnt["st"] % 2]
        fw.op("pool", [], [bS0], "memset", S0[:], 0.0)
        if d == 0:
            order = ctx_blocks + lat_blocks
        else:
            order = ctx_blocks[::-1] + lat_blocks[::-1]
        t0, tb, lo, hi = order[0]
        prep(t0, tb, lo, hi, d, 0)
        for f in pre_slices(0, tb // CH, d):
            f()
        for n, (t0, tb, lo, hi) in enumerate(order):
            par = n % 2
            sl = []
            if n + 1 < len(order):
                t0n, tbn, lon, hin = order[n + 1]
                prep(t0n, tbn, lon, hin, d, par ^ 1)
                sl = pre_slices(par ^ 1, tbn // CH, d)
            nch = tb // CH
            chs = list(range(nch)) if d == 0 else list(range(nch - 1, -1, -1))
            per = -(-len(sl) // (3 * nch)) if sl else 0
            k = 0
            for c in chs:
                for _ in state(par, c, d):
                    for _ in range(per):
                        if k < len(sl):
                            sl[k](); k += 1
            while k < len(sl):
                sl[k](); k += 1
            if d == 0:
                yb, byb = ybP[par]
                fw.dma("sp", [byb], [byf_dram], yf_o[:, t0:t0 + tb], yb[:, 0:tb], is_output=True)
            else:
                readout(t0, tb, par)
    fw.finish()
    return nc


def natten_patterns(rows):
    def rs(r):
        return int(np.clip(r - 4, 0, rows - 8))
    pats = {}
    per_pair = []
    nkl = rows // 2
    for r in range(0, rows, 2):
        ws = rs(r)
        kb0 = ws // 2
        nblk = min(5, nkl - kb0)
        key = (rs(r) - r, rs(r + 1) - (r + 1), nblk)
        if key not in pats:
            pats[key] = (len(pats), r)
        per_pair.append((pats[key][0], kb0, nblk))
    return pats, per_pair


def natten_bias_tables(rpb2, rows):
    pats, _ = natten_patterns(rows)
    def rs(r):
        return int(np.clip(r - 4, 0, rows - 8))
    tab = np.full((len(pats), 2, 128, 5, 128), -30000.0, np.float32)
    p = np.arange(128)
    f = np.arange(128)
    kc = (p % 64)[:, None]
    c = (f % 64)[None, :]
    csc = np.clip(c - 8, 0, 48)
    colok = (kc >= csc) & (kc < csc + 16)
    dcol = kc - c + 15
    for key, (pid, r) in pats.items():
        ws = rs(r)
        for blk in range(key[2]):
            keyrow = (ws + 2 * blk + p // 64)[:, None]
            qrow = (r + f // 64)[None, :]
            rsq = np.clip(qrow - 4, 0, rows - 8)
            ok = colok & (keyrow >= rsq) & (keyrow < rsq + 8)
            drow = keyrow - qrow + 7
            idx_r = np.clip(drow, 0, 14)
            idx_c = np.clip(dcol, 0, 30)
            for hh in range(2):
                vals = rpb2[hh][idx_r, idx_c]
                tab[pid, hh, :, blk, :] = np.where(ok, vals, np.float32(-30000.0))
    return tab


def build_kbn(NLAT, NCTX):
    NTOT = NLAT + NCTX
    NKB = NTOT // 128
    NKL = NLAT // 128
    rows = NLAT // 64
    pats, per_pair = natten_patterns(rows)
    NP = len(pats)
    PC = 2048
    nc = bass.Bass("TRN2", target_bir_lowering=False)
    q_d = nc.dram_tensor("q", [128, NTOT], BF16, kind="ExternalInput").ap()
    k_d = nc.dram_tensor("k", [128, NTOT], BF16, kind="ExternalInput").ap()
    v_d = [nc.dram_tensor("v%d" % m, [128, NKB * 65], BF16, kind="ExternalInput").ap() for m in range(2)]
    tab_d = nc.dram_tensor("tab", [NP, 2, 128, 5, 128], F32, kind="ExternalInput").ap()
    sel_d = nc.dram_tensor("sel", [65, 64], F32, kind="ExternalInput").ap()
    y_o = nc.dram_tensor("y_o", [2, 64, NTOT], BF16, kind="ExternalOutput").ap()
    fw = FW(nc)
    npc = max(1, NLAT // PC)
    pcs = [(i * PC, PC if i < npc - 1 else NTOT - i * PC) for i in range(npc)]
    Q = SBT(nc, "Q", [128, NTOT], BF16); bQ = [Buf() for _ in pcs]
    K = SBT(nc, "K", [128, NTOT], BF16); bK = [Buf() for _ in pcs]
    V = [SBT(nc, "V%d" % m, [128, NKB, 65], BF16) for m in range(2)]; bV = [[Buf() for _ in pcs] for m in range(2)]
    E = SBT(nc, "E", [128, NP * 2, 5, 128], F32); bE = Buf()
    sel = SBT(nc, "sel_s", [65, 64], F32); bsel = Buf()
    fw.dma("sp", [], [bsel], sel[:], sel_d[:, :])
    for pid in range(NP):
        for hh in range(2):
            fw.dma("sp" if hh == 0 else "act", [], [bE], E[:, pid * 2 + hh, :, :], tab_d[pid, hh])
    for pid in range(NP):
        fw.op("act", [bE], [bE], "activation", out=E[:, pid * 2:pid * 2 + 2, :, :], in_=E[:, pid * 2:pid * 2 + 2, :, :],
              func=AF.Exp)
    for pi, (c0, cw) in enumerate(pcs):
        fw.dma("sp", [], [bK[pi]], K[:, c0:c0 + cw], k_d[:, c0:c0 + cw])
        fw.dma("act", [], [bQ[pi]], Q[:, c0:c0 + cw], q_d[:, c0:c0 + cw])
        kb0 = c0 // 128; kbn = cw // 128
        for m in range(2):
            fw.dma("pool", [], [bV[m][pi]], V[m][:, kb0:kb0 + kbn, :],
                   v_d[m][:, kb0 * 65:(kb0 + kbn) * 65].rearrange("p (k d) -> p k d", d=65))

    def pc_of(col):
        return min(col // PC, npc - 1)

    pS = [PST(nc, "pS%d" % i, [128, 8, 128], F32) for i in range(2)]; bpS = [Buf(excl=True) for _ in range(2)]
    pO = [PST(nc, "pO%d" % i, [128, 512], F32) for i in range(2)]; bpO = [Buf(excl=True), Buf(excl=True)]
    pB = PST(nc, "pB", [128, 512], F32); bpB = Buf(excl=True)
    PTf = [SBT(nc, "PTf%d" % i, [128, 5, 128], F32) for i in range(2)]; bPTf = [Buf(), Buf()]
    PTb = [SBT(nc, "PTb%d" % i, [128, 7, 128], BF16) for i in range(2)]; bPTb = [Buf(), Buf()]
    osb = [SBT(nc, "osb%d" % i, [65, 512], F32) for i in range(2)]; bosb = [Buf(), Buf()]
    rden = SBT(nc, "rden", [64, 512], F32); brden = Buf()
    yo = [SBT(nc, "yo%d" % i, [64, 512], BF16) for i in range(2)]; byo = [Buf(), Buf()]
    cnt = {"s": 0, "o": 0}
    nctxb = NCTX // 128

    def epilogue(po, bpo, oi, hh, q0, qn):
        ob = osb[oi % 2]; bob = bosb[oi % 2]
        fw.op("act", [bpo], [bob], "activation", out=ob[:, 0:qn], in_=po[0:65, 0:qn], func=AF.Copy)
        fw.op("pe", [bob, bsel], [bpB], "matmul", pB[0:64, 0:qn], lhsT=sel[:], rhs=ob[:, 0:qn], start=True, stop=True)
        fw.op("dve", [bpB], [brden], "reciprocal", out=rden[:, 0:qn], in_=pB[0:64, 0:qn])
        y = yo[oi % 2]; by = byo[oi % 2]
        fw.op("dve", [bob, brden], [by], "tensor_tensor", out=y[:, 0:qn], in0=ob[0:64, 0:qn], in1=rden[:, 0:qn],
              op=ALU.mult)
        fw.dma("sp", [by], [], y_o[hh, :, q0:q0 + qn], y[:, 0:qn], is_output=True)

    for hh in range(2):
        hs = slice(hh * 64, (hh + 1) * 64)
        npairs = rows // 2
        for g0 in range(0, npairs, 4):
            oi = cnt["o"]; cnt["o"] += 1
            po = pO[oi % 2]; bpo = bpO[oi % 2]
            gn = min(4, npairs - g0)
            for sl in range(gn):
                pr = g0 + sl
                pid, kb0, nblk = per_pair[pr]
                q0 = pr * 128
                si = cnt["s"]; cnt["s"] += 1
                ps = pS[si % 2]; bps = bpS[si % 2]
                ptf = PTf[si % 2]; bptf = bPTf[si % 2]
                ptb = PTb[si % 2]; bptb = bPTb[si % 2]
                bq = bQ[pc_of(q0)]
                for blk in range(nblk):
                    kb = kb0 + blk
                    fw.op("pe", [bK[pc_of(kb * 128)], bq], [bps], "matmul", ps[:, blk, :],
                          lhsT=K[hs, kb * 128:(kb + 1) * 128], rhs=Q[hs, q0:q0 + 128], start=True, stop=True)
                for i in range(nctxb):
                    kb = NKL + i
                    fw.op("pe", [bK[pc_of(kb * 128)], bq], [bps], "matmul", ps[:, 5 + i, :],
                          lhsT=K[hs, kb * 128:(kb + 1) * 128], rhs=Q[hs, q0:q0 + 128], start=True, stop=True)
                fw.op("act", [bps], [bptf], "activation", out=ptf[:, 0:nblk, :], in_=ps[:, 0:nblk, :], func=AF.Exp,
                      scale=SCALE)
                fw.op("act", [bps], [bptb], "activation", out=ptb[:, 5:5 + nctxb, :], in_=ps[:, 5:5 + nctxb, :],
                      func=AF.Exp, scale=SCALE)
                fw.op("dve", [bptf, bE], [bptb], "tensor_tensor", out=ptb[:, 0:nblk, :], in0=ptf[:, 0:nblk, :],
                      in1=E[:, pid * 2 + hh, 0:nblk, :], op=ALU.mult)
                nmm = nblk + nctxb
                n = 0
                for blk in range(nblk):
                    kb = kb0 + blk
                    fw.op("pe", [bptb, bV[hh][pc_of(kb * 128)]], [bpo], "matmul", po[0:65, sl * 128:(sl + 1) * 128],
                          lhsT=V[hh][:, kb, 0:65], rhs=ptb[:, blk, :], start=(n == 0), stop=(n == nmm - 1))
                    n += 1
                for i in range(nctxb):
                    kb = NKL + i
                    fw.op("pe", [bptb, bV[hh][pc_of(kb * 128)]], [bpo], "matmul", po[0:65, sl * 128:(sl + 1) * 128],
                          lhsT=V[hh][:, kb, 0:65], rhs=ptb[:, 5 + i, :], start=(n == 0), stop=(n == nmm - 1))
                    n += 1
            epilogue(po, bpo, oi, hh, g0 * 128, gn * 128)
        oi = cnt["o"]; cnt["o"] += 1
        po = pO[oi % 2]; bpo = bpO[oi % 2]
        si = cnt["s"]; cnt["s"] += 1
        ps = pS[si % 2]; bps = bpS[si % 2]
        ptb = PTb[si % 2]; bptb = bPTb[si % 2]
        for i in range(nctxb):
            kb = NKL + i
            for qi in range(nctxb):
                fw.op("pe", [bK[pc_of(kb * 128)], bQ[pc_of(NLAT)]], [bps], "matmul", ps[:, i * nctxb + qi, :],
                      lhsT=K[hs, kb * 128:(kb + 1) * 128], rhs=Q[hs, NLAT + qi * 128:NLAT + (qi + 1) * 128],
                      start=True, stop=True)
        fw.op("act", [bps], [bptb], "activation", out=ptb[:, 0:nctxb * nctxb, :], in_=ps[:, 0:nctxb * nctxb, :],
              func=AF.Exp, scale=SCALE)
        for qi in range(nctxb):
            for i in range(nctxb):
                kb = NKL + i
                fw.op("pe", [bptb, bV[hh][pc_of(kb * 128)]], [bpo], "matmul", po[0:65, qi * 128:(qi + 1) * 128],
                      lhsT=V[hh][:, kb, 0:65], rhs=ptb[:, i * nctxb + qi, :], start=(i == 0), stop=(i == nctxb - 1))
        epilogue(po, bpo, oi, hh, NLAT, NCTX)
    fw.finish()
    return nc


def build_k0():
    NCOL = 3072
    nc = bass.Bass("TRN2", target_bir_lowering=False)
    cT_d = nc.dram_tensor("cT", [128, 8, 3], F32, kind="ExternalInput").ap()
    wm_d = nc.dram_tensor("wm", [D, NCOL], F32, kind="ExternalInput").ap()
    bm_d = nc.dram_tensor("bm", [3, NCOL], F32, kind="ExternalInput").ap()
    mo_d = nc.dram_tensor("mo", [3, NCOL], F32, kind="ExternalOutput").ap()
    fw = FW(nc)
    cT = SBT(nc, "cT_s", [128, 8, 3], F32); bc = Buf()
    sT = SBT(nc, "sT_s", [128, 8, 3], F32); bs = Buf()
    wm = SBT(nc, "wm_s", [128, 8, NCOL], F32); bw = [Buf() for _ in range(8)]
    bm = SBT(nc, "bm_s", [3, NCOL], F32); bb = Buf()
    mo = SBT(nc, "mo_s", [3, NCOL], F32); bmo = Buf()
    ps = [PST(nc, "ps%d" % i, [128, 512], F32) for i in range(2)]; bps = [Buf(excl=True), Buf(excl=True)]
    fw.dma("sp", [], [bc], cT[:], cT_d[:, :, :])
    fw.dma("sp", [], [bb], bm[:], bm_d[:, :])
    for k in range(8):
        fw.dma("sp" if k % 2 == 0 else "act", [], [bw[k]], wm[:, k, :], wm_d[k * 128:(k + 1) * 128, :])
    fw.op("act", [bc], [bs], "activation", out=sT[:], in_=cT[:], func=AF.Silu)
    for cb in range(NCOL // 512):
        p = ps[cb % 2]; bp = bps[cb % 2]
        for k in range(8):
            fw.op("pe", [bs, bw[k]], [bp], "matmul", p[0:3, :], lhsT=sT[:, k, :], rhs=wm[:, k, cb * 512:(cb + 1) * 512],
                  start=(k == 0), stop=(k == 7))
        fw.op("dve", [bp, bb], [bmo], "tensor_tensor", out=mo[:, cb * 512:(cb + 1) * 512], in0=p[0:3, :],
              in1=bm[:, cb * 512:(cb + 1) * 512], op=ALU.add)
    fw.dma("sp", [bmo], [], mo_d[:, :], mo[:], is_output=True)
    fw.finish()
    return nc


_NC_CACHE = {}


def _get_nc(key, builder):
    if key not in _NC_CACHE:
        _NC_CACHE[key] = builder()
    return _NC_CACHE[key]


def _run(key, builder, in_maps):
    nc = _get_nc(key, builder)
    res = run_bass_kernel_spmd(nc, in_maps, core_ids=list(range(NCORES)))
    return res.results


def _fmcol(v):
    return np.asarray(v, np.float32).reshape(8, 128).T


def _c(a):
    return np.ascontiguousarray(a)


def kernel(x, c, ctx, c_ctx, w_mod, b_mod, norm_mix, norm_ffn, w_in_even, w_out_even, q_norm_a, k_norm_a,
           sink_b, w_in_odd, w_out_odd, rpb_c, shift_mu, decay_w0, decay_w2, iclr_a0, iclr_a2, gate_g2,
           k_k, k_a, r_k, ln_x_w, ln_x_b, w_ffn_in, w_ffn_out, norm_out):
    f32 = lambda a: np.asarray(a, np.float32)
    x = f32(x); c = f32(c); ctx = f32(ctx); c_ctx = f32(c_ctx)
    w_mod = f32(w_mod); b_mod = f32(b_mod)
    B = 2
    NQ = 4
    TL = SEQ // NQ
    NTC = TL + CTX
    NTOT = SEQ + CTX
    cvec = np.stack([c[0], c[1], c_ctx], 1)
    cT = _c(cvec.reshape(8, 128, 3).transpose(1, 0, 2))
    ims = []
    for i in range(NCORES):
        l, hf = i // 2, i % 2
        ims.append({"cT": cT, "wm": _c(w_mod[l][:, hf * 3072:(hf + 1) * 3072]),
                    "bm": _c(np.broadcast_to(b_mod[l][hf * 3072:(hf + 1) * 3072], (3, 3072)))})
    r0 = _run("k0", build_k0, ims)
    mods = np.zeros((DEPTH, 3, 6 * D), np.float32)
    for i in range(NCORES):
        l, hf = i // 2, i % 2
        mods[l][:, hf * 3072:(hf + 1) * 3072] = r0[i]["mo"]

    def mv(l, g, which):
        return mods[l, g, which * D:(which + 1) * D]

    xs = [[_c(x[b, q * TL:(q + 1) * TL]) for q in range(NQ)] for b in range(B)]
    cx = [_c(ctx[b]) for b in range(B)]
    rm, ob = rope_consts()
    masks = band_masks()
    sel = sel_const()
    mk, ob1, sm, id2 = rwkv_consts()
    out = None
    for l in range(DEPTH):
        even = (l % 2 == 0)
        li = l // 2
        ims = []
        for b in range(B):
            for q in range(NQ):
                fmv = _c(np.concatenate([_fmcol(norm_mix[l]), _fmcol(mv(l, b, 1)), _fmcol(mv(l, b, 0)),
                                         _fmcol(mv(l, 2, 1)), _fmcol(mv(l, 2, 0))], 1))
                d = {"x": _c(np.concatenate([xs[b][q], cx[b]], 0)), "fmv": fmv}
                if even:
                    d["w_in"] = f32(w_in_even[li])
                    pos = np.concatenate([np.arange(TL) + q * TL, np.zeros(CTX, np.int64)])
                    isc = np.arange(NTC) >= TL
                    d["cs"] = rope_tables(pos, isc)
                    d["gv"] = _c(np.stack([np.tile(f32(q_norm_a[li]), 2), np.tile(f32(k_norm_a[li]), 2)], 1))
                    d["rm"] = rm
                    d["ob"] = ob
                else:
                    d["w_in"] = f32(w_in_odd[li])
                ims.append(d)
        ra = _run("ka_even" if even else "ka_odd", lambda: build_ka(TL, CTX, even), ims)

        def gather_fm(name, b):
            parts = [ra[b * NQ + q][name][:, :, :TL] for q in range(NQ)] + [ra[b * NQ][name][:, :, TL:]]
            return np.concatenate(parts, axis=2)

        def gather_tm(name, b):
            parts = [ra[b * NQ + q][name][:TL] for q in range(NQ)] + [ra[b * NQ][name][TL:]]
            return np.concatenate(parts, axis=0)

        yT = [np.zeros((D, NTOT), NPBF) for _ in range(B)]
        if even:
            ims = []
            for b in range(B):
                qk = gather_fm("qk_o", b)
                v = gather_tm("v_o", b)
                for j in range(4):
                    kvh = j // 2
                    ka = qk[4][kvh * 64:(kvh + 1) * 64]
                    kb = qk[9][kvh * 64:(kvh + 1) * 64]
                    ims.append({"q0": _c(qk[j]), "k0": _c(np.concatenate([ka, ka], 0)),
                                "v0": vaug_layout(_c(v[:, kvh * 64:(kvh + 1) * 64])),
                                "q1": _c(qk[5 + j]), "k1": _c(np.concatenate([kb, kb], 0)),
                                "v1": vaug_layout(_c(v[:, 128 + kvh * 64:128 + (kvh + 1) * 64])),
                                "mask": masks, "sel": sel,
                                "sink": _c(np.broadcast_to(f32(sink_b[li])[2 * j:2 * j + 2], (64, 2)))})
            rb = _run("kbe", lambda: build_kbe(SEQ, CTX), ims)
            for b in range(B):
                for j in range(4):
                    yo = rb[b * 4 + j]["y_o"]
                    for hh in range(2):
                        h = 2 * j + hh
                        yT[b][h * 64:(h + 1) * 64] = yo[hh]
                        yT[b][512 + h * 64:512 + (h + 1) * 64] = yo[2 + hh]
        else:
            imn = []
            imr = []
            hcs = [slice(j * 128, (j + 1) * 128) for j in range(4)]
            mu = f32(shift_mu[li])
            for b in range(B):
                qk = gather_fm("qk_o", b)
                v = gather_tm("v_o", b)
                zd = gather_fm("zd_o", b)
                for j in range(4):
                    hc = hcs[j]
                    imn.append({"q": _c(qk[j]), "k": _c(qk[4 + j]),
                                "v0": vaug_layout(_c(v[:, (2 * j) * 64:(2 * j + 1) * 64])),
                                "v1": vaug_layout(_c(v[:, (2 * j + 1) * 64:(2 * j + 2) * 64])),
                                "tab": natten_bias_tables(f32(rpb_c[li])[2 * j:2 * j + 2], SEQ // 64), "sel": sel})
                    pv = np.stack([mu[j * 128:(j + 1) * 128], mu[512 + j * 128:512 + (j + 1) * 128],
                                   mu[1024 + j * 128:1024 + (j + 1) * 128], mu[1536:1664], mu[1664:1792],
                                   f32(decay_w0[li])[0][hc], f32(decay_w0[li])[1][hc], f32(iclr_a0[li])[0][hc],
                                   f32(iclr_a0[li])[1][hc], f32(k_k[li])[hc], f32(k_a[li])[hc],
                                   f32(r_k[li]).reshape(-1)[hc], f32(ln_x_w[li])[hc], f32(ln_x_b[li])[hc]], 1)
                    imr.append({"z": _c(np.stack([zd[j], zd[4 + j], zd[8 + j], zd[12], zd[13]])),
                                "pv": _c(pv.astype(np.float32)), "w2": _c(f32(decay_w2[li])[:, :, hc]),
                                "a2": _c(f32(iclr_a2[li])[:, :, hc]), "g2": _c(f32(gate_g2[li])[:, hc]),
                                "mk": mk, "ob1": ob1, "sm": sm, "id2": id2})
            rn = _run("kbn", lambda: build_kbn(SEQ, CTX), imn)
            rr = _run("kbr", lambda: build_kbr2(SEQ, CTX), imr)
            for b in range(B):
                for j in range(4):
                    yo = rn[b * 4 + j]["y_o"]
                    for hh in range(2):
                        h = 2 * j + hh
                        yT[b][h * 64:(h + 1) * 64] = yo[hh]
                    yT[b][512 + j * 128:512 + (j + 1) * 128] = rr[b * 4 + j]["yd_o"]
        final = (l == DEPTH - 1)
        w_o = f32(w_out_even[li]) if even else f32(w_out_odd[li])
        ims = []
        for b in range(B):
            for q in range(NQ):
                grep = np.zeros((2, 2, 128, D), np.float32)
                grep[0, 0] = mv(l, b, 2)[None]
                grep[0, 1] = mv(l, b, 5)[None]
                grep[1, 0] = mv(l, 2, 2)[None]
                grep[1, 1] = mv(l, 2, 5)[None]
                fmv = _c(np.concatenate([_fmcol(norm_ffn[l]), _fmcol(mv(l, b, 4)), _fmcol(mv(l, b, 3)),
                                         _fmcol(mv(l, 2, 4)), _fmcol(mv(l, 2, 3))], 1))
                yTc = _c(np.concatenate([yT[b][:, q * TL:(q + 1) * TL], yT[b][:, SEQ:]], 1))
                ims.append({"x": _c(np.concatenate([xs[b][q], cx[b]], 0)), "yT": yTc, "w_out": w_o,
                            "w_fi": f32(w_ffn_in[l]), "w_fo": f32(w_ffn_out[l]), "grep": grep, "fmv": fmv,
                            "norep": _c(np.broadcast_to(f32(norm_out), (128, D)))})
        rc = _run("kc_final" if final else "kc", lambda: build_kc(TL, CTX, final), ims)
        for b in range(B):
            for q in range(NQ):
                xo = rc[b * NQ + q]["xo"]
                xs[b][q] = _c(xo[:TL])
            cx[b] = _c(rc[b * NQ]["xo"][TL:])
    out = np.stack([np.concatenate(xs[b], 0) for b in range(B)], 0).astype(np.float32)
    return out
```
